# Optimizing a Trainium2 kernel written in Bass

```python
import math
import jax, jax.numpy as jnp
from jax import lax
import numpy as np

D_MODEL = 4096
BATCH = 16
SEQ = 256
DEPTH = 2
DEC_BATCH = 8
DEC_SEQ = 2048
PAST_LEN = 512

GRID_W = 64
N_GROUPS = 4
GROUP_W = D_MODEL // N_GROUPS
HEAD_DIM = 128
NA_HEADS = GROUP_W // HEAD_DIM
DIFF_HEADS = GROUP_W // (2 * HEAD_DIM)
RET_HEADS = GROUP_W // HEAD_DIM
NA_KH = 8
NA_KW = 16
Q_BLOCK = 128
RET_CHUNK = 128
CONV_K = 3
ROPE_BASE = 10000.0
IN_PARTS = 16
D_IN = IN_PARTS * GROUP_W
ALPHA = (2 * DEPTH) ** 0.25
BETA = (8 * DEPTH) ** -0.25
LN_EPS = 1e-6
NEG_INF = -1e30

kernel_name = 'hybrid_diffusion_prefix_trunk'

F32 = jnp.float32


def _layer_norm(x):
    xf = x.astype(F32)
    mu = xf.mean(-1, keepdims=True)
    var = jnp.square(xf - mu).mean(-1, keepdims=True)
    return ((xf - mu) * lax.rsqrt(var + LN_EPS)).astype(x.dtype)


def _rms_norm(x, g):
    xf = x.astype(F32)
    return (xf * lax.rsqrt(jnp.mean(xf * xf, -1, keepdims=True) + LN_EPS)).astype(x.dtype) * g


def _rope_half(x, ang):
    m = x.shape[-1] // 2
    cos = jnp.cos(ang).astype(x.dtype)
    sin = jnp.sin(ang).astype(x.dtype)
    x1, x2 = x[..., :m], x[..., m:]
    return jnp.concatenate([x1 * cos - x2 * sin, x2 * cos + x1 * sin], -1)


def _axial_rope(x):
    L, d = x.shape[1], x.shape[-1]
    half = d // 2
    nf = half // 2
    t = jnp.arange(L)
    inv = ROPE_BASE ** (-jnp.arange(nf, dtype=F32) / nf)
    shape = (L,) + (1,) * (x.ndim - 3) + (nf,)
    ang_r = ((t // GRID_W).astype(F32)[:, None] * inv).reshape(shape)
    ang_c = ((t % GRID_W).astype(F32)[:, None] * inv).reshape(shape)
    return jnp.concatenate([_rope_half(x[..., :half], ang_r), _rope_half(x[..., half:], ang_c)], -1)


def _split_q_blocks(q):
    B, L = q.shape[:2]
    return jnp.moveaxis(q.reshape((B, L // Q_BLOCK, Q_BLOCK) + q.shape[2:]), 1, 0)


def _merge_q_blocks(o):
    nb, B = o.shape[:2]
    return jnp.moveaxis(o, 0, 1).reshape((B, nb * Q_BLOCK) + o.shape[3:])


def _softmax_attention(q, k, v):
    scale = q.shape[-1] ** -0.5

    def block(qb):
        s = jnp.einsum('bqhd,bkhd->bhqk', qb, k).astype(F32) * scale
        p = jax.nn.softmax(s, -1).astype(v.dtype)
        return jnp.einsum('bhqk,bkhe->bqhe', p, v)

    return _merge_q_blocks(lax.map(block, _split_q_blocks(q)))


def _diff_attention(q, k, v, lam):
    scale = q.shape[-1] ** -0.5

    def block(qb):
        s = jnp.einsum('bqhtd,bkhtd->bthqk', qb, k).astype(F32) * scale
        p = jax.nn.softmax(s, -1)
        a = (p[:, 0] - lam * p[:, 1]).astype(v.dtype)
        return jnp.einsum('bhqk,bkhe->bqhe', a, v)

    return _merge_q_blocks(lax.map(block, _split_q_blocks(q)))


def _neighbourhood_attention(q, k, v, ck, cv, rel_bias):
    B, L, H, d = q.shape
    rows = L // GRID_W
    kh = min(NA_KH, rows)
    scale = d ** -0.5
    r = jnp.arange(rows)
    row_idx = jnp.clip(r - kh // 2, 0, rows - kh)[:, None] + jnp.arange(kh)[None, :]
    col = jnp.arange(GRID_W)
    c0 = jnp.clip(col - NA_KW // 2, 0, GRID_W - NA_KW)
    col_ok = (col[None, :] >= c0[:, None]) & (col[None, :] < c0[:, None] + NA_KW)
    dr = row_idx - r[:, None] + NA_KH - 1
    dc = jnp.clip(col[None, :] - col[:, None] + NA_KW - 1, 0, 2 * NA_KW - 2)
    bias = rel_bias[:, dr[:, None, :, None], dc[None, :, None, :]]
    bias = jnp.swapaxes(bias, 0, 1).astype(F32)
    qg = q.reshape(B, rows, GRID_W, H, d)
    kg = k.reshape(B, rows, GRID_W, H, d)[:, row_idx]
    vg = v.reshape(B, rows, GRID_W, H, d)[:, row_idx]
    s_loc = jnp.einsum('brqhd,brkwhd->brhqkw', qg, kg).astype(F32) * scale + bias
    s_loc = jnp.where(col_ok[:, None, :], s_loc, NEG_INF).reshape(B, rows, H, GRID_W, kh * GRID_W)
    s_ctx = jnp.einsum('brqhd,bchd->brhqc', qg, ck).astype(F32) * scale
    p = jax.nn.softmax(jnp.concatenate([s_loc, s_ctx], -1), -1).astype(v.dtype)
    p_loc = p[..., :kh * GRID_W].reshape(B, rows, H, GRID_W, kh, GRID_W)
    p_ctx = p[..., kh * GRID_W:]
    o = (jnp.einsum('brhqkw,brkwhd->brqhd', p_loc, vg)
         + jnp.einsum('brhqc,bchd->brqhd', p_ctx, cv))
    return o.reshape(B, L, H, d)


def _retention_scan(q, k, v, log_gamma, s0):
    B, L, H, _ = q.shape
    dv = v.shape[-1]
    nc = L // RET_CHUNK
    idx = jnp.arange(RET_CHUNK, dtype=F32)
    lg = log_gamma[:, None]
    diff = idx[:, None] - idx[None, :]
    dmat = jnp.where(diff >= 0, jnp.exp(jnp.maximum(diff, 0.0)[None] * lg[:, :, None]), 0.0)
    q_decay = jnp.exp((idx + 1.0)[None] * lg).T
    k_decay = jnp.exp((RET_CHUNK - 1.0 - idx)[None] * lg).T
    chunk_decay = jnp.exp(RET_CHUNK * log_gamma)

    def chunks(t):
        return jnp.moveaxis(t.astype(F32).reshape(B, nc, RET_CHUNK, H, t.shape[-1]), 1, 0)

    def step(S, inp):
        qc, kc, vc = inp
        a = jnp.einsum('bihd,bjhd->bhij', qc, kc) * dmat
        o = (jnp.einsum('bhij,bjhe->bihe', a, vc)
             + jnp.einsum('bihd,bhde->bihe', qc, S) * q_decay[None, :, :, None])
        S = (S * chunk_decay[None, :, None, None]
             + jnp.einsum('bjhd,bjhe->bhde', kc * k_decay[None, :, :, None], vc))
        return S, o

    S, o = lax.scan(step, s0, (chunks(q), chunks(k), chunks(v)))
    o = jnp.moveaxis(o, 0, 1).reshape(B, L, H, dv).astype(v.dtype)
    return o, S


def _bidir_retention(q, k, v, log_decay, s0):
    o_f, s_f = _retention_scan(q, k, v, log_decay[0], s0[:, 0])
    o_b, s_b = _retention_scan(jnp.flip(q, 1), jnp.flip(k, 1), jnp.flip(v, 1), log_decay[1], s0[:, 1])
    return o_f + jnp.flip(o_b, 1), jnp.stack([s_f, s_b], 1)


def _short_conv(u, w):
    up = jnp.pad(u, ((0, 0), (1, 1), (0, 0)))
    return up[:, :-2] * w[:, 0] + up[:, 1:-1] * w[:, 1] + up[:, 2:] * w[:, 2]


def _layer(x, cvec, w_in, w_out, w_ada, b_ada, ln_g, ln_b, na_bias, diff_lam, diff_subln,
           ret_decay, conv_w, lam_init, ctx=None):
    B, L, _ = x.shape
    m = (jax.nn.silu(cvec) @ w_ada + b_ada)[:, None, :]
    shift, scale, gate = jnp.split(m, 3, -1)
    h = _layer_norm(x) * (1 + scale) + shift
    p = jnp.einsum('bld,de->ble', h, w_in)
    (aq, ak, av, ag, bq, bk, bv, bg, rq, rk, rv, rg, dh, db, dcg, dg) = jnp.split(p, IN_PARTS, -1)
    aq = aq.reshape(B, L, NA_HEADS, HEAD_DIM)
    ak = ak.reshape(B, L, NA_HEADS, HEAD_DIM)
    av = av.reshape(B, L, NA_HEADS, HEAD_DIM)
    bq = bq.reshape(B, L, DIFF_HEADS, 2, HEAD_DIM)
    bk = bk.reshape(B, L, DIFF_HEADS, 2, HEAD_DIM)
    bv = bv.reshape(B, L, DIFF_HEADS, 2 * HEAD_DIM)
    lf = diff_lam.astype(F32)
    lam = jnp.exp(jnp.sum(lf[0] * lf[1])) - jnp.exp(jnp.sum(lf[2] * lf[3])) + lam_init
    rq = rq.reshape(B, L, RET_HEADS, HEAD_DIM)
    rk = rk.reshape(B, L, RET_HEADS, HEAD_DIM) * (HEAD_DIM ** -0.5)
    rv = rv.reshape(B, L, RET_HEADS, HEAD_DIM)
    log_decay = jax.nn.log_sigmoid(ret_decay.astype(F32))
    if ctx is None:
        a_out = _softmax_attention(aq, ak, av)
        b_out = _diff_attention(bq, bk, bv, lam)
        s0 = jnp.zeros((B, 2, RET_HEADS, HEAD_DIM, HEAD_DIM), F32)
    else:
        c_na_k, c_na_v, c_df_k, c_df_v, c_st = ctx
        a_out = _neighbourhood_attention(aq, ak, av, c_na_k, c_na_v, na_bias)
        bq_r = _axial_rope(bq)
        bk_r = _axial_rope(bk)
        b_out = _diff_attention(bq_r, jnp.concatenate([c_df_k, bk_r], 1),
                                jnp.concatenate([c_df_v, bv], 1), lam)
        rq = _axial_rope(rq)
        rk = _axial_rope(rk)
        s0 = c_st.astype(F32)
    r_out, r_state = _bidir_retention(rq, rk, rv, log_decay, s0)
    r_out = _layer_norm(r_out)
    b_out = _rms_norm(b_out, diff_subln) * (1.0 - lam_init)
    d_out = db * _short_conv(dcg * dh, conv_w)
    y = jnp.concatenate([
        jax.nn.silu(ag) * a_out.reshape(B, L, GROUP_W),
        jax.nn.silu(bg) * b_out.reshape(B, L, GROUP_W),
        jax.nn.silu(rg) * r_out.reshape(B, L, GROUP_W),
        jax.nn.silu(dg) * d_out], -1)
    y = jnp.einsum('ble,ed->bld', y, w_out)
    x = _layer_norm(ALPHA * x + gate * y) * ln_g + ln_b
    if ctx is None:
        return x, (ak, av, bk, bv, r_state.astype(x.dtype))
    return x, None


def setup_inputs(seed: int = 0) -> dict:
    key = jax.random.key(seed)
    ks = jax.random.split(key, 22)
    n = jax.random.normal
    ret_base = jnp.log(2.0 ** (5.0 + jnp.arange(RET_HEADS, dtype=F32)) - 1.0)
    return {
        'x_prompt': n(ks[0], (BATCH, SEQ, D_MODEL), F32),
        'x_sample': n(ks[1], (DEC_BATCH, DEC_SEQ, D_MODEL), F32),
        'cache_na_k': n(ks[2], (DEC_BATCH, DEPTH, PAST_LEN, NA_HEADS, HEAD_DIM), F32),
        'cache_na_v': n(ks[3], (DEC_BATCH, DEPTH, PAST_LEN, NA_HEADS, HEAD_DIM), F32),
        'cache_diff_k': n(ks[4], (DEC_BATCH, DEPTH, PAST_LEN, DIFF_HEADS, 2, HEAD_DIM), F32),
        'cache_diff_v': n(ks[5], (DEC_BATCH, DEPTH, PAST_LEN, DIFF_HEADS, 2 * HEAD_DIM), F32),
        'state_ret': n(ks[6], (DEC_BATCH, DEPTH, 2, RET_HEADS, HEAD_DIM, HEAD_DIM), F32),
        'c': n(ks[7], (DEC_BATCH, D_MODEL), F32),
        'c_ctx': n(ks[8], (D_MODEL,), F32),
        'w_ada': n(ks[9], (DEPTH, D_MODEL, 3 * D_MODEL), F32) * D_MODEL ** -0.5,
        'b_ada': n(ks[10], (DEPTH, 3 * D_MODEL), F32) * 0.01,
        'w_in': n(ks[11], (DEPTH, D_MODEL, D_IN), F32) * D_MODEL ** -0.5,
        'w_out': n(ks[12], (DEPTH, D_MODEL, D_MODEL), F32) * (D_MODEL ** -0.5 * BETA),
        'ln_g': 1.0 + 0.01 * n(ks[13], (DEPTH, D_MODEL), F32),
        'ln_b': 0.01 * n(ks[14], (DEPTH, D_MODEL), F32),
        'na_bias': 0.1 * n(ks[15], (DEPTH, NA_HEADS, 2 * NA_KH - 1, 2 * NA_KW - 1), F32),
        'diff_lam': 0.1 * n(ks[16], (DEPTH, 4, HEAD_DIM), F32),
        'diff_subln': 1.0 + 0.01 * n(ks[17], (DEPTH, 2 * HEAD_DIM), F32),
        'ret_decay': ret_base + 0.1 * n(ks[18], (DEPTH, 2, RET_HEADS), F32),
        'conv_w': n(ks[19], (DEPTH, GROUP_W, CONV_K), F32) * CONV_K ** -0.5,
    }


def reference(x_prompt, x_sample, cache_na_k, cache_na_v, cache_diff_k, cache_diff_v, state_ret,
              c, c_ctx, w_ada, b_ada, w_in, w_out, ln_g, ln_b, na_bias, diff_lam, diff_subln,
              ret_decay, conv_w):
    y_p = x_prompt
    y_s = x_sample
    nk, nv, dk, dv, st = [], [], [], [], []
    for l in range(DEPTH):
        lam_init = 0.8 - 0.6 * math.exp(-0.3 * l)
        w = (w_in[l], w_out[l], w_ada[l], b_ada[l], ln_g[l], ln_b[l], na_bias[l], diff_lam[l],
             diff_subln[l], ret_decay[l], conv_w[l], lam_init)
        y_p, (k_a, v_a, k_b, v_b, s_r) = _layer(y_p, c_ctx[None], *w)
        nk.append(k_a)
        nv.append(v_a)
        dk.append(k_b)
        dv.append(v_b)
        st.append(s_r)
        ctx = (cache_na_k[:, l], cache_na_v[:, l], cache_diff_k[:, l], cache_diff_v[:, l], state_ret[:, l])
        y_s, _ = _layer(y_s, c, *w, ctx=ctx)
    new_na_k = jnp.stack(nk, 1)
    new_na_v = jnp.stack(nv, 1)
    new_diff_k = jnp.stack(dk, 1)
    new_diff_v = jnp.stack(dv, 1)
    new_state_ret = jnp.stack(st, 1)
    return (y_p, y_s, new_na_k, new_na_v, new_diff_k, new_diff_v, new_state_ret)
```

```python
import math
import contextlib
import numpy as np
import concourse.bass as bass
import concourse.mybir as mybir
from concourse.bass_utils import run_bass_kernel_spmd

F32 = mybir.dt.float32
BF16 = mybir.dt.bfloat16
AF = mybir.ActivationFunctionType
ALU = mybir.AluOpType
AX = mybir.AxisListType

D = 4096
DEPTH = 2
LS = 2048
LP = 256
NTOK = LS + 2 * LP
NT = NTOK // 128
PAST = 512
HD = 128
GW = 1024
DIN = 16384
ALPHA = (2 * DEPTH) ** 0.25
EPS = 1e-6
SCALE = HD ** -0.5
NEG = -30000.0
SB_LO = 16512
SB_HI = 229344

ENGS = ('pe', 'act', 'dve', 'pool', 'sp')


class Res:
    __slots__ = ('name', 'wc', 'wd', 'rc', 'rd', 'war_c', 'war_d', 'bank', 'gen')

    def __init__(self, name, bank=None):
        self.name = name
        self.bank = bank
        self.gen = None
        self.wc = {}
        self.wd = []
        self.rc = {}
        self.rd = []
        self.war_c = {}
        self.war_d = []


class Prog:
    def __init__(self, nc, n_sp=24, n_pool=16, n_act=4):
        self.nc = nc
        self.ins = []
        self.ring_sizes = {'sp': n_sp, 'pool': n_pool, 'act': n_act}
        self.all_res = []
        self.locks = {}

    def res(self, name='', bank=None):
        r = Res(name, bank)
        self.all_res.append(r)
        return r

    def add(self, eng, fn, reads=(), writes=(), swrites=(), dma=False):
        idx = len(self.ins)
        deps = set()
        for r in reads:
            deps.update(r.wc.values())
            deps.update(r.wd)
        for r in writes:
            deps.update(r.wc.values()); deps.update(r.wd)
            deps.update(r.rc.values()); deps.update(r.rd)
            deps.update(r.war_c.values()); deps.update(r.war_d)
        for r in swrites:
            if r.rc or r.rd:
                r.war_c, r.war_d = r.rc, r.rd
                r.rc, r.rd = {}, []
                r.wc, r.wd = {}, []
                r.gen = None
            deps.update(r.war_c.values()); deps.update(r.war_d)
            if r.gen is not None:
                deps.add(r.gen)
        for r in reads:
            if dma:
                r.rd.append(idx)
            else:
                r.rc[eng] = idx
        for r in writes:
            wc_ = dict(r.wc)
            for k_, v_ in r.rc.items():
                if wc_.get(k_, -1) < v_:
                    wc_[k_] = v_
            r.war_c, r.war_d = wc_, r.rd + r.wd
            r.rc, r.rd = {}, []
            r.gen = idx
            if dma:
                r.wc, r.wd = {}, [idx]
            else:
                r.wc, r.wd = {eng: idx}, []
        for r in swrites:
            if dma:
                r.wd.append(idx)
            else:
                r.wc[eng] = idx
        banks = None
        for grp_ in (reads, writes, swrites):
            for r in grp_:
                if r.bank is not None:
                    if banks is None:
                        banks = set()
                    banks.add(r.bank)
        if banks:
            for b_ in banks:
                L = self.locks.setdefault(b_, {})
                for e2, i2 in L.items():
                    if e2 != eng:
                        deps.add(i2)
                L[eng] = idx
        deps.discard(idx)
        self.ins.append([eng, fn, deps, dma])
        return idx

    def dma(self, q, out, in_, reads=(), writes=(), swrites=(), **kw):
        def fn(e, out=out, in_=in_, kw=kw):
            return e.dma_start(out=out, in_=in_, **kw)
        return self.add(q, fn, reads, writes, swrites, dma=True)

    def barrier(self):
        allr = self.res('barrier')
        deps = set()
        for r in self.all_res:
            deps.update(r.wc.values()); deps.update(r.wd)
            deps.update(r.rc.values()); deps.update(r.rd)
            deps.update(r.war_c.values()); deps.update(r.war_d)
        first = True
        for rnd in range(2):
            for eng in ENGS:
                def fn(e):
                    return e.nop()
                if rnd == 0:
                    i = self.add(eng, fn, writes=[allr])
                    if first:
                        self.ins[i][2].update(deps)
                        first = False
                else:
                    self.add(eng, fn, reads=[allr])
        for r in self.all_res:
            if r is allr:
                continue
            r.wc, r.wd, r.rc, r.rd, r.war_c, r.war_d = {}, [], {}, [], {}, []
            r.gen = None
        self.locks = {}
        self.all_res = [r for r in self.all_res if r is allr or getattr(r, 'name', '') != 'barrier']

    def emit(self):
        nc = self.nc
        ins = self.ins
        n = len(ins)
        engobj = {'pe': nc.tensor, 'act': nc.scalar, 'dve': nc.vector, 'pool': nc.gpsimd, 'sp': nc.sync}
        needed = [False] * n
        for i in range(n):
            eng, fn, deps, dma = ins[i]
            keep = []
            for d in deps:
                deng, _, _, ddma = ins[d]
                if (not dma) and (not ddma) and deng == eng and eng == 'pe':
                    continue
                keep.append(d)
            ins[i][2] = keep
            for d in keep:
                needed[d] = True
        self._stack = contextlib.ExitStack()
        sems = {e: self._stack.enter_context(nc.semaphore('s_' + e)) for e in ENGS}
        rings = {q: [self._stack.enter_context(nc.semaphore('r_%s%d' % (q, j))) for j in range(k)]
                 for q, k in self.ring_sizes.items()}
        ring_pos = {q: 0 for q in rings}
        ring_val = {q: [0] * len(rings[q]) for q in rings}
        ring_used = {q: [False] * len(rings[q]) for q in rings}
        tick = {e: 0 for e in ENGS}
        ev = [None] * n
        waited = {e: {} for e in engobj}
        nwaits = 0
        for i in range(n):
            eng, fn, deps, dma = ins[i]
            e = engobj[eng]
            need = {}
            if dma:
                q = eng
                pos = ring_pos[q]
                ring_pos[q] = (pos + 1) % len(rings[q])
                if ring_used[q][pos]:
                    need[('r', q, pos)] = ring_val[q][pos]
            for d in deps:
                k, v = ev[d]
                if need.get(k, -1) < v:
                    need[k] = v
            w = waited[eng]
            for k, v in need.items():
                if w.get(k, -1) >= v:
                    continue
                w[k] = v
                so = sems[k[1]] if k[0] == 'c' else rings[k[1]][k[2]]
                e.wait_ge(so, v)
                nwaits += 1
            inst = fn(e)
            if dma:
                ring_val[q][pos] += 16
                ring_used[q][pos] = True
                inst.then_inc(rings[q][pos], 16)
                ev[i] = (('r', q, pos), ring_val[q][pos])
            else:
                if needed[i]:
                    tick[eng] += 1
                    inst.then_inc(sems[eng], 1)
                    ev[i] = (('c', eng), tick[eng])
            ins[i][1] = None
        self.stats = dict(n=n, nwaits=nwaits, ticks=dict(tick))
        return self.stats


class Alloc:
    def __init__(self, nc):
        self.nc = nc
        self.off = SB_LO
        self.cnt = 0
        self.mark_ = SB_LO

    def __call__(self, shape, dt, name='t'):
        per = 1
        for s in shape[1:]:
            per *= s
        nbytes = per * (4 if dt == F32 else 2)
        nbytes = (nbytes + 63) // 64 * 64
        assert self.off + nbytes <= SB_HI, ('SBUF overflow', name, self.off, nbytes)
        self.cnt += 1
        t = self.nc.alloc_sbuf_tensor_at('%s_%d' % (name, self.cnt), list(shape), dt, offset=self.off)
        self.off += nbytes
        return t

    def mark(self):
        self.mark_ = self.off

    def reset(self):
        self.off = self.mark_


class Builder:
    def __init__(self, debug=False, stop_after=None, lite=()):
        self.debug = debug
        self.stop_after = stop_after
        self.lite = lite
        nc = self.nc = bass.Bass("TRN2", target_bir_lowering=False)
        self.P = Prog(nc)
        self.A = Alloc(nc)
        self.inputs = {}
        self.outputs = {}
        self.dbg_names = []

    def din(self, name, shape, dt=F32):
        if name in self.lite:
            shape = [DEPTH, 128, 512]
        t = self.nc.dram_tensor(name, list(shape), dt, kind="ExternalInput").ap()
        self.inputs[name] = t
        return t

    def dout(self, name, shape, dt=F32):
        t = self.nc.dram_tensor(name, list(shape), dt, kind="ExternalOutput").ap()
        self.outputs[name] = t
        return t

    def dscr(self, name, shape, dt=F32):
        if self.debug:
            t = self.nc.dram_tensor(name, list(shape), dt, kind="ExternalOutput").ap()
            self.dbg_names.append(name)
        else:
            t = self.nc.dram_tensor(name, list(shape), dt).ap()
        return t

    def mm(self, out, lhsT, rhs, start, stop, reads, wres, first=None):
        first = start if first is None else first
        fn = lambda e: e.matmul(out, lhsT=lhsT, rhs=rhs, start=start, stop=stop)
        if first:
            self.P.add('pe', fn, reads=reads, writes=[wres])
        else:
            self.P.add('pe', fn, reads=reads, swrites=[wres])

    def tr(self, out, in_, ident, reads, wres, excl):
        fn = lambda e: e.transpose(out=out, in_=in_, identity=ident)
        if excl:
            self.P.add('pe', fn, reads=reads, writes=[wres])
        else:
            self.P.add('pe', fn, reads=reads, swrites=[wres])

    def act(self, out, in_, func, reads, writes=(), swrites=(), **kw):
        self.P.add('act', lambda e: e.activation(out=out, in_=in_, func=func, **kw), reads, writes, swrites)

    def ts(self, eng, out, in0, s1, s2, op0, op1, reads, writes=(), swrites=(), **kw):
        if op1 is None:
            fn = lambda e: e.tensor_scalar(out=out, in0=in0, scalar1=s1, scalar2=None, op0=op0, **kw)
        else:
            fn = lambda e: e.tensor_scalar(out=out, in0=in0, scalar1=s1, scalar2=s2, op0=op0, op1=op1, **kw)
        self.P.add(eng, fn, reads, writes, swrites)

    def tt(self, eng, out, in0, in1, op, reads, writes=(), swrites=()):
        self.P.add(eng, lambda e: e.tensor_tensor(out=out, in0=in0, in1=in1, op=op), reads, writes, swrites)

    def stt(self, out, in0, scalar, in1, op0, op1, reads, writes=(), swrites=(), eng='dve'):
        self.P.add(eng, lambda e: e.scalar_tensor_tensor(out=out, in0=in0, scalar=scalar, in1=in1, op0=op0, op1=op1),
                   reads, writes, swrites)

    def cp(self, eng, out, in_, reads, writes=(), swrites=()):
        if eng == 'act':
            self.P.add('act', lambda e: e.copy(out=out, in_=in_), reads, writes, swrites)
        else:
            self.P.add(eng, lambda e: e.tensor_copy(out=out, in_=in_), reads, writes, swrites)

    def memset(self, eng, out, val, writes=(), swrites=()):
        self.P.add(eng, lambda e: e.memset(out, val), (), writes, swrites)


    NSTG = 4

    def alloc_stg(self):
        self.stg = [self.A([128, 2, 512], F32, 'stg%d' % i) for i in range(self.NSTG)]
        self.r_stg = [self.P.res('stg%d' % i) for i in range(self.NSTG)]
        self._wst = 0

    def w_pieces(self, dst, r_dst, src):
        P = self.P
        first = True
        for pc in range(16):
            slot = self._wst % self.NSTG
            self._wst += 1
            P.dma('sp', self.stg[slot][:], src[:, 2 * pc:2 * pc + 2, :], writes=[self.r_stg[slot]])
            self.cp('pool', dst[:, 2 * pc:2 * pc + 2, :], self.stg[slot][:], [self.r_stg[slot]],
                    [r_dst] if first else (), () if first else [r_dst])
            first = False
            yield

    def w_pieces_d(self, dst, r_dst, srcs):
        P = self.P
        first = True
        dv = dst.rearrange("p k (q c) -> p k q c", q=4)
        for q in range(4):
            for k8 in range(4):
                slot = self._wst % self.NSTG
                self._wst += 1
                sv = self.stg[slot][:].rearrange("p a (b c) -> p (a b) c", c=128)
                P.dma('sp', sv, srcs[q][:, k8 * 8:(k8 + 1) * 8, :], writes=[self.r_stg[slot]])
                self.cp('pool', dv[:, k8 * 8:(k8 + 1) * 8, q, :], sv, [self.r_stg[slot]],
                        [r_dst] if first else (), () if first else [r_dst])
                first = False
                yield

    @staticmethod
    def drain(gen, n=None):
        if gen is None:
            return
        if n is None:
            for _ in gen:
                pass
        else:
            for _ in range(n):
                if next(gen, 'end') == 'end':
                    break

    def rstd_from(self, dst, src, res, mult=1.0):
        self.ts('dve', dst, src, mult, EPS, ALU.mult, ALU.add, [res], [res])
        self.act(dst, dst, AF.Ln, [res], [res])
        self.act(dst, dst, AF.Exp, [res], [res], scale=-0.5)

    def build(self):
        nc, P, A = self.nc, self.P, self.A
        d = self.d = {}
        d['x_s'] = self.din('x_s', [LS, D])
        d['x_p'] = self.din('x_p', [2 * LP, D])
        d['c_na_k'] = self.din('c_na_k', [DEPTH, PAST, 8 * HD])
        d['c_na_v'] = self.din('c_na_v', [DEPTH, PAST, 8 * HD])
        d['c_df_k'] = self.din('c_df_k', [DEPTH, PAST, 8 * HD])
        d['c_df_v'] = self.din('c_df_v', [DEPTH, PAST, 4 * 2 * HD])
        d['st_ret'] = self.din('st_ret', [DEPTH, 2, 8, HD, HD])
        d['cvec'] = self.din('cvec', [2, D])
        d['w_ada'] = self.din('w_ada', [DEPTH, D, 3 * D])
        d['b_ada'] = self.din('b_ada', [DEPTH, 3 * D])
        d['w_in'] = self.din('w_in', [DEPTH, D, DIN])
        d['w_out'] = self.din('w_out', [DEPTH, D, D])
        d['ln_g'] = self.din('ln_g', [DEPTH, D])
        d['ln_b'] = self.din('ln_b', [DEPTH, D])
        d['na_bias'] = self.din('na_bias', [DEPTH, 8 * 15, 31])
        d['diff_lam'] = self.din('diff_lam', [DEPTH, 4 * HD])
        d['diff_subln'] = self.din('diff_subln', [DEPTH, 2 * HD])
        d['ret_decay'] = self.din('ret_decay', [DEPTH, 16])
        d['conv_w'] = self.din('conv_w', [DEPTH, GW, 3])
        d['k_ident'] = self.din('k_ident', [128, 128])
        d['k_cos'] = self.din('k_cos', [LS, 128])
        d['k_sin'] = self.din('k_sin', [LS, 128])
        d['k_namask'] = self.din('k_namask', [5, 128, 640])
        d['k_rmask'] = self.din('k_rmask', [2, 128, 128])
        d['k_coef'] = self.din('k_coef', [128, 4])
        d['o_ys'] = self.dout('o_ys', [LS, D])
        d['o_yp'] = self.dout('o_yp', [2 * LP, D])
        d['o_nak'] = self.dout('o_nak', [2, DEPTH, LP, GW])
        d['o_nav'] = self.dout('o_nav', [2, DEPTH, LP, GW])
        d['o_dfk'] = self.dout('o_dfk', [2, DEPTH, LP, GW])
        d['o_dfv'] = self.dout('o_dfv', [2, DEPTH, LP, GW])
        d['o_st'] = self.dout('o_st', [2, DEPTH, 2, 8, HD, HD])
        d['MADA'] = self.dscr('MADA', [DEPTH, 2, 3 * D])
        d['XRES'] = self.dscr('XRES', [NTOK, D])
        for nm in ('QTA', 'KTA', 'QTB', 'KTB', 'QFT', 'QBT', 'KFT', 'KBT'):
            d[nm] = self.dscr(nm, [8, HD, NTOK], BF16)
        for nm in ('VA', 'VB', 'VC', 'KF', 'KB'):
            d[nm] = self.dscr(nm, [NTOK, GW], BF16)
        d['GT'] = self.dscr('GT', [NTOK, 3 * GW], BF16)
        d['UT'] = self.dscr('UT', [GW, NTOK])
        d['GDT'] = self.dscr('GDT', [GW, NTOK])
        d['Y'] = self.dscr('Y', [NTOK, 3 * GW], BF16)
        d['YTD'] = self.dscr('YTD', [GW, NTOK], BF16)
        d['Z'] = self.dscr('Z', [NTOK, D])
        d['PBREP'] = self.dscr('PBREP', [120, 64, 128])

        self._es = contextlib.ExitStack()
        PS = self._es.enter_context(nc.psum_tensor("psum_all", [128, 4096], F32))
        self.PS = PS

        def bank(b, lo=0, hi=512):
            return PS[:, b * 512 + lo: b * 512 + hi]

        def bank_bf(b, lo=0, hi=1024):
            return PS[:, b * 512 + lo // 2: b * 512 + hi // 2].bitcast(BF16)
        self.bank, self.bank_bf = bank, bank_bf
        self.r_bank = [P.res('bank%d' % b, bank=b) for b in range(8)]
        r_bank = self.r_bank

        self.r_const = r_const = P.res('const')
        self.ident = ident = A([128, 128], F32, 'ident')
        self.identb = identb = A([128, 128], BF16, 'identb')
        P.dma('sp', ident[:], d['k_ident'][:, :], swrites=[r_const])
        P.dma('pool', identb[:], d['k_ident'][:, :], swrites=[r_const])
        self.coef = coef = A([128, 4], F32, 'coef')
        P.dma('sp', coef[:], d['k_coef'][:, :], swrites=[r_const])
        A.mark()
        self.base_mark = A.mark_

        cT = A([128, 32, 2], F32, 'cT'); r_cT = P.res('cT')
        cTb = A([128, 32, 2], BF16, 'cTb'); r_cTb = P.res('cTb')
        wbuf = [A([128, 32, 512], BF16, 'wbuf%d' % i) for i in range(2)]
        r_wbuf = [P.res('wbuf%d' % i) for i in range(2)]
        mrow = [A([2, 512], F32, 'mrow%d' % i) for i in range(2)]
        brow = [A([2, 512], F32, 'brow%d' % i) for i in range(2)]
        r_mrow = [P.res() for i in range(2)]
        r_brow = [P.res() for i in range(2)]
        self.alloc_stg()
        for cv in range(2):
            P.dma('sp', cT[:, :, cv], d['cvec'][cv].rearrange("(kc p) -> p kc", p=128), swrites=[r_cT],
                  allow_slow_non_contiguous=True)
        self.act(cTb[:], cT[:], AF.Silu, [r_cT], [r_cTb])
        wcnt = 0
        r_mada = P.res('MADA')
        for l in range(DEPTH if 'w_ada' not in self.lite else 0):
            wl = d['w_ada'][l].rearrange("(kc p) n -> p kc n", p=128)
            for ch in range(24):
                s = wcnt % 2
                wcnt += 1
                self.drain(self.w_pieces(wbuf[s][:], r_wbuf[s], wl[:, :, ch * 512:(ch + 1) * 512]))
                P.dma('sp', brow[s][:], d['b_ada'][l:l + 1, ch * 512:(ch + 1) * 512].broadcast_to([2, 512]),
                      writes=[r_brow[s]])
                pb = ch % 4
                for kc in range(32):
                    self.mm(bank(pb)[0:2, :], cTb[:, kc, :], wbuf[s][:, kc, :], kc == 0, kc == 31,
                            [r_cTb, r_wbuf[s]], r_bank[pb])
                self.tt('dve', mrow[s][:], bank(pb)[0:2, :], brow[s][:], ALU.add,
                        [r_bank[pb], r_brow[s]], [r_mrow[s]])
                P.dma('sp', d['MADA'][l, :, ch * 512:(ch + 1) * 512], mrow[s][:], reads=[r_mrow[s]],
                      swrites=[r_mada])
        P.barrier()
        A.reset()
        if self.stop_after == 'A':
            return self.finish()

        self.groups = [list(range(0, 8)) + [16, 17], list(range(8, 16)) + [18, 19]]

        for l in range(DEPTH):
            self.layer(l)
            if self.stopped:
                break
        return self.finish()

    stopped = False

    def finish(self):
        self.P.barrier()
        return self.P.emit()

    def x_src(self, l, tt, c0=0, c1=D):
        d = self.d
        if l == 0:
            if tt < 16:
                return d['x_s'][tt * 128:(tt + 1) * 128, c0:c1]
            return d['x_p'][(tt - 16) * 128:(tt - 15) * 128, c0:c1]
        return d['XRES'][tt * 128:(tt + 1) * 128, c0:c1]

    def layer(self, l):
        nc, P, A = self.nc, self.P, self.A
        d = self.d
        coef = self.coef
        A.mark_ = self.base_mark
        A.reset()
        r_tab = P.res('tab')
        mod = A([128, 2, 2, 32], F32, 'mod')
        for cv in range(2):
            for wh in range(2):
                P.dma('sp', mod[:, cv, wh, :],
                      d['MADA'][l, cv, wh * D:(wh + 1) * D].rearrange("(kc p) -> p kc", p=128),
                      swrites=[r_tab], allow_slow_non_contiguous=True)
        self.ts('dve', mod[:, :, 1, :], mod[:, :, 1, :], 1.0, None, ALU.add, None, [r_tab], [r_tab])
        cos2 = A([128, 16, 128], F32, 'cos2')
        sin2 = A([128, 16, 128], F32, 'sin2')
        P.dma('sp', cos2[:], d['k_cos'].rearrange("(t p) c -> p t c", p=128), swrites=[r_tab])
        P.dma('sp', sin2[:], d['k_sin'].rearrange("(t p) c -> p t c", p=128), swrites=[r_tab])
        ld = A([128, 16], F32, 'ld')
        P.dma('sp', ld[:], d['ret_decay'][l:l + 1, :].broadcast_to([128, 16]), writes=[r_tab])
        self.act(ld[:], ld[:], AF.Exp, [r_tab], [r_tab], scale=-1.0)
        self.ts('dve', ld[:], ld[:], 1.0, None, ALU.add, None, [r_tab], [r_tab])
        self.act(ld[:], ld[:], AF.Ln, [r_tab], [r_tab])
        self.ts('dve', ld[:], ld[:], -1.0, None, ALU.mult, None, [r_tab], [r_tab])
        dec = A([128, 4, 8], F32, 'dec')
        self.act(dec[:, 0, :], ld[:, 0:8], AF.Exp, [r_tab], [r_tab], scale=coef[:, 0:1])
        self.act(dec[:, 1, :], ld[:, 8:16], AF.Exp, [r_tab], [r_tab], scale=coef[:, 2:3])
        self.act(dec[:, 2, :], ld[:, 0:8], AF.Exp, [r_tab], [r_tab], scale=coef[:, 1:2])
        self.act(dec[:, 3, :], ld[:, 8:16], AF.Exp, [r_tab], [r_tab], scale=coef[:, 3:4])
        self.ts('dve', dec[:, 2:4, :], dec[:, 2:4, :], SCALE, None, ALU.mult, None, [r_tab], [r_tab])
        cdt = A([128, 16], F32, 'cdt')
        self.act(cdt[:], ld[:], AF.Exp, [r_tab], [r_tab], scale=128.0)
        A.mark()
        self.layer_mark = A.mark_
        self.tabs = dict(mod=mod, cos2=cos2, sin2=sin2, dec=dec, cdt=cdt, r_tab=r_tab)

        if self.stop_after == 'TAB%d' % l:
            self.stopped = True
            return
        self.phase_BC(l)
        if self.stopped:
            return
        A.mark_ = self.layer_mark
        P.barrier()
        if self.stop_after == 'BC%d' % l:
            self.stopped = True
            return
        self.phase_mixers(l)
        P.barrier()
        if self.stop_after == 'MIX%d' % l:
            self.stopped = True
            return
        self.phase_EF(l)
        P.barrier()
        if self.stop_after == 'L%d' % l:
            self.stopped = True

    def phase_BC(self, l):
        nc, P, A = self.nc, self.P, self.A
        d = self.d
        bank, bank_bf, r_bank = self.bank, self.bank_bf, self.r_bank
        ident, identb, r_const = self.ident, self.identb, self.r_const
        T = self.tabs
        r_tab = T['r_tab']
        mod, cos2, sin2, dec = T['mod'], T['cos2'], T['sin2'], T['dec']
        A.reset()
        hT = A([128, 32, 1280], BF16, 'hT')
        r_hT = [P.res('hT%d' % s) for s in range(10)]
        wbuf = [A([128, 32, 512], BF16, 'wb%d' % i) for i in range(2)]
        r_wbuf = [P.res('wb%d' % i) for i in range(2)]
        A.mark()
        w_l = d['w_in'][l]
        w_tok = w_l.rearrange("(kc p) n -> p kc n", p=128)
        r_scr = self.r_scr = getattr(self, 'r_scr', None) or {k: P.res(k) for k in
                                                             ('Q', 'V', 'G', 'UD', 'OUT', 'Y', 'Z', 'X')}

        for gi, grp in enumerate(self.groups):
            A.reset()
            xt = [A([128, D], F32, 'xt%d' % i) for i in range(2)]
            r_xt = [P.res() for i in range(2)]
            st = [A([128, 8, 6], F32, 'st%d' % i) for i in range(2)]
            mv = [A([128, 4], F32, 'mv%d' % i) for i in range(2)]
            r_mv = [P.res() for i in range(2)]
            tb_i = 0
            ev_i = 0
            for s, tt in enumerate(grp):
                b = s % 2
                cv = 0 if tt < 16 else 1
                P.dma('sp', xt[b][:], self.x_src(l, tt), writes=[r_xt[b]])
                for c8 in range(8):
                    P.add('dve', lambda e, o=st[b][:, c8, :], i=xt[b][:, c8 * 512:(c8 + 1) * 512]: e.bn_stats(out=o, in_=i),
                          reads=[r_xt[b]], writes=[r_mv[b]] if c8 == 0 else (), swrites=() if c8 == 0 else [r_mv[b]])
                P.add('dve', lambda e, o=mv[b][:, 0:2], i=st[b][:].rearrange("p a b -> p (a b)"): e.bn_aggr(out=o, in_=i),
                      reads=[r_mv[b]], writes=[r_mv[b]])
                self.rstd_from(mv[b][:, 2:3], mv[b][:, 1:2], r_mv[b])
                self.ts('dve', xt[b][:], xt[b][:], mv[b][:, 0:1], mv[b][:, 2:3], ALU.subtract, ALU.mult,
                        [r_mv[b], r_xt[b]], [r_xt[b]])
                for kq in range(8):
                    pb = (tb_i % 4)
                    tb_i += 1
                    for j in range(4):
                        kc = kq * 4 + j
                        self.tr(bank(pb)[:, j * 128:(j + 1) * 128], xt[b][:, kc * 128:(kc + 1) * 128], ident[:],
                                [r_xt[b], r_const], r_bank[pb], j == 0)
                    for j in range(4):
                        kc = kq * 4 + j
                        o = hT[:, kc, s * 128:(s + 1) * 128]
                        i_ = bank(pb)[:, j * 128:(j + 1) * 128]
                        if kq % 2 == 0:
                            self.act(o, i_, AF.Identity, [r_bank[pb], r_tab], (), [r_hT[s]],
                                     scale=mod[:, cv, 1, kc:kc + 1], bias=mod[:, cv, 0, kc:kc + 1])
                        else:
                            self.ts('dve', o, i_, mod[:, cv, 1, kc:kc + 1], mod[:, cv, 0, kc:kc + 1],
                                    ALU.mult, ALU.add, [r_bank[pb], r_tab], (), [r_hT[s]])
                        ev_i += 1
            P.barrier()
            if self.stop_after == 'B%d' % l:
                self.stopped = True
                return
            A.reset()
            sb16 = [A([128, 512], BF16, 'sb16_%d' % i) for i in range(2)]
            sb16b = [A([128, 512], BF16, 'sb16b_%d' % i) for i in range(2)]
            sf32 = [A([128, 512], F32, 'sf32_%d' % i) for i in range(2)]
            of32 = [A([128, 512], F32, 'of32_0')] * 2
            rt1 = [A([128, 512], F32, 'rt1_%d' % i) for i in range(2)]
            rt2 = [A([128, 512], F32, 'rt2_%d' % i) for i in range(2)]
            trs = [A([128, 1024], BF16, 'trs%d' % i) for i in range(2)]
            r_sb16 = [P.res() for i in range(2)]
            r_sb16b = [P.res() for i in range(2)]
            r_sf32 = [P.res() for i in range(2)]
            r_of32 = [P.res()] * 2
            r_rt = [P.res() for i in range(2)]
            r_trs = [P.res() for i in range(2)]
            dstg0 = A([128, 2, 512], F32, 'dstg')
            dtmp0 = A([128, 2, 512], F32, 'dtmp')
            dstg, dtmp = [dstg0, dstg0], [dtmp0, dtmp0]
            r_dstg0, r_dtmp0 = P.res(), P.res()
            r_dstg, r_dtmp = [r_dstg0, r_dstg0], [r_dtmp0, r_dtmp0]
            self.alloc_stg()

            nunits = 24 + 8
            wslot = [0]

            def load_w(ui, ws):
                if ui < 24:
                    return self.w_pieces(wbuf[ws][:], r_wbuf[ws], w_tok[:, :, ui * 512:(ui + 1) * 512])
                j = ui - 24
                srcs = [w_tok[:, :, (12 + q) * GW + j * 128:(12 + q) * GW + (j + 1) * 128] for q in range(4)]
                return self.w_pieces_d(wbuf[ws][:], r_wbuf[ws], srcs)

            self.drain(load_w(0, 0))
            pending = []
            acc_i = 0
            u_i = 0
            for ui in range(nunits):
                ws = ui % 2
                wgen = load_w(ui + 1, (ui + 1) % 2) if ui + 1 < nunits else None
                if ui < 24:
                    cc = ui
                    part, half = cc // 2, cc % 2
                    for s, tt in enumerate(grp):
                        pb = acc_i % 4
                        acc_i += 1
                        for kc in range(32):
                            self.mm(bank(pb), hT[:, kc, s * 128:(s + 1) * 128], wbuf[ws][:, kc, :], kc == 0, kc == 31,
                                    [r_hT[s], r_wbuf[ws]], r_bank[pb])
                        self.drain(wgen, 2)
                        for f in pending:
                            f()
                        pending = []
                        k = u_i % 2
                        u_i += 1
                        is_s = tt < 16
                        tok0 = tt * 128
                        ps = bank(pb)
                        cols = slice(half * 512, (half + 1) * 512)
                        if not is_s:
                            seq = (tt - 16) // 2
                            ptok = ((tt - 16) % 2) * 128

                        def rope(dst, k=k, ps=ps, pb=pb, tt=tt):
                            c2 = cos2[:, tt, :].unsqueeze(1).broadcast_to([128, 4, 128])
                            psv = ps.rearrange("p (b c) -> p b c", c=128)
                            self.tt('dve', rt1[k][:].rearrange("p (b c) -> p b c", c=128), psv, c2, ALU.mult,
                                    [r_bank[pb], r_tab], [r_rt[k]])
                            ps5 = ps.rearrange("p (b a x f) -> p b a x f", a=2, x=2, f=32)
                            r25 = rt2[k][:].rearrange("p (b a x f) -> p b a x f", a=2, x=2, f=32)
                            s25 = sin2[:, tt, :].rearrange("p (a x f) -> p a x f", a=2, x=2)
                            for x in range(2):
                                for a in range(2):
                                    self.tt('dve', r25[:, :, a, x, :], ps5[:, :, a, 1 - x, :],
                                            s25[:, a, x, :].unsqueeze(1).broadcast_to([128, 4, 32]), ALU.mult,
                                            [r_bank[pb], r_tab], (), [r_rt[k]])
                            return rt1[k], rt2[k]

                        def transposes(srcs, dsts, k=k, tok0=tok0):
                            def f():
                                tb = 4 + (self._tb % 4)
                                self._tb += 1
                                n = 0
                                for (src, rs) in srcs:
                                    for j in range(4):
                                        self.tr(bank_bf(tb)[:, n * 128:(n + 1) * 128], src[:, j * 128:(j + 1) * 128],
                                                identb[:], [rs, r_const], r_bank[tb], n == 0)
                                        n += 1
                                self.cp('dve', trs[k][:, 0:n * 128], bank_bf(tb)[:, 0:n * 128], [r_bank[tb]], [r_trs[k]])
                                for i, dst in enumerate(dsts):
                                    P.dma('sp', dst.rearrange("h d t -> d h t"),
                                          trs[k][:, i * 512:(i + 1) * 512].rearrange("p (h t) -> p h t", h=4),
                                          reads=[r_trs[k]], swrites=[r_scr['Q']])
                            return f

                        if part in (0, 1):
                            self.cp('act', sb16[k][:], ps, [r_bank[pb]], [r_sb16[k]])
                            if part == 1 and not is_s:
                                self.cp('dve', of32[k][:], ps, [r_bank[pb]], [r_of32[k]])
                                P.dma('sp', d['o_nak'][seq, l, ptok:ptok + 128, cols], of32[k][:],
                                      reads=[r_of32[k]], swrites=[r_scr['OUT']])
                            dst = d['QTA' if part == 0 else 'KTA'][half * 4:half * 4 + 4, :, tok0:tok0 + 128]
                            pending.append(transposes([(sb16[k], r_sb16[k])], [dst]))
                        elif part in (2, 6, 10):
                            self.cp('act', sb16[k][:], ps, [r_bank[pb]], [r_sb16[k]])
                            nm = {2: 'VA', 6: 'VB', 10: 'VC'}[part]
                            P.dma('sp', d[nm][tok0:tok0 + 128, cols], sb16[k][:], reads=[r_sb16[k]],
                                  swrites=[r_scr['V']])
                            if not is_s and part in (2, 6):
                                self.cp('dve', of32[k][:], ps, [r_bank[pb]], [r_of32[k]])
                                P.dma('sp', d['o_nav' if part == 2 else 'o_dfv'][seq, l, ptok:ptok + 128, cols],
                                      of32[k][:], reads=[r_of32[k]], swrites=[r_scr['OUT']])
                        elif part in (3, 7, 11):
                            self.act(sb16[k][:], ps, AF.Silu, [r_bank[pb]], [r_sb16[k]])
                            gi_ = {3: 0, 7: 1, 11: 2}[part]
                            P.dma('sp', d['GT'][tok0:tok0 + 128, gi_ * GW + half * 512: gi_ * GW + (half + 1) * 512],
                                  sb16[k][:], reads=[r_sb16[k]], swrites=[r_scr['G']])
                        elif part in (4, 5):
                            if is_s:
                                a1, a2 = rope(None)
                                self.tt('dve', sb16[k][:], a1[:], a2[:], ALU.add, [r_rt[k]], [r_sb16[k]])
                            else:
                                self.cp('act', sb16[k][:], ps, [r_bank[pb]], [r_sb16[k]])
                                if part == 5:
                                    self.cp('dve', of32[k][:], ps, [r_bank[pb]], [r_of32[k]])
                                    P.dma('sp', d['o_dfk'][seq, l, ptok:ptok + 128, cols], of32[k][:],
                                          reads=[r_of32[k]], swrites=[r_scr['OUT']])
                            dst = d['QTB' if part == 4 else 'KTB'][half * 4:half * 4 + 4, :, tok0:tok0 + 128]
                            pending.append(transposes([(sb16[k], r_sb16[k])], [dst]))
                        elif part in (8, 9):
                            if is_s:
                                a1, a2 = rope(None)
                                self.tt('dve', sf32[k][:], a1[:], a2[:], ALU.add, [r_rt[k]], [r_sf32[k]])
                            else:
                                self.cp('act', sf32[k][:], ps, [r_bank[pb]], [r_sf32[k]])
                            base = 0 if part == 8 else 2
                            sfv = sf32[k][:].rearrange("p (h c) -> p h c", c=128)
                            for di, (dstt, rdst) in enumerate(((sb16[k], r_sb16[k]), (sb16b[k], r_sb16b[k]))):
                                dc = dec[:, base + di, half * 4:half * 4 + 4].unsqueeze(2).broadcast_to([128, 4, 128])
                                self.tt('dve', dstt[:].rearrange("p (h c) -> p h c", c=128), sfv, dc, ALU.mult,
                                        [r_sf32[k], r_tab], [rdst])
                            if part == 9:
                                P.dma('sp', d['KF'][tok0:tok0 + 128, cols], sb16[k][:], reads=[r_sb16[k]],
                                      swrites=[r_scr['V']])
                                P.dma('sp', d['KB'][tok0:tok0 + 128, cols], sb16b[k][:], reads=[r_sb16b[k]],
                                      swrites=[r_scr['V']])
                            n1, n2 = ('QFT', 'QBT') if part == 8 else ('KFT', 'KBT')
                            dst1 = d[n1][half * 4:half * 4 + 4, :, tok0:tok0 + 128]
                            dst2 = d[n2][half * 4:half * 4 + 4, :, tok0:tok0 + 128]
                            pending.append(transposes([(sb16[k], r_sb16[k]), (sb16b[k], r_sb16b[k])], [dst1, dst2]))
                    self.drain(wgen)
                else:
                    j = ui - 24
                    for f in pending:
                        f()
                    pending = []
                    wv = wbuf[ws][:].rearrange("p k (q c) -> p k q c", q=4)
                    chunks = [(0, 512), (512, 1024), (1024, 1280)]
                    for ci, (t0, t1) in enumerate(chunks):
                        n = t1 - t0
                        par = (j * 3 + ci) % 2
                        for q in range(4):
                            pb = par * 4 + q
                            for kc in range(32):
                                self.mm(bank(pb)[:, 0:n], wv[:, kc, q, :], hT[:, kc, t0:t1], kc == 0, kc == 31,
                                        [r_wbuf[ws]] + r_hT[t0 // 128:t1 // 128], r_bank[pb])
                        self.drain(wgen, 6)
                        k = par
                        b0 = par * 4
                        tt0 = grp[t0 // 128]
                        g0 = tt0 * 128
                        self.act(dtmp[k][:, 0, 0:n], bank(b0 + 3)[:, 0:n], AF.Silu, [r_bank[b0 + 3]], [r_dtmp[k]])
                        self.cp('act', dtmp[k][:, 1, 0:n], bank(b0 + 2)[:, 0:n], [r_bank[b0 + 2]], (), [r_dtmp[k]])
                        self.tt('dve', dstg[k][:, 0, 0:n], bank(b0 + 1)[:, 0:n], dtmp[k][:, 0, 0:n], ALU.mult,
                                [r_bank[b0 + 1], r_dtmp[k]], [r_dstg[k]])
                        self.tt('dve', dstg[k][:, 1, 0:n], bank(b0)[:, 0:n], dtmp[k][:, 1, 0:n], ALU.mult,
                                [r_bank[b0], r_dtmp[k]], (), [r_dstg[k]])
                        P.dma('sp', d['GDT'][j * 128:(j + 1) * 128, g0:g0 + n], dstg[k][:, 0, 0:n],
                              reads=[r_dstg[k]], swrites=[r_scr['UD']])
                        P.dma('sp', d['UT'][j * 128:(j + 1) * 128, g0:g0 + n], dstg[k][:, 1, 0:n],
                              reads=[r_dstg[k]], swrites=[r_scr['UD']])
                    self.drain(wgen)
            for f in pending:
                f()
            pending = []
            P.barrier()

    _tb = 0
    def phase_mixers(self, l):
        self.mixer_A(l)
        self.P.barrier()
        self.mixer_B(l)
        self.P.barrier()
        self.mixer_C(l)
        self.P.barrier()
        self.mixer_D(l)

    def mixer_A(self, l):
        nc, P, A = self.nc, self.P, self.A
        d = self.d
        bank, bank_bf, r_bank = self.bank, self.bank_bf, self.r_bank
        identb, r_const = self.identb, self.r_const
        A.reset()
        r_y = P.res('Yw')
        pr = A([120, 128], F32, 'pr'); r_pr = P.res('pr')
        self.memset('dve', pr[:], 0.0, writes=[r_pr])
        P.dma('sp', pr[:, 48:79], d['na_bias'][l], writes=[r_pr])
        r_pb = P.res('pbrep')
        P.dma('sp', d['PBREP'][:, :, :], pr[:].unsqueeze(1).broadcast_to([120, 64, 128]), reads=[r_pr], writes=[r_pb])
        masks = A([128, 5, 640], F32, 'masks'); r_masks = P.res('masks')
        P.dma('sp', masks[:], d['k_namask'].rearrange("t p k -> p t k"), writes=[r_masks])
        NB = 2
        qT = [A([128, LS], BF16, 'qT%d' % i) for i in range(NB)]
        kT = [A([128, LS], BF16, 'kT%d' % i) for i in range(NB)]
        V = [A([128, 16, 128], BF16, 'V%d' % i) for i in range(NB)]
        ckl = [A([128, 4, 128], BF16, 'ckl%d' % i) for i in range(NB)]
        ckT = [A([128, 512], BF16, 'ckT%d' % i) for i in range(NB)]
        cV = [A([128, 4, 128], BF16, 'cV%d' % i) for i in range(NB)]
        gate = [A([128, 16, 128], BF16, 'gate%d' % i) for i in range(NB)]
        TB2 = [A([128, 15, 64], F32, 'TB2_%d' % i) for i in range(NB)]
        BT = [A([128, 5, 640], F32, 'BT%d' % i) for i in range(NB)]
        yst = [A([128, 16, 128], BF16, 'yst%d' % i) for i in range(NB)]
        r_h = [P.res('hA%d' % i) for i in range(NB)]
        r_ckl = [P.res() for i in range(NB)]
        r_ckT = [P.res() for i in range(NB)]
        r_TB2 = [P.res() for i in range(NB)]
        r_BT = [P.res() for i in range(NB)]
        r_yst = [P.res() for i in range(NB)]
        sc = [A([128, 1152], F32, 'sc%d' % i) for i in range(2)]
        pbf = [A([128, 1152], BF16, 'pbf%d' % i) for i in range(2)]
        PT = [A([128, 9, 128], BF16, 'PT%d' % i) for i in range(2)]
        stt_ = [A([128, 4], F32, 'st%d' % i) for i in range(2)]
        r_sc = [P.res() for i in range(2)]
        r_pbf = [P.res() for i in range(2)]
        r_PT = [P.res() for i in range(2)]
        r_st = [P.res() for i in range(2)]
        r_tail = [P.res('tail', bank=4 + i) for i in range(2)]
        r_t9 = [P.res('t9', bank=4 + i) for i in range(2)]
        r_o = [P.res('o', bank=4 + i) for i in range(2)]
        self._u = 0

        def unit(qTb, rq, loc, nloc, bias, rbias, ctx, rctx, vlist, out_dst, r_out, gate_ap, rgate, first_out):
            u = self._u % 2
            self._u += 1
            bA, bC, bM, bT = 0 + u, 2 + u, 4 + u, 6 + u
            n1 = min(512, nloc)
            tot = nloc + (512 if ctx is not None else 0)
            nblk = tot // 128
            self.mm(bank(bA)[:, 0:n1], qTb, loc[:, 0:n1], True, True, [rq], r_bank[bA])
            if nloc > 512:
                self.mm(bank(bM)[:, 0:nloc - 512], qTb, loc[:, 512:nloc], True, True, [rq], r_tail[u])
            if ctx is not None:
                self.mm(bank(bC)[:, 0:512], qTb, ctx, True, True, [rq, rctx], r_bank[bC])
            if bias is not None:
                self.stt(sc[u][:, 0:n1], bank(bA)[:, 0:n1], SCALE, bias[:, 0:n1], ALU.mult, ALU.add,
                         [r_bank[bA], rbias], [r_sc[u]])
                self.stt(sc[u][:, 512:nloc], bank(bM)[:, 0:nloc - 512], SCALE, bias[:, 512:nloc], ALU.mult, ALU.add,
                         [r_tail[u], rbias], (), [r_sc[u]])
            else:
                P.add('act', lambda e, o=sc[u][:, 0:n1], i=bank(bA)[:, 0:n1]: e.mul(o, i, SCALE),
                      [r_bank[bA]], [r_sc[u]])
            if ctx is not None:
                P.add('act', lambda e, o=sc[u][:, nloc:tot], i=bank(bC)[:, 0:512]: e.mul(o, i, SCALE),
                      [r_bank[bC]], (), [r_sc[u]])
            self.memset('dve', stt_[u][:], 0.0, writes=[r_st[u]])
            P.add('dve', lambda e, o=stt_[u][:, 0:1], i=sc[u][:, 0:tot]: e.reduce_max(out=o, in_=i, axis=AX.X),
                  [r_sc[u]], (), [r_st[u]])
            self.ts('dve', stt_[u][:, 1:2], stt_[u][:, 0:1], -1.0, None, ALU.mult, None, [r_st[u]], (), [r_st[u]])
            self.act(pbf[u][:, 0:tot], sc[u][:, 0:tot], AF.Exp, [r_sc[u], r_st[u]], [r_pbf[u]],
                     bias=stt_[u][:, 1:2], scale=1.0, accum_out=stt_[u][:, 2:3])
            P.add('dve', lambda e, o=stt_[u][:, 3:4], i=stt_[u][:, 2:3]: e.reciprocal(out=o, in_=i),
                  [r_pbf[u], r_st[u]], (), [r_st[u]])
            nb8 = min(8, nblk)
            for b in range(nb8):
                self.tr(bank_bf(bT)[:, b * 128:(b + 1) * 128], pbf[u][:, b * 128:(b + 1) * 128], identb[:],
                        [r_pbf[u], r_const], r_bank[bT], b == 0)
            if nblk == 9:
                self.tr(bank_bf(bM, 256, 384), pbf[u][:, 1024:1152], identb[:], [r_pbf[u], r_const], r_t9[u], True)
            self.cp('act', PT[u][:, 0:nb8, :].rearrange("p b q -> p (b q)"), bank_bf(bT)[:, 0:nb8 * 128],
                    [r_bank[bT]], [r_PT[u]])
            if nblk == 9:
                self.cp('dve', PT[u][:, 8, :], bank_bf(bM, 256, 384), [r_t9[u]], (), [r_PT[u]])
            for b in range(nblk):
                self.mm(bank(bM)[:, 256:384], PT[u][:, b, :], vlist[b][0], b == 0, b == nblk - 1,
                        [r_PT[u], vlist[b][1]], r_o[u])
            self.stt(out_dst, bank(bM)[:, 256:384], stt_[u][:, 3:4], gate_ap, ALU.mult, ALU.mult,
                     [r_o[u], r_st[u], rgate], [r_out] if first_out else (), () if first_out else [r_out])

        cl = l
        for h in range(8):
            n = h % NB
            hs = slice(h * 128, (h + 1) * 128)
            P.dma('sp', qT[n][:], d['QTA'][h, :, 0:LS], writes=[r_h[n]])
            P.dma('sp', kT[n][:], d['KTA'][h, :, 0:LS], swrites=[r_h[n]])
            P.dma('sp', V[n][:], d['VA'][0:LS, hs].rearrange("(t p) e -> p t e", p=128), swrites=[r_h[n]])
            P.dma('sp', gate[n][:], d['GT'][0:LS, hs].rearrange("(t p) e -> p t e", p=128), swrites=[r_h[n]])
            P.dma('pool', cV[n][:], d['c_na_v'][cl, :, hs].rearrange("(t p) e -> p t e", p=128), swrites=[r_h[n]])
            P.dma('pool', ckl[n][:], d['c_na_k'][cl, :, hs].rearrange("(t p) e -> p t e", p=128), writes=[r_ckl[n]])
            bT = 6 + (h % 2)
            for t in range(4):
                self.tr(bank_bf(bT)[:, t * 128:(t + 1) * 128], ckl[n][:, t, :], identb[:], [r_ckl[n], r_const],
                        r_bank[bT], t == 0)
            self.cp('dve', ckT[n][:], bank_bf(bT)[:, 0:512], [r_bank[bT]], [r_ckT[n]])
            for half in range(2):
                src = bass.AP(tensor=d['PBREP'].tensor, offset=(h * 15) * 8192 + 63,
                              ap=[[127, 64], [8192, 15], [1, 64]])
                if half == 0:
                    P.dma('sp', TB2[n][0:64, :, :], src, reads=[r_pb], writes=[r_TB2[n]])
                else:
                    P.dma('sp', TB2[n][64:128, :, :], src, reads=[r_pb], swrites=[r_TB2[n]])
            self.cp('pool', BT[n][:], masks[:], [r_masks], [r_BT[n]])
            for ty, qrel0 in enumerate((0, 2, 4, 6, 8)):
                for half in range(2):
                    qr = qrel0 + half
                    k0, k1 = max(0, qr - 7), min(9, qr + 7)
                    d0 = k0 - qr + 7
                    nk = k1 - k0 + 1
                    ps_ = slice(half * 64, (half + 1) * 64)
                    o = BT[n][ps_, ty, k0 * 64:(k1 + 1) * 64]
                    self.tt('dve', o, o, TB2[n][ps_, d0:d0 + nk, :].rearrange("p a b -> p (a b)"), ALU.add,
                            [r_TB2[n], r_BT[n]], (), [r_BT[n]])
            for qb in range(16):
                tw = min(max(qb - 2, 0), 11)
                ty = {0: 0, 1: 1, 14: 3, 15: 4}.get(qb, 2)
                vlist = [(V[n][:, tw + b, :], r_h[n]) for b in range(5)] + [(cV[n][:, b, :], r_h[n]) for b in range(4)]
                unit(qT[n][:, qb * 128:(qb + 1) * 128], r_h[n], kT[n][:, tw * 128:tw * 128 + 640], 640,
                     BT[n][:, ty, :], r_BT[n], ckT[n][:], r_ckT[n], vlist, yst[n][:, qb, :], r_yst[n],
                     gate[n][:, qb, :], r_h[n], qb == 0)
            P.dma('sp', d['Y'][0:LS, hs].rearrange("(t p) e -> p t e", p=128), yst[n][:], reads=[r_yst[n]],
                  swrites=[r_y])
        for sq in range(2):
            t0 = LS + sq * LP
            for h in range(8):
                n = h % NB
                hs = slice(h * 128, (h + 1) * 128)
                P.dma('sp', qT[n][:, 0:LP], d['QTA'][h, :, t0:t0 + LP], writes=[r_h[n]])
                P.dma('sp', kT[n][:, 0:LP], d['KTA'][h, :, t0:t0 + LP], swrites=[r_h[n]])
                P.dma('sp', V[n][:, 0:2, :], d['VA'][t0:t0 + LP, hs].rearrange("(t p) e -> p t e", p=128),
                      swrites=[r_h[n]])
                P.dma('sp', gate[n][:, 0:2, :], d['GT'][t0:t0 + LP, hs].rearrange("(t p) e -> p t e", p=128),
                      swrites=[r_h[n]])
                for qb in range(2):
                    vlist = [(V[n][:, b, :], r_h[n]) for b in range(2)]
                    unit(qT[n][:, qb * 128:(qb + 1) * 128], r_h[n], kT[n][:, 0:LP], LP, None, None, None, None,
                         vlist, yst[n][:, qb, :], r_yst[n], gate[n][:, qb, :], r_h[n], qb == 0)
                P.dma('sp', d['Y'][t0:t0 + LP, hs].rearrange("(t p) e -> p t e", p=128), yst[n][:, 0:2, :],
                      reads=[r_yst[n]], swrites=[r_y])

    def mixer_B(self, l):
        nc, P, A = self.nc, self.P, self.A
        d = self.d
        bank, bank_bf, r_bank = self.bank, self.bank_bf, self.r_bank
        identb, r_const = self.identb, self.r_const
        lam_init = 0.8 - 0.6 * math.exp(-0.3 * l)
        A.reset()
        r_y = P.res('Yw')
        lt = A([128, 4, 128], F32, 'lt'); r_lt = P.res('lt')
        P.dma('sp', lt[:].rearrange("p a b -> p (a b)"), d['diff_lam'][l:l + 1, :].broadcast_to([128, 512]),
              writes=[r_lt])
        lm = A([128, 2, 128], F32, 'lm')
        self.tt('dve', lm[:, 0, :], lt[:, 0, :], lt[:, 1, :], ALU.mult, [r_lt], [r_lt])
        self.tt('dve', lm[:, 1, :], lt[:, 2, :], lt[:, 3, :], ALU.mult, [r_lt], [r_lt])
        lam = A([128, 4], F32, 'lam')
        P.add('dve', lambda e: e.reduce_sum(out=lam[:, 0:2], in_=lm[:], axis=AX.X), [r_lt], [r_lt])
        self.act(lam[:, 0:2], lam[:, 0:2], AF.Exp, [r_lt], [r_lt])
        self.tt('dve', lam[:, 2:3], lam[:, 0:1], lam[:, 1:2], ALU.subtract, [r_lt], [r_lt])
        self.ts('dve', lam[:, 2:3], lam[:, 2:3], lam_init, None, ALU.add, None, [r_lt], [r_lt])
        wsub = A([128, 256], F32, 'wsub')
        P.dma('sp', wsub[:], d['diff_subln'][l:l + 1, :].broadcast_to([128, 256]), writes=[r_lt])
        self.ts('dve', wsub[:], wsub[:], 1.0 - lam_init, None, ALU.mult, None, [r_lt], [r_lt])

        NKB = (PAST + LS) // 128
        qT = [A([128, LS], BF16, 'bqT%d' % t) for t in range(2)]
        kT = [A([128, PAST + LS], BF16, 'bkT%d' % t) for t in range(2)]
        V = A([128, NKB, 256], BF16, 'bV')
        ckl = A([128, 4, 256], BF16, 'bckl')
        gate = A([128, 16, 256], BF16, 'bgate')
        yst = A([128, 16, 256], BF16, 'byst')
        r_h = P.res('hB'); r_ckl = P.res(); r_yst = P.res()
        ex = [A([128, PAST + LS], F32, 'ex%d' % t) for t in range(2)]
        r_ex = [P.res() for t in range(2)]
        abf = A([128, PAST + LS], BF16, 'abf'); r_abf = P.res()
        aT = A([128, NKB, 128], BF16, 'aT'); r_aT = P.res()
        stt_ = [A([128, 32], F32, 'bst%d' % i) for i in range(2)]
        r_st = [P.res() for i in range(2)]
        otmp = A([128, 256], F32, 'otmp'); r_otmp = P.res()
        junk = A([128, 256], F32, 'junk')
        r_o = P.res('bo')
        self._sb = 0
        self._ub = 0
        self._tbb = 0

        def unit(Lk, qcols, yout, first_out, gate_ap):
            u = self._ub % 2
            self._ub += 1
            st = stt_[u]
            nkb = Lk // 128
            nch = (Lk + 511) // 512
            self.memset('dve', st[:], 0.0, writes=[r_st[u]])
            chunks = []
            for t in range(2):
                for c in range(nch):
                    w = min(512, Lk - c * 512)
                    b = self._sb % 6
                    self._sb += 1
                    self.mm(bank(b)[:, 0:w], qT[t][:, qcols], kT[t][:, c * 512:c * 512 + w], True, True,
                            [r_h], r_bank[b])
                    P.add('dve', lambda e, o=st[:, t * 5 + c:t * 5 + c + 1], i=bank(b)[:, 0:w]:
                          e.reduce_max(out=o, in_=i, axis=AX.X), [r_bank[b]], (), [r_st[u]])
                    chunks.append((t, c, w, b))
                P.add('dve', lambda e, o=st[:, 10 + t:11 + t], i=st[:, t * 5:t * 5 + nch]:
                      e.reduce_max(out=o, in_=i, axis=AX.X), [r_st[u]], (), [r_st[u]])
                self.ts('dve', st[:, 12 + t:13 + t], st[:, 10 + t:11 + t], -SCALE, None, ALU.mult, None,
                        [r_st[u]], (), [r_st[u]])
                for (t_, c, w, b) in chunks[-nch:]:
                    self.act(ex[t][:, c * 512:c * 512 + w], bank(b)[:, 0:w], AF.Exp, [r_bank[b], r_st[u]],
                             [r_ex[t]] if c == 0 else (), () if c == 0 else [r_ex[t]],
                             scale=SCALE, bias=st[:, 12 + t:13 + t], accum_out=st[:, 14 + t * 5 + c:15 + t * 5 + c])
                P.add('dve', lambda e, o=st[:, 24 + t:25 + t], i=st[:, 14 + t * 5:14 + t * 5 + nch]:
                      e.reduce_sum(out=o, in_=i, axis=AX.X), [r_st[u], r_ex[t]], (), [r_st[u]])
            P.add('dve', lambda e, o=st[:, 26:28], i=st[:, 24:26]: e.reciprocal(out=o, in_=i), [r_st[u]], (), [r_st[u]])
            self.tt('dve', st[:, 28:29], st[:, 27:28], lam[:, 2:3], ALU.mult, [r_st[u], r_lt], (), [r_st[u]])
            self.ts('pool', ex[1][:, 0:Lk], ex[1][:, 0:Lk], st[:, 28:29], None, ALU.mult, None,
                    [r_ex[1], r_st[u]], [r_ex[1]])
            self.stt(abf[:, 0:Lk], ex[0][:, 0:Lk], st[:, 26:27], ex[1][:, 0:Lk], ALU.mult, ALU.subtract,
                     [r_ex[0], r_ex[1], r_st[u]], [r_abf])
            kb = 0
            first = True
            while kb < nkb:
                nb = min(8, nkb - kb)
                bT = 6 + (self._tbb % 2)
                self._tbb += 1
                for j in range(nb):
                    self.tr(bank_bf(bT)[:, j * 128:(j + 1) * 128], abf[:, (kb + j) * 128:(kb + j + 1) * 128], identb[:],
                            [r_abf, r_const], r_bank[bT], j == 0)
                self.cp('act' if (self._tbb % 2) else 'dve', aT[:, kb:kb + nb, :].rearrange("p b q -> p (b q)"),
                        bank_bf(bT)[:, 0:nb * 128], [r_bank[bT]], [r_aT] if first else (), () if first else [r_aT])
                first = False
                kb += nb
            bo = self._sb % 6
            self._sb += 1
            for b_ in range(nkb):
                self.mm(bank(bo)[:, 0:256], aT[:, b_, :], V[:, b_, :], b_ == 0, b_ == nkb - 1, [r_aT, r_h], r_bank[bo])
            self.act(junk[:], bank(bo)[:, 0:256], AF.Square, [r_bank[bo]], [r_otmp], accum_out=st[:, 29:30])
            self.ts('dve', st[:, 30:31], st[:, 29:30], 1.0 / 256.0, EPS, ALU.mult, ALU.add, [r_st[u], r_otmp], (), [r_st[u]])
            self.act(st[:, 30:31], st[:, 30:31], AF.Ln, [r_st[u]], (), [r_st[u]])
            self.act(st[:, 30:31], st[:, 30:31], AF.Exp, [r_st[u]], (), [r_st[u]], scale=-0.5)
            self.stt(otmp[:], bank(bo)[:, 0:256], st[:, 30:31], wsub[:], ALU.mult, ALU.mult,
                     [r_bank[bo], r_st[u], r_lt], [r_otmp])
            self.tt('pool', yout, otmp[:], gate_ap, ALU.mult, [r_otmp, r_h], [r_yst] if first_out else (),
                    () if first_out else [r_yst])

        for h in range(4):
            vs = slice(h * 256, (h + 1) * 256)
            P.dma('pool', ckl[:], d['c_df_k'][l, :, vs].rearrange("(t p) e -> p t e", p=128), writes=[r_ckl])
            first = True
            for t in range(2):
                P.dma('sp', qT[t][:], d['QTB'][h * 2 + t, :, 0:LS], writes=[r_h] if first else (),
                      swrites=() if first else [r_h])
                first = False
                P.dma('sp', kT[t][:, PAST:PAST + LS], d['KTB'][h * 2 + t, :, 0:LS], swrites=[r_h])
                bT = 6 + (t % 2)
                for tb in range(4):
                    self.tr(bank_bf(bT)[:, tb * 128:(tb + 1) * 128], ckl[:, tb, t * 128:(t + 1) * 128], identb[:],
                            [r_ckl, r_const], r_bank[bT], tb == 0)
                self.cp('dve', kT[t][:, 0:PAST], bank_bf(bT)[:, 0:512], [r_bank[bT]], (), [r_h])
            P.dma('pool', V[:, 0:4, :], d['c_df_v'][l, :, vs].rearrange("(t p) e -> p t e", p=128), swrites=[r_h])
            P.dma('sp', V[:, 4:20, :], d['VB'][0:LS, vs].rearrange("(t p) e -> p t e", p=128), swrites=[r_h])
            P.dma('sp', gate[:], d['GT'][0:LS, GW + h * 256:GW + (h + 1) * 256].rearrange("(t p) e -> p t e", p=128),
                  swrites=[r_h])
            for qb in range(16):
                unit(PAST + LS, slice(qb * 128, (qb + 1) * 128), yst[:, qb, :], qb == 0, gate[:, qb, :])
            P.dma('sp', d['Y'][0:LS, GW + h * 256:GW + (h + 1) * 256].rearrange("(t p) e -> p t e", p=128), yst[:],
                  reads=[r_yst], swrites=[r_y])
            for sq in range(2):
                t0 = LS + sq * LP
                first = True
                for t in range(2):
                    P.dma('sp', qT[t][:, 0:LP], d['QTB'][h * 2 + t, :, t0:t0 + LP], writes=[r_h] if first else (),
                          swrites=() if first else [r_h])
                    first = False
                    P.dma('sp', kT[t][:, 0:LP], d['KTB'][h * 2 + t, :, t0:t0 + LP], swrites=[r_h])
                P.dma('sp', V[:, 0:2, :], d['VB'][t0:t0 + LP, vs].rearrange("(t p) e -> p t e", p=128), swrites=[r_h])
                P.dma('sp', gate[:, 0:2, :],
                      d['GT'][t0:t0 + LP, GW + h * 256:GW + (h + 1) * 256].rearrange("(t p) e -> p t e", p=128),
                      swrites=[r_h])
                for qb in range(2):
                    unit(LP, slice(qb * 128, (qb + 1) * 128), yst[:, qb, :], qb == 0, gate[:, qb, :])
                P.dma('sp', d['Y'][t0:t0 + LP, GW + h * 256:GW + (h + 1) * 256].rearrange("(t p) e -> p t e", p=128),
                      yst[:, 0:2, :], reads=[r_yst], swrites=[r_y])
    def mixer_C(self, l):
        nc, P, A = self.nc, self.P, self.A
        d = self.d
        bank, bank_bf, r_bank = self.bank, self.bank_bf, self.r_bank
        cdt, r_tab = self.tabs['cdt'], self.tabs['r_tab']
        A.reset()
        r_y = P.res('Yw')
        rmask = A([128, 2, 128], F32, 'rmask'); r_rm = P.res('rmask')
        P.dma('sp', rmask[:], d['k_rmask'].rearrange("a j i -> j a i"), writes=[r_rm])
        NS = 2
        names = ('QFT', 'QBT', 'KFT', 'KBT')
        fT = [[A([128, LS], BF16, 'c%s%d' % (nm, i)) for nm in names] for i in range(NS)]
        tk = [[A([128, 16, 128], BF16, 'c%s%d' % (nm, i)) for nm in ('KF', 'KB', 'VC')] for i in range(NS)]
        gate = [A([128, 16, 128], BF16, 'cg%d' % i) for i in range(NS)]
        oacc = [A([128, 16, 128], F32, 'oacc%d' % i) for i in range(NS)]
        tmp1 = [A([128, 16, 128], F32, 'ctmp%d' % i) for i in range(NS)]
        yst = [A([128, 16, 128], BF16, 'cy%d' % i) for i in range(NS)]
        S = [[A([128, 128], F32, 'S%d_%d' % (i, dr)) for dr in range(2)] for i in range(NS)]
        Sb = [[A([128, 128], BF16, 'Sb%d_%d' % (i, dr)) for dr in range(2)] for i in range(NS)]
        atm = [[A([128, 128], BF16, 'atm%d_%d' % (i, dr)) for dr in range(2)] for i in range(NS)]
        stat = [A([128, 4, 16], F32, 'cst%d' % i) for i in range(NS)]
        r_h = [P.res() for i in range(NS)]
        r_o = [P.res() for i in range(NS)]
        r_S = [[P.res() for dr in range(2)] for i in range(NS)]
        r_Sb = [[P.res() for dr in range(2)] for i in range(NS)]
        r_atm = [[P.res() for dr in range(2)] for i in range(NS)]
        r_stat = [P.res() for i in range(NS)]
        r_yst = [P.res() for i in range(NS)]
        r_pa = [[P.res('pa', bank=i * 2 + dr) for dr in range(2)] for i in range(NS)]
        r_po = [[P.res('po', bank=i * 2 + dr) for dr in range(2)] for i in range(NS)]
        r_pd = [[P.res('pd', bank=i * 2 + dr) for dr in range(2)] for i in range(NS)]

        seqs = [(0, LS, True, None)] + [(LS + sq * LP, LP, False, sq) for sq in range(2)]
        for (t0, L, is_s, sq) in seqs:
            ncn = L // 128
            for hg in range(0, 8, NS):
                for i in range(NS):
                    h = hg + i
                    hs = slice(h * 128, (h + 1) * 128)
                    first = True
                    for j, nm in enumerate(names):
                        P.dma('sp', fT[i][j][:, 0:L], d[nm][h, :, t0:t0 + L], writes=[r_h[i]] if first else (),
                              swrites=() if first else [r_h[i]])
                        first = False
                    for j, nm in enumerate(('KF', 'KB', 'VC')):
                        P.dma('sp', tk[i][j][:, 0:ncn, :], d[nm][t0:t0 + L, hs].rearrange("(t p) e -> p t e", p=128),
                              swrites=[r_h[i]])
                    P.dma('sp', gate[i][:, 0:ncn, :],
                          d['GT'][t0:t0 + L, 2 * GW + h * 128:2 * GW + (h + 1) * 128].rearrange("(t p) e -> p t e", p=128),
                          swrites=[r_h[i]])
                    for dr in range(2):
                        if is_s:
                            P.dma('sp', S[i][dr][:], d['st_ret'][l, dr, h], writes=[r_S[i][dr]])
                        else:
                            self.memset('dve', S[i][dr][:], 0.0, writes=[r_S[i][dr]])
                        self.cp('pool', Sb[i][dr][:], S[i][dr][:], [r_S[i][dr]], [r_Sb[i][dr]])
                for step in range(ncn):
                    for i in range(NS):
                        h = hg + i
                        for dr in range(2):
                            c = step if dr == 0 else ncn - 1 - step
                            cs = slice(c * 128, (c + 1) * 128)
                            pb = i * 2 + dr
                            qTt, kTt = fT[i][dr], fT[i][2 + dr]
                            ktok, vtok = tk[i][dr], tk[i][2]
                            self.mm(bank(pb)[:, 0:128], kTt[:, cs], qTt[:, cs], True, True, [r_h[i]], r_pa[i][dr])
                            self.tt('dve', atm[i][dr][:], bank(pb)[:, 0:128], rmask[:, dr, :], ALU.mult,
                                    [r_pa[i][dr], r_rm], [r_atm[i][dr]])
                            self.mm(bank(pb)[:, 128:256], atm[i][dr][:], vtok[:, c, :], True, False,
                                    [r_atm[i][dr], r_h[i]], r_po[i][dr], first=True)
                            self.mm(bank(pb)[:, 128:256], qTt[:, cs], Sb[i][dr][:], False, True,
                                    [r_h[i], r_Sb[i][dr]], r_po[i][dr], first=False)
                            first_touch = (step < (ncn + 1) // 2) if ncn > 1 else (dr == 0)
                            if ncn % 2 == 1 and step == ncn // 2:
                                first_touch = (dr == 0)
                            if first_touch:
                                self.cp('act', oacc[i][:, c, :], bank(pb)[:, 128:256], [r_po[i][dr]], (), [r_o[i]])
                            else:
                                self.tt('dve', oacc[i][:, c, :], oacc[i][:, c, :], bank(pb)[:, 128:256], ALU.add,
                                        [r_po[i][dr], r_o[i]], [r_o[i]])
                            self.mm(bank(pb)[:, 256:384], ktok[:, c, :], vtok[:, c, :], True, True, [r_h[i]], r_pd[i][dr])
                            cd = cdt[:, dr * 8 + h:dr * 8 + h + 1]
                            self.ts('dve', S[i][dr][:], S[i][dr][:], cd, None, ALU.mult, None, [r_S[i][dr], r_tab],
                                    [r_S[i][dr]])
                            self.stt(S[i][dr][:], bank(pb)[:, 256:384], cd, S[i][dr][:], ALU.mult, ALU.add,
                                     [r_pd[i][dr], r_S[i][dr], r_tab], [r_S[i][dr]])
                            self.cp('pool', Sb[i][dr][:], S[i][dr][:], [r_S[i][dr]], [r_Sb[i][dr]])
                for i in range(NS):
                    h = hg + i
                    o3 = oacc[i][:, 0:ncn, :]
                    t3 = tmp1[i][:, 0:ncn, :]
                    sm, sq2, mean, rstd = (stat[i][:, k, 0:ncn] for k in range(4))
                    P.add('dve', lambda e, o=sm, i_=o3: e.reduce_sum(out=o, in_=i_, axis=AX.X), [r_o[i]], [r_stat[i]])
                    self.tt('pool', t3, o3, o3, ALU.mult, [r_o[i]], [r_yst[i]])
                    P.add('dve', lambda e, o=sq2, i_=t3: e.reduce_sum(out=o, in_=i_, axis=AX.X), [r_yst[i]], (),
                          [r_stat[i]])
                    self.ts('dve', mean, sm, 1.0 / 128.0, None, ALU.mult, None, [r_stat[i]], [r_stat[i]])
                    self.tt('dve', sm, mean, mean, ALU.mult, [r_stat[i]], [r_stat[i]])
                    self.stt(sq2, sq2, 1.0 / 128.0, sm, ALU.mult, ALU.subtract, [r_stat[i]], [r_stat[i]])
                    self.rstd_from(rstd, sq2, r_stat[i])
                    self.tt('dve', t3, o3, mean.unsqueeze(2).broadcast_to([128, ncn, 128]), ALU.subtract,
                            [r_o[i], r_stat[i]], [r_yst[i]])
                    self.tt('pool', t3, t3, rstd.unsqueeze(2).broadcast_to([128, ncn, 128]), ALU.mult,
                            [r_yst[i], r_stat[i]], [r_yst[i]])
                    self.tt('dve', yst[i][:, 0:ncn, :], t3, gate[i][:, 0:ncn, :], ALU.mult, [r_yst[i], r_h[i]],
                            [r_yst[i]])
                    P.dma('sp', d['Y'][t0:t0 + L, 2 * GW + h * 128:2 * GW + (h + 1) * 128].rearrange("(t p) e -> p t e", p=128),
                          yst[i][:, 0:ncn, :], reads=[r_yst[i]], swrites=[r_y])
                    if not is_s:
                        for dr in range(2):
                            P.dma('sp', d['o_st'][sq, l, dr, h], S[i][dr][:], reads=[r_S[i][dr]],
                                  swrites=[self.r_scr['OUT']])

    def mixer_D(self, l):
        nc, P, A = self.nc, self.P, self.A
        d = self.d
        A.reset()
        r_y = P.res('Yw')
        NB = 2
        Lmax = LS
        u = [A([128, Lmax + 2], F32, 'du%d' % i) for i in range(NB)]
        gd = [A([128, Lmax], F32, 'dgd%d' % i) for i in range(NB)]
        tacc = [A([128, Lmax], F32, 'dt%d' % i) for i in range(NB)]
        yb = [A([128, Lmax], BF16, 'dy%d' % i) for i in range(NB)]
        cw = [A([128, 3], F32, 'cw%d' % i) for i in range(NB)]
        r_in = [P.res() for i in range(NB)]
        r_t = [P.res() for i in range(NB)]
        r_yb = [P.res() for i in range(NB)]
        n = 0
        for j in range(8):
            rows = slice(j * 128, (j + 1) * 128)
            for (t0, L) in ((0, LS), (LS, LP), (LS + LP, LP)):
                b = n % NB
                n += 1
                self.memset('dve', u[b][:, 0:1], 0.0, writes=[r_in[b]])
                self.memset('dve', u[b][:, L + 1:L + 2], 0.0, swrites=[r_in[b]])
                P.dma('sp', u[b][:, 1:L + 1], d['UT'][rows, t0:t0 + L], swrites=[r_in[b]])
                P.dma('sp', gd[b][:, 0:L], d['GDT'][rows, t0:t0 + L], swrites=[r_in[b]])
                P.dma('sp', cw[b][:], d['conv_w'][l, rows, :], swrites=[r_in[b]])
                self.ts('pool', tacc[b][:, 0:L], u[b][:, 0:L], cw[b][:, 0:1], None, ALU.mult, None, [r_in[b]], [r_t[b]])
                self.stt(tacc[b][:, 0:L], u[b][:, 1:L + 1], cw[b][:, 1:2], tacc[b][:, 0:L], ALU.mult, ALU.add,
                         [r_in[b], r_t[b]], [r_t[b]])
                self.stt(tacc[b][:, 0:L], u[b][:, 2:L + 2], cw[b][:, 2:3], tacc[b][:, 0:L], ALU.mult, ALU.add,
                         [r_in[b], r_t[b]], [r_t[b]])
                self.tt('pool', yb[b][:, 0:L], tacc[b][:, 0:L], gd[b][:, 0:L], ALU.mult, [r_t[b], r_in[b]], [r_yb[b]])
                P.dma('sp', d['YTD'][rows, t0:t0 + L], yb[b][:, 0:L], reads=[r_yb[b]], swrites=[r_y])

    def phase_EF(self, l):
        nc, P, A = self.nc, self.P, self.A
        d = self.d
        bank, bank_bf, r_bank = self.bank, self.bank_bf, self.r_bank
        identb, r_const = self.identb, self.r_const
        A.off = self.base_mark
        r_z = P.res('Zw')
        gbc = A([128, 2, D], F32, 'gbc'); r_g = P.res('gbc')
        for cv in range(2):
            P.dma('sp', gbc[:, cv, :], d['MADA'][l, cv:cv + 1, 2 * D:3 * D].broadcast_to([128, D]),
                  writes=[r_g] if cv == 0 else (), swrites=() if cv == 0 else [r_g])
        yT = A([128, 32, 1280], BF16, 'yT')
        r_yT = [P.res() for s in range(10)]
        wbuf = [A([128, 32, 512], BF16, 'wo%d' % i) for i in range(2)]
        r_wbuf = [P.res() for i in range(2)]
        ytile = [A([128, 3 * GW], BF16, 'ytile0')] * 2
        r_ytile = [P.res()] * 2
        xch = [A([128, 512], F32, 'xch%d' % i) for i in range(2)] + [None]
        xch[2] = xch[0]
        r_xch = [P.res() for i in range(2)]
        r_xch.append(r_xch[0])
        zst = [A([128, 512], F32, 'zst%d' % i) for i in range(2)] + [None]
        zst[2] = zst[0]
        r_zst = [P.res() for i in range(2)]
        r_zst.append(r_zst[0])
        self.alloc_stg()
        w_o = d['w_out'][l].rearrange("(kc p) n -> p kc n", p=128)
        for gi, grp in enumerate(self.groups):
            tbi = 0
            for s, tt in enumerate(grp):
                b = s % 2
                tok0 = tt * 128
                P.dma('sp', ytile[b][:], d['Y'][tok0:tok0 + 128, :], writes=[r_ytile[b]])
                P.dma('sp', yT[:, 24:32, s * 128:(s + 1) * 128],
                      d['YTD'][:, tok0:tok0 + 128].rearrange("(j p) t -> p j t", p=128), swrites=[r_yT[s]])
                for k8 in range(3):
                    bT = 4 + (tbi % 4)
                    tbi += 1
                    for j in range(8):
                        kc = k8 * 8 + j
                        self.tr(bank_bf(bT)[:, j * 128:(j + 1) * 128], ytile[b][:, kc * 128:(kc + 1) * 128], identb[:],
                                [r_ytile[b], r_const], r_bank[bT], j == 0)
                    self.cp('act' if k8 % 2 == 0 else 'dve', yT[:, k8 * 8:(k8 + 1) * 8, s * 128:(s + 1) * 128],
                            bank_bf(bT)[:, 0:1024].rearrange("p (j t) -> p j t", j=8), [r_bank[bT]], (), [r_yT[s]])
            self.drain(self.w_pieces(wbuf[0][:], r_wbuf[0], w_o[:, :, 0:512]))
            acc_i = 0
            xi = 0
            for oc in range(8):
                ws = oc % 2
                wgen = None
                if oc + 1 < 8:
                    wgen = self.w_pieces(wbuf[(oc + 1) % 2][:], r_wbuf[(oc + 1) % 2],
                                         w_o[:, :, (oc + 1) * 512:(oc + 2) * 512])
                cols = slice(oc * 512, (oc + 1) * 512)
                for s, tt in enumerate(grp):
                    pb = acc_i % 4
                    acc_i += 1
                    k = xi % 2
                    xi += 1
                    cv = 0 if tt < 16 else 1
                    P.dma('sp', xch[k][:], self.x_src(l, tt, oc * 512, (oc + 1) * 512), writes=[r_xch[k]])
                    for kc in range(32):
                        self.mm(bank(pb), yT[:, kc, s * 128:(s + 1) * 128], wbuf[ws][:, kc, :], kc == 0, kc == 31,
                                [r_yT[s], r_wbuf[ws]], r_bank[pb])
                    self.tt('dve', zst[k][:], bank(pb), gbc[:, cv, cols], ALU.mult, [r_bank[pb], r_g], [r_zst[k]])
                    self.stt(zst[k][:], xch[k][:], ALPHA, zst[k][:], ALU.mult, ALU.add, [r_xch[k], r_zst[k]],
                             [r_zst[k]], eng='dve')
                    P.dma('sp', d['Z'][tt * 128:(tt + 1) * 128, cols], zst[k][:], reads=[r_zst[k]], swrites=[r_z])
                    self.drain(wgen, 2)
                self.drain(wgen)
            P.barrier()
        A.off = self.base_mark
        gb = A([128, 2, D], F32, 'lngb'); r_gb = P.res('lngb')
        P.dma('sp', gb[:, 0, :], d['ln_g'][l:l + 1, :].broadcast_to([128, D]), writes=[r_gb])
        P.dma('sp', gb[:, 1, :], d['ln_b'][l:l + 1, :].broadcast_to([128, D]), swrites=[r_gb])
        zt = [A([128, D], F32, 'zt%d' % i) for i in range(2)]
        r_zt = [P.res() for i in range(2)]
        st = [A([128, 8, 6], F32, 'fst%d' % i) for i in range(2)]
        mv = [A([128, 4], F32, 'fmv%d' % i) for i in range(2)]
        r_mv = [P.res() for i in range(2)]
        r_x = P.res('Xw')
        for tt in range(NT):
            b = tt % 2
            P.dma('sp', zt[b][:], d['Z'][tt * 128:(tt + 1) * 128, :], writes=[r_zt[b]])
            for c8 in range(8):
                P.add('dve', lambda e, o=st[b][:, c8, :], i=zt[b][:, c8 * 512:(c8 + 1) * 512]: e.bn_stats(out=o, in_=i),
                      reads=[r_zt[b]], writes=[r_mv[b]] if c8 == 0 else (), swrites=() if c8 == 0 else [r_mv[b]])
            P.add('dve', lambda e, o=mv[b][:, 0:2], i=st[b][:].rearrange("p a b -> p (a b)"): e.bn_aggr(out=o, in_=i),
                  reads=[r_mv[b]], writes=[r_mv[b]])
            self.rstd_from(mv[b][:, 2:3], mv[b][:, 1:2], r_mv[b])
            self.stt(mv[b][:, 3:4], mv[b][:, 0:1], -1.0, mv[b][:, 2:3], ALU.mult, ALU.mult, [r_mv[b]], [r_mv[b]])
            self.act(zt[b][:], zt[b][:], AF.Identity, [r_zt[b], r_mv[b]], [r_zt[b]], scale=mv[b][:, 2:3],
                     bias=mv[b][:, 3:4])
            self.tt('pool', zt[b][:], zt[b][:], gb[:, 0, :], ALU.mult, [r_zt[b], r_gb], [r_zt[b]])
            self.tt('dve', zt[b][:], zt[b][:], gb[:, 1, :], ALU.add, [r_zt[b], r_gb], [r_zt[b]])
            if l == DEPTH - 1:
                if tt < 16:
                    dst = d['o_ys'][tt * 128:(tt + 1) * 128, :]
                else:
                    dst = d['o_yp'][(tt - 16) * 128:(tt - 15) * 128, :]
            else:
                dst = d['XRES'][tt * 128:(tt + 1) * 128, :]
            P.dma('sp', dst, zt[b][:], reads=[r_zt[b]], swrites=[r_x])


def _consts():
    k = {}
    k['k_ident'] = np.eye(128, dtype=np.float32)
    t = np.arange(LS)
    nf = 32
    inv = (10000.0 ** (-np.arange(nf, dtype=np.float32) / nf)).astype(np.float32)
    ang_r = ((t // 64).astype(np.float32)[:, None] * inv).astype(np.float32)
    ang_c = ((t % 64).astype(np.float32)[:, None] * inv).astype(np.float32)
    cos = np.zeros((LS, 2, 2, 32), np.float32)
    sin = np.zeros((LS, 2, 2, 32), np.float32)
    for a, ang in enumerate((ang_r, ang_c)):
        cos[:, a, 0] = np.cos(ang); cos[:, a, 1] = np.cos(ang)
        sin[:, a, 0] = -np.sin(ang); sin[:, a, 1] = np.sin(ang)
    k['k_cos'] = cos.reshape(LS, 128)
    k['k_sin'] = sin.reshape(LS, 128)
    rows = LS // 64
    m = np.full((5, 128, 640), NEG, np.float32)
    col = np.arange(64)
    c0 = np.clip(col - 8, 0, 48)
    col_ok = (col[None, :] >= c0[:, None]) & (col[None, :] < c0[:, None] + 16)
    for ty, qb in enumerate((0, 1, 5, 14, 15)):
        tw = min(max(qb - 2, 0), 11)
        for half in range(2):
            r = 2 * qb + half
            w0 = min(max(r - 4, 0), rows - 8)
            for kr in range(10):
                ra = 2 * tw + kr
                if w0 <= ra < w0 + 8:
                    blk = np.where(col_ok, 0.0, NEG).astype(np.float32)
                    m[ty, half * 64:(half + 1) * 64, kr * 64:(kr + 1) * 64] = blk
    k['k_namask'] = m
    j = np.arange(128)[:, None]
    i = np.arange(128)[None, :]
    k['k_rmask'] = np.stack([(j <= i), (j >= i)]).astype(np.float32)
    ii = np.arange(128, dtype=np.float32)
    k['k_coef'] = np.stack([ii + 1.0, -(ii + 1.0), 128.0 - ii, -(128.0 - ii)], 1).astype(np.float32)
    return k


_CACHE = {}


def _get_builder(debug=False, stop_after=None):
    key = (debug, stop_after)
    if key not in _CACHE:
        b = Builder(debug=debug, stop_after=stop_after)
        b.stats = b.build()
        _CACHE[key] = b
    return _CACHE[key]


def make_in_maps(inputs, cores):
    f = lambda a: np.ascontiguousarray(np.asarray(a, dtype=np.float32))
    k = _consts()
    shared = dict(
        w_ada=f(inputs['w_ada']), b_ada=f(inputs['b_ada']), w_in=f(inputs['w_in']), w_out=f(inputs['w_out']),
        ln_g=f(inputs['ln_g']), ln_b=f(inputs['ln_b']),
        na_bias=f(inputs['na_bias']).reshape(DEPTH, 120, 31),
        diff_lam=f(inputs['diff_lam']).reshape(DEPTH, 512), diff_subln=f(inputs['diff_subln']),
        ret_decay=f(inputs['ret_decay']).reshape(DEPTH, 16), conv_w=f(inputs['conv_w']), **k)
    xs, xp = f(inputs['x_sample']), f(inputs['x_prompt'])
    maps = []
    for i in cores:
        m = dict(shared)
        m['x_s'] = xs[i]
        m['x_p'] = xp[2 * i:2 * i + 2].reshape(2 * LP, D)
        m['c_na_k'] = f(inputs['cache_na_k'][i]).reshape(DEPTH, PAST, 1024)
        m['c_na_v'] = f(inputs['cache_na_v'][i]).reshape(DEPTH, PAST, 1024)
        m['c_df_k'] = f(inputs['cache_diff_k'][i]).reshape(DEPTH, PAST, 1024)
        m['c_df_v'] = f(inputs['cache_diff_v'][i]).reshape(DEPTH, PAST, 1024)
        m['st_ret'] = f(inputs['state_ret'][i])
        m['cvec'] = np.stack([f(inputs['c'])[i], f(inputs['c_ctx'])])
        maps.append(m)
    return maps


def kernel(**inputs):
    n = 8
    b = _get_builder()
    maps = make_in_maps(inputs, range(n))
    res = run_bass_kernel_spmd(b.nc, maps, core_ids=list(range(n)))
    R = res.results
    y_s = np.stack([R[i]['o_ys'] for i in range(n)])
    y_p = np.concatenate([R[i]['o_yp'].reshape(2, LP, D) for i in range(n)])

    def cat(nm, shape):
        return np.concatenate([R[i][nm].reshape((2, DEPTH, LP) + shape) for i in range(n)])
    nak = cat('o_nak', (8, HD))
    nav = cat('o_nav', (8, HD))
    dfk = cat('o_dfk', (4, 2, HD))
    dfv = cat('o_dfv', (4, 2 * HD))
    st = np.concatenate([R[i]['o_st'] for i in range(n)])
    return (y_p.astype(np.float32), y_s.astype(np.float32), nak, nav, dfk, dfv, st)
```

```python
import math
import contextlib
import numpy as np
import concourse.bass as bass
import concourse.mybir as mybir
from concourse.bass_utils import run_bass_kernel_spmd

F32 = mybir.dt.float32
BF16 = mybir.dt.bfloat16
AF = mybir.ActivationFunctionType
ALU = mybir.AluOpType
AX = mybir.AxisListType

D = 4096
DEPTH = 2
LS = 2048
LP = 256
NTOK = LS + 2 * LP
NT = NTOK // 128
PAST = 512
HD = 128
GW = 1024
DIN = 16384
ALPHA = (2 * DEPTH) ** 0.25
EPS = 1e-6
SCALE = HD ** -0.5
NEG = -30000.0
SB_LO = 16512
SB_HI = 229344

ENGS = ('pe', 'act', 'dve', 'pool', 'sp')


class Res:
    __slots__ = ('name', 'wc', 'wd', 'rc', 'rd', 'war_c', 'war_d', 'bank', 'gen')

    def __init__(self, name, bank=None):
        self.name = name
        self.bank = bank
        self.gen = None
        self.wc = {}
        self.wd = []
        self.rc = {}
        self.rd = []
        self.war_c = {}
        self.war_d = []


class Prog:
    def __init__(self, nc, n_sp=24, n_pool=16, n_act=4):
        self.nc = nc
        self.ins = []
        self.ring_sizes = {'sp': n_sp, 'pool': n_pool, 'act': n_act}
        self.all_res = []
        self.locks = {}

    def res(self, name='', bank=None):
        r = Res(name, bank)
        self.all_res.append(r)
        return r

    def add(self, eng, fn, reads=(), writes=(), swrites=(), dma=False):
        idx = len(self.ins)
        deps = set()
        for r in reads:
            deps.update(r.wc.values())
            deps.update(r.wd)
        for r in writes:
            deps.update(r.wc.values()); deps.update(r.wd)
            deps.update(r.rc.values()); deps.update(r.rd)
            deps.update(r.war_c.values()); deps.update(r.war_d)
        for r in swrites:
            if r.rc or r.rd:
                r.war_c, r.war_d = r.rc, r.rd
                r.rc, r.rd = {}, []
                r.wc, r.wd = {}, []
                r.gen = None
            deps.update(r.war_c.values()); deps.update(r.war_d)
            if r.gen is not None:
                deps.add(r.gen)
        for r in reads:
            if dma:
                r.rd.append(idx)
            else:
                r.rc[eng] = idx
        for r in writes:
            wc_ = dict(r.wc)
            for k_, v_ in r.rc.items():
                if wc_.get(k_, -1) < v_:
                    wc_[k_] = v_
            r.war_c, r.war_d = wc_, r.rd + r.wd
            r.rc, r.rd = {}, []
            r.gen = idx
            if dma:
                r.wc, r.wd = {}, [idx]
            else:
                r.wc, r.wd = {eng: idx}, []
        for r in swrites:
            if dma:
                r.wd.append(idx)
            else:
                r.wc[eng] = idx
        banks = None
        for grp_ in (reads, writes, swrites):
            for r in grp_:
                if r.bank is not None:
                    if banks is None:
                        banks = set()
                    banks.add(r.bank)
        if banks:
            for b_ in banks:
                L = self.locks.setdefault(b_, {})
                for e2, i2 in L.items():
                    if e2 != eng:
                        deps.add(i2)
                L[eng] = idx
        deps.discard(idx)
        self.ins.append([eng, fn, deps, dma])
        return idx

    def dma(self, q, out, in_, reads=(), writes=(), swrites=(), **kw):
        def fn(e, out=out, in_=in_, kw=kw):
            return e.dma_start(out=out, in_=in_, **kw)
        return self.add(q, fn, reads, writes, swrites, dma=True)

    def barrier(self):
        allr = self.res('barrier')
        deps = set()
        for r in self.all_res:
            deps.update(r.wc.values()); deps.update(r.wd)
            deps.update(r.rc.values()); deps.update(r.rd)
            deps.update(r.war_c.values()); deps.update(r.war_d)
        first = True
        for rnd in range(2):
            for eng in ENGS:
                def fn(e):
                    return e.nop()
                if rnd == 0:
                    i = self.add(eng, fn, writes=[allr])
                    if first:
                        self.ins[i][2].update(deps)
                        first = False
                else:
                    self.add(eng, fn, reads=[allr])
        for r in self.all_res:
            if r is allr:
                continue
            r.wc, r.wd, r.rc, r.rd, r.war_c, r.war_d = {}, [], {}, [], {}, []
            r.gen = None
        self.locks = {}
        self.all_res = [r for r in self.all_res if r is allr or getattr(r, 'name', '') != 'barrier']

    def emit(self):
        nc = self.nc
        ins = self.ins
        n = len(ins)
        engobj = {'pe': nc.tensor, 'act': nc.scalar, 'dve': nc.vector, 'pool': nc.gpsimd, 'sp': nc.sync}
        needed = [False] * n
        for i in range(n):
            eng, fn, deps, dma = ins[i]
            keep = []
            for d in deps:
                deng, _, _, ddma = ins[d]
                if (not dma) and (not ddma) and deng == eng and eng == 'pe':
                    continue
                keep.append(d)
            ins[i][2] = keep
            for d in keep:
                needed[d] = True
        self._stack = contextlib.ExitStack()
        sems = {e: self._stack.enter_context(nc.semaphore('s_' + e)) for e in ENGS}
        rings = {q: [self._stack.enter_context(nc.semaphore('r_%s%d' % (q, j))) for j in range(k)]
                 for q, k in self.ring_sizes.items()}
        ring_pos = {q: 0 for q in rings}
        ring_val = {q: [0] * len(rings[q]) for q in rings}
        ring_used = {q: [False] * len(rings[q]) for q in rings}
        tick = {e: 0 for e in ENGS}
        ev = [None] * n
        waited = {e: {} for e in engobj}
        nwaits = 0
        for i in range(n):
            eng, fn, deps, dma = ins[i]
            e = engobj[eng]
            need = {}
            if dma:
                q = eng
                pos = ring_pos[q]
                ring_pos[q] = (pos + 1) % len(rings[q])
                if ring_used[q][pos]:
                    need[('r', q, pos)] = ring_val[q][pos]
            for d in deps:
                k, v = ev[d]
                if need.get(k, -1) < v:
                    need[k] = v
            w = waited[eng]
            for k, v in need.items():
                if w.get(k, -1) >= v:
                    continue
                w[k] = v
                so = sems[k[1]] if k[0] == 'c' else rings[k[1]][k[2]]
                e.wait_ge(so, v)
                nwaits += 1
            inst = fn(e)
            if dma:
                ring_val[q][pos] += 16
                ring_used[q][pos] = True
                inst.then_inc(rings[q][pos], 16)
                ev[i] = (('r', q, pos), ring_val[q][pos])
            else:
                if needed[i]:
                    tick[eng] += 1
                    inst.then_inc(sems[eng], 1)
                    ev[i] = (('c', eng), tick[eng])
            ins[i][1] = None
        self.stats = dict(n=n, nwaits=nwaits, ticks=dict(tick))
        return self.stats


class Alloc:
    def __init__(self, nc):
        self.nc = nc
        self.off = SB_LO
        self.cnt = 0
        self.mark_ = SB_LO

    def __call__(self, shape, dt, name='t'):
        per = 1
        for s in shape[1:]:
            per *= s
        nbytes = per * (4 if dt == F32 else 2)
        nbytes = (nbytes + 63) // 64 * 64
        assert self.off + nbytes <= SB_HI, ('SBUF overflow', name, self.off, nbytes)
        self.cnt += 1
        t = self.nc.alloc_sbuf_tensor_at('%s_%d' % (name, self.cnt), list(shape), dt, offset=self.off)
        self.off += nbytes
        return t

    def mark(self):
        self.mark_ = self.off

    def reset(self):
        self.off = self.mark_


class Builder:
    def __init__(self, debug=False, stop_after=None, lite=()):
        self.debug = debug
        self.stop_after = stop_after
        self.lite = lite
        nc = self.nc = bass.Bass("TRN2", target_bir_lowering=False)
        self.P = Prog(nc)
        self.A = Alloc(nc)
        self.inputs = {}
        self.outputs = {}
        self.dbg_names = []

    def din(self, name, shape, dt=F32):
        if name in self.lite:
            shape = [DEPTH, 128, 512]
        t = self.nc.dram_tensor(name, list(shape), dt, kind="ExternalInput").ap()
        self.inputs[name] = t
        return t

    def dout(self, name, shape, dt=F32):
        t = self.nc.dram_tensor(name, list(shape), dt, kind="ExternalOutput").ap()
        self.outputs[name] = t
        return t

    def dscr(self, name, shape, dt=F32):
        if self.debug:
            t = self.nc.dram_tensor(name, list(shape), dt, kind="ExternalOutput").ap()
            self.dbg_names.append(name)
        else:
            t = self.nc.dram_tensor(name, list(shape), dt).ap()
        return t

    def mm(self, out, lhsT, rhs, start, stop, reads, wres, first=None):
        first = start if first is None else first
        fn = lambda e: e.matmul(out, lhsT=lhsT, rhs=rhs, start=start, stop=stop)
        if first:
            self.P.add('pe', fn, reads=reads, writes=[wres])
        else:
            self.P.add('pe', fn, reads=reads, swrites=[wres])

    def tr(self, out, in_, ident, reads, wres, excl):
        fn = lambda e: e.transpose(out=out, in_=in_, identity=ident)
        if excl:
            self.P.add('pe', fn, reads=reads, writes=[wres])
        else:
            self.P.add('pe', fn, reads=reads, swrites=[wres])

    def act(self, out, in_, func, reads, writes=(), swrites=(), **kw):
        self.P.add('act', lambda e: e.activation(out=out, in_=in_, func=func, **kw), reads, writes, swrites)

    def ts(self, eng, out, in0, s1, s2, op0, op1, reads, writes=(), swrites=(), **kw):
        if op1 is None:
            fn = lambda e: e.tensor_scalar(out=out, in0=in0, scalar1=s1, scalar2=None, op0=op0, **kw)
        else:
            fn = lambda e: e.tensor_scalar(out=out, in0=in0, scalar1=s1, scalar2=s2, op0=op0, op1=op1, **kw)
        self.P.add(eng, fn, reads, writes, swrites)

    def tt(self, eng, out, in0, in1, op, reads, writes=(), swrites=()):
        self.P.add(eng, lambda e: e.tensor_tensor(out=out, in0=in0, in1=in1, op=op), reads, writes, swrites)

    def stt(self, out, in0, scalar, in1, op0, op1, reads, writes=(), swrites=(), eng='dve'):
        self.P.add(eng, lambda e: e.scalar_tensor_tensor(out=out, in0=in0, scalar=scalar, in1=in1, op0=op0, op1=op1),
                   reads, writes, swrites)

    def cp(self, eng, out, in_, reads, writes=(), swrites=()):
        if eng == 'act':
            self.P.add('act', lambda e: e.copy(out=out, in_=in_), reads, writes, swrites)
        else:
            self.P.add(eng, lambda e: e.tensor_copy(out=out, in_=in_), reads, writes, swrites)

    def memset(self, eng, out, val, writes=(), swrites=()):
        self.P.add(eng, lambda e: e.memset(out, val), (), writes, swrites)


    NSTG = 4
    NSTG_A = 8

    def alloc_stg(self, n=4):
        self.NSTG = n
        self.stg = [self.A([128, 2, 512], F32, 'stg%d' % i) for i in range(self.NSTG)]
        self.r_stg = [self.P.res('stg%d' % i) for i in range(self.NSTG)]
        self._wst = 0

    def w_pieces(self, dst, r_dst, src, engs=('pool',)):
        P = self.P
        first = True
        for pc in range(16):
            slot = self._wst % self.NSTG
            self._wst += 1
            P.dma('sp', self.stg[slot][:], src[:, 2 * pc:2 * pc + 2, :], writes=[self.r_stg[slot]])
            self.cp(engs[pc % len(engs)], dst[:, 2 * pc:2 * pc + 2, :], self.stg[slot][:], [self.r_stg[slot]],
                    [r_dst] if first else (), () if first else [r_dst])
            first = False
            yield

    def w_pieces_d(self, dst, r_dst, srcs):
        P = self.P
        first = True
        dv = dst.rearrange("p k (q c) -> p k q c", q=4)
        for q in range(4):
            for k8 in range(4):
                slot = self._wst % self.NSTG
                self._wst += 1
                sv = self.stg[slot][:].rearrange("p a (b c) -> p (a b) c", c=128)
                P.dma('sp', sv, srcs[q][:, k8 * 8:(k8 + 1) * 8, :], writes=[self.r_stg[slot]])
                self.cp('pool', dv[:, k8 * 8:(k8 + 1) * 8, q, :], sv, [self.r_stg[slot]],
                        [r_dst] if first else (), () if first else [r_dst])
                first = False
                yield

    @staticmethod
    def drain(gen, n=None):
        if gen is None:
            return
        if n is None:
            for _ in gen:
                pass
        else:
            for _ in range(n):
                if next(gen, 'end') == 'end':
                    break

    def rstd_from(self, dst, src, res, mult=1.0):
        self.ts('dve', dst, src, mult, EPS, ALU.mult, ALU.add, [res], [res])
        self.act(dst, dst, AF.Ln, [res], [res])
        self.act(dst, dst, AF.Exp, [res], [res], scale=-0.5)

    def build(self):
        nc, P, A = self.nc, self.P, self.A
        d = self.d = {}
        d['x_s'] = self.din('x_s', [LS, D])
        d['x_p'] = self.din('x_p', [2 * LP, D])
        d['c_na_k'] = self.din('c_na_k', [DEPTH, PAST, 8 * HD])
        d['c_na_v'] = self.din('c_na_v', [DEPTH, PAST, 8 * HD])
        d['c_df_k'] = self.din('c_df_k', [DEPTH, PAST, 8 * HD])
        d['c_df_v'] = self.din('c_df_v', [DEPTH, PAST, 4 * 2 * HD])
        d['st_ret'] = self.din('st_ret', [DEPTH, 2, 8, HD, HD])
        d['cvec'] = self.din('cvec', [2, D])
        d['w_ada'] = self.din('w_ada', [DEPTH, D, 3 * D])
        d['b_ada'] = self.din('b_ada', [DEPTH, 3 * D])
        d['w_in'] = self.din('w_in', [DEPTH, D, DIN])
        d['w_out'] = self.din('w_out', [DEPTH, D, D])
        d['ln_g'] = self.din('ln_g', [DEPTH, D])
        d['ln_b'] = self.din('ln_b', [DEPTH, D])
        d['na_bias'] = self.din('na_bias', [DEPTH, 8 * 15, 31])
        d['diff_lam'] = self.din('diff_lam', [DEPTH, 4 * HD])
        d['diff_subln'] = self.din('diff_subln', [DEPTH, 2 * HD])
        d['ret_decay'] = self.din('ret_decay', [DEPTH, 16])
        d['conv_w'] = self.din('conv_w', [DEPTH, GW, 3])
        d['k_ident'] = self.din('k_ident', [128, 128])
        d['k_cos'] = self.din('k_cos', [LS, 128])
        d['k_sin'] = self.din('k_sin', [LS, 128])
        d['k_namask'] = self.din('k_namask', [5, 128, 640])
        d['k_rmask'] = self.din('k_rmask', [2, 128, 128])
        d['k_coef'] = self.din('k_coef', [128, 4])
        d['o_ys'] = self.dout('o_ys', [LS, D])
        d['o_yp'] = self.dout('o_yp', [2 * LP, D])
        d['o_nak'] = self.dout('o_nak', [2, DEPTH, LP, GW])
        d['o_nav'] = self.dout('o_nav', [2, DEPTH, LP, GW])
        d['o_dfk'] = self.dout('o_dfk', [2, DEPTH, LP, GW])
        d['o_dfv'] = self.dout('o_dfv', [2, DEPTH, LP, GW])
        d['o_st'] = self.dout('o_st', [2, DEPTH, 2, 8, HD, HD])
        d['MADA'] = self.dscr('MADA', [DEPTH, 2, 3 * D])
        d['XRES'] = self.dscr('XRES', [NTOK, D])
        for nm in ('QTA', 'KTA', 'QTB', 'KTB', 'QFT', 'QBT', 'KFT', 'KBT'):
            d[nm] = self.dscr(nm, [8, HD, NTOK], BF16)
        for nm in ('VA', 'VB', 'VC', 'KF', 'KB'):
            d[nm] = self.dscr(nm, [NTOK, GW], BF16)
        d['GT'] = self.dscr('GT', [NTOK, 3 * GW], BF16)
        d['UT'] = self.dscr('UT', [GW, NTOK])
        d['GDT'] = self.dscr('GDT', [GW, NTOK])
        d['Y'] = self.dscr('Y', [NTOK, 3 * GW], BF16)
        d['YTD'] = self.dscr('YTD', [GW, NTOK], BF16)
        d['Z'] = self.dscr('Z', [NTOK, D])
        d['PBREP'] = self.dscr('PBREP', [120, 64, 128])

        self._es = contextlib.ExitStack()
        PS = self._es.enter_context(nc.psum_tensor("psum_all", [128, 4096], F32))
        self.PS = PS

        def bank(b, lo=0, hi=512):
            return PS[:, b * 512 + lo: b * 512 + hi]

        def bank_bf(b, lo=0, hi=1024):
            return PS[:, b * 512 + lo // 2: b * 512 + hi // 2].bitcast(BF16)
        self.bank, self.bank_bf = bank, bank_bf
        self.r_bank = [P.res('bank%d' % b, bank=b) for b in range(8)]
        r_bank = self.r_bank

        self.r_const = r_const = P.res('const')
        self.ident = ident = A([128, 128], F32, 'ident')
        self.identb = identb = A([128, 128], BF16, 'identb')
        P.dma('sp', ident[:], d['k_ident'][:, :], swrites=[r_const])
        P.dma('pool', identb[:], d['k_ident'][:, :], swrites=[r_const])
        self.coef = coef = A([128, 4], F32, 'coef')
        P.dma('sp', coef[:], d['k_coef'][:, :], swrites=[r_const])
        A.mark()
        self.base_mark = A.mark_

        cT = A([128, 32, 2], F32, 'cT'); r_cT = P.res('cT')
        cTb = A([128, 32, 2], BF16, 'cTb'); r_cTb = P.res('cTb')
        wbuf = [A([128, 32, 512], BF16, 'wbuf%d' % i) for i in range(2)]
        r_wbuf = [P.res('wbuf%d' % i) for i in range(2)]
        mrow = [A([2, 512], F32, 'mrow%d' % i) for i in range(2)]
        brow = [A([2, 512], F32, 'brow%d' % i) for i in range(2)]
        r_mrow = [P.res() for i in range(2)]
        r_brow = [P.res() for i in range(2)]
        self.alloc_stg(8)
        for cv in range(2):
            P.dma('sp', cT[:, :, cv], d['cvec'][cv].rearrange("(kc p) -> p kc", p=128), swrites=[r_cT],
                  allow_slow_non_contiguous=True)
        self.act(cTb[:], cT[:], AF.Silu, [r_cT], [r_cTb])
        wcnt = 0
        r_mada = P.res('MADA')
        for l in range(DEPTH if 'w_ada' not in self.lite else 0):
            wl = d['w_ada'][l].rearrange("(kc p) n -> p kc n", p=128)
            for ch in range(24):
                s = wcnt % 2
                wcnt += 1
                self.drain(self.w_pieces(wbuf[s][:], r_wbuf[s], wl[:, :, ch * 512:(ch + 1) * 512],
                                          engs=('pool', 'act', 'dve', 'act', 'dve')))
                P.dma('sp', brow[s][:], d['b_ada'][l:l + 1, ch * 512:(ch + 1) * 512].broadcast_to([2, 512]),
                      writes=[r_brow[s]])
                pb = ch % 4
                for kc in range(32):
                    self.mm(bank(pb)[0:2, :], cTb[:, kc, :], wbuf[s][:, kc, :], kc == 0, kc == 31,
                            [r_cTb, r_wbuf[s]], r_bank[pb])
                self.tt('dve', mrow[s][:], bank(pb)[0:2, :], brow[s][:], ALU.add,
                        [r_bank[pb], r_brow[s]], [r_mrow[s]])
                P.dma('sp', d['MADA'][l, :, ch * 512:(ch + 1) * 512], mrow[s][:], reads=[r_mrow[s]],
                      swrites=[r_mada])
        P.barrier()
        A.reset()
        if self.stop_after == 'A':
            return self.finish()

        self.groups = [list(range(0, 8)) + [16, 17], list(range(8, 16)) + [18, 19]]

        for l in range(DEPTH):
            self.layer(l)
            if self.stopped:
                break
        return self.finish()

    stopped = False

    def finish(self):
        self.P.barrier()
        return self.P.emit()

    def x_src(self, l, tt, c0=0, c1=D):
        d = self.d
        if l == 0:
            if tt < 16:
                return d['x_s'][tt * 128:(tt + 1) * 128, c0:c1]
            return d['x_p'][(tt - 16) * 128:(tt - 15) * 128, c0:c1]
        return d['XRES'][tt * 128:(tt + 1) * 128, c0:c1]

    def layer(self, l):
        nc, P, A = self.nc, self.P, self.A
        d = self.d
        coef = self.coef
        A.mark_ = self.base_mark
        A.reset()
        r_tab = P.res('tab')
        mod = A([128, 2, 2, 32], F32, 'mod')
        for cv in range(2):
            for wh in range(2):
                P.dma('sp', mod[:, cv, wh, :],
                      d['MADA'][l, cv, wh * D:(wh + 1) * D].rearrange("(kc p) -> p kc", p=128),
                      swrites=[r_tab], allow_slow_non_contiguous=True)
        self.ts('dve', mod[:, :, 1, :], mod[:, :, 1, :], 1.0, None, ALU.add, None, [r_tab], [r_tab])
        cos2 = A([128, 16, 128], F32, 'cos2')
        sin2 = A([128, 16, 128], F32, 'sin2')
        P.dma('sp', cos2[:], d['k_cos'].rearrange("(t p) c -> p t c", p=128), swrites=[r_tab])
        P.dma('sp', sin2[:], d['k_sin'].rearrange("(t p) c -> p t c", p=128), swrites=[r_tab])
        ld = A([128, 16], F32, 'ld')
        P.dma('sp', ld[:], d['ret_decay'][l:l + 1, :].broadcast_to([128, 16]), writes=[r_tab])
        self.act(ld[:], ld[:], AF.Exp, [r_tab], [r_tab], scale=-1.0)
        self.ts('dve', ld[:], ld[:], 1.0, None, ALU.add, None, [r_tab], [r_tab])
        self.act(ld[:], ld[:], AF.Ln, [r_tab], [r_tab])
        self.ts('dve', ld[:], ld[:], -1.0, None, ALU.mult, None, [r_tab], [r_tab])
        dec = A([128, 4, 8], F32, 'dec')
        self.act(dec[:, 0, :], ld[:, 0:8], AF.Exp, [r_tab], [r_tab], scale=coef[:, 0:1])
        self.act(dec[:, 1, :], ld[:, 8:16], AF.Exp, [r_tab], [r_tab], scale=coef[:, 2:3])
        self.act(dec[:, 2, :], ld[:, 0:8], AF.Exp, [r_tab], [r_tab], scale=coef[:, 1:2])
        self.act(dec[:, 3, :], ld[:, 8:16], AF.Exp, [r_tab], [r_tab], scale=coef[:, 3:4])
        self.ts('dve', dec[:, 2:4, :], dec[:, 2:4, :], SCALE, None, ALU.mult, None, [r_tab], [r_tab])
        cdt = A([128, 16], F32, 'cdt')
        self.act(cdt[:], ld[:], AF.Exp, [r_tab], [r_tab], scale=128.0)
        A.mark()
        self.layer_mark = A.mark_
        self.tabs = dict(mod=mod, cos2=cos2, sin2=sin2, dec=dec, cdt=cdt, r_tab=r_tab)

        if self.stop_after == 'TAB%d' % l:
            self.stopped = True
            return
        self.phase_BC(l)
        if self.stopped:
            return
        A.mark_ = self.layer_mark
        P.barrier()
        if self.stop_after == 'BC%d' % l:
            self.stopped = True
            return
        self.phase_mixers(l)
        P.barrier()
        if self.stop_after == 'MIX%d' % l:
            self.stopped = True
            return
        self.phase_EF(l)
        P.barrier()
        if self.stop_after == 'L%d' % l:
            self.stopped = True

    def phase_BC(self, l):
        nc, P, A = self.nc, self.P, self.A
        d = self.d
        bank, bank_bf, r_bank = self.bank, self.bank_bf, self.r_bank
        ident, identb, r_const = self.ident, self.identb, self.r_const
        T = self.tabs
        r_tab = T['r_tab']
        mod, cos2, sin2, dec = T['mod'], T['cos2'], T['sin2'], T['dec']
        A.reset()
        hT = A([128, 32, 1280], BF16, 'hT')
        r_hT = [P.res('hT%d' % s) for s in range(10)]
        wbuf = [A([128, 32, 512], BF16, 'wb%d' % i) for i in range(2)]
        r_wbuf = [P.res('wb%d' % i) for i in range(2)]
        A.mark()
        w_l = d['w_in'][l]
        w_tok = w_l.rearrange("(kc p) n -> p kc n", p=128)
        r_scr = self.r_scr = getattr(self, 'r_scr', None) or {k: P.res(k) for k in
                                                             ('Q', 'V', 'G', 'UD', 'OUT', 'Y', 'Z', 'X')}

        for gi, grp in enumerate(self.groups):
            A.reset()
            xt = [A([128, D], F32, 'xt%d' % i) for i in range(2)]
            r_xt = [P.res() for i in range(2)]
            st = [A([128, 8, 6], F32, 'st%d' % i) for i in range(2)]
            mv = [A([128, 4], F32, 'mv%d' % i) for i in range(2)]
            r_mv = [P.res() for i in range(2)]
            tb_i = 0
            ev_i = 0
            for s, tt in enumerate(grp):
                b = s % 2
                cv = 0 if tt < 16 else 1
                P.dma('sp', xt[b][:], self.x_src(l, tt), writes=[r_xt[b]])
                for c8 in range(8):
                    P.add('dve', lambda e, o=st[b][:, c8, :], i=xt[b][:, c8 * 512:(c8 + 1) * 512]: e.bn_stats(out=o, in_=i),
                          reads=[r_xt[b]], writes=[r_mv[b]] if c8 == 0 else (), swrites=() if c8 == 0 else [r_mv[b]])
                P.add('dve', lambda e, o=mv[b][:, 0:2], i=st[b][:].rearrange("p a b -> p (a b)"): e.bn_aggr(out=o, in_=i),
                      reads=[r_mv[b]], writes=[r_mv[b]])
                self.rstd_from(mv[b][:, 2:3], mv[b][:, 1:2], r_mv[b])
                self.ts('dve', xt[b][:], xt[b][:], mv[b][:, 0:1], mv[b][:, 2:3], ALU.subtract, ALU.mult,
                        [r_mv[b], r_xt[b]], [r_xt[b]])
                for kq in range(8):
                    pb = (tb_i % 4)
                    tb_i += 1
                    for j in range(4):
                        kc = kq * 4 + j
                        self.tr(bank(pb)[:, j * 128:(j + 1) * 128], xt[b][:, kc * 128:(kc + 1) * 128], ident[:],
                                [r_xt[b], r_const], r_bank[pb], j == 0)
                    for j in range(4):
                        kc = kq * 4 + j
                        o = hT[:, kc, s * 128:(s + 1) * 128]
                        i_ = bank(pb)[:, j * 128:(j + 1) * 128]
                        if kq % 2 == 0:
                            self.act(o, i_, AF.Identity, [r_bank[pb], r_tab], (), [r_hT[s]],
                                     scale=mod[:, cv, 1, kc:kc + 1], bias=mod[:, cv, 0, kc:kc + 1])
                        else:
                            self.ts('dve', o, i_, mod[:, cv, 1, kc:kc + 1], mod[:, cv, 0, kc:kc + 1],
                                    ALU.mult, ALU.add, [r_bank[pb], r_tab], (), [r_hT[s]])
                        ev_i += 1
            P.barrier()
            if self.stop_after == 'B%d' % l:
                self.stopped = True
                return
            A.reset()
            sb16 = [A([128, 512], BF16, 'sb16_%d' % i) for i in range(2)]
            sb16b = [A([128, 512], BF16, 'sb16b_%d' % i) for i in range(2)]
            sf32 = [A([128, 512], F32, 'sf32_%d' % i) for i in range(2)]
            of32 = [A([128, 512], F32, 'of32_0')] * 2
            rt1 = [A([128, 512], F32, 'rt1_%d' % i) for i in range(2)]
            rt2 = [A([128, 512], F32, 'rt2_%d' % i) for i in range(2)]
            trs = [A([128, 1024], BF16, 'trs%d' % i) for i in range(2)]
            r_sb16 = [P.res() for i in range(2)]
            r_sb16b = [P.res() for i in range(2)]
            r_sf32 = [P.res() for i in range(2)]
            r_of32 = [P.res()] * 2
            r_rt = [P.res() for i in range(2)]
            r_trs = [P.res() for i in range(2)]
            dstg0 = A([128, 2, 512], F32, 'dstg')
            dtmp0 = A([128, 2, 512], F32, 'dtmp')
            dstg, dtmp = [dstg0, dstg0], [dtmp0, dtmp0]
            r_dstg0, r_dtmp0 = P.res(), P.res()
            r_dstg, r_dtmp = [r_dstg0, r_dstg0], [r_dtmp0, r_dtmp0]
            self.alloc_stg()

            nunits = 24 + 8
            wslot = [0]

            def load_w(ui, ws):
                if ui < 24:
                    return self.w_pieces(wbuf[ws][:], r_wbuf[ws], w_tok[:, :, ui * 512:(ui + 1) * 512])
                j = ui - 24
                srcs = [w_tok[:, :, (12 + q) * GW + j * 128:(12 + q) * GW + (j + 1) * 128] for q in range(4)]
                return self.w_pieces_d(wbuf[ws][:], r_wbuf[ws], srcs)

            self.drain(load_w(0, 0))
            pending = []
            acc_i = 0
            u_i = 0
            for ui in range(nunits):
                ws = ui % 2
                wgen = load_w(ui + 1, (ui + 1) % 2) if ui + 1 < nunits else None
                if ui < 24:
                    cc = ui
                    part, half = cc // 2, cc % 2
                    for s, tt in enumerate(grp):
                        pb = acc_i % 4
                        acc_i += 1
                        for kc in range(32):
                            self.mm(bank(pb), hT[:, kc, s * 128:(s + 1) * 128], wbuf[ws][:, kc, :], kc == 0, kc == 31,
                                    [r_hT[s], r_wbuf[ws]], r_bank[pb])
                        self.drain(wgen, 2)
                        for f in pending:
                            f()
                        pending = []
                        k = u_i % 2
                        u_i += 1
                        is_s = tt < 16
                        tok0 = tt * 128
                        ps = bank(pb)
                        cols = slice(half * 512, (half + 1) * 512)
                        if not is_s:
                            seq = (tt - 16) // 2
                            ptok = ((tt - 16) % 2) * 128

                        def rope(dst, k=k, ps=ps, pb=pb, tt=tt):
                            c2 = cos2[:, tt, :].unsqueeze(1).broadcast_to([128, 4, 128])
                            psv = ps.rearrange("p (b c) -> p b c", c=128)
                            self.tt('dve', rt1[k][:].rearrange("p (b c) -> p b c", c=128), psv, c2, ALU.mult,
                                    [r_bank[pb], r_tab], [r_rt[k]])
                            ps5 = ps.rearrange("p (b a x f) -> p b a x f", a=2, x=2, f=32)
                            r25 = rt2[k][:].rearrange("p (b a x f) -> p b a x f", a=2, x=2, f=32)
                            s25 = sin2[:, tt, :].rearrange("p (a x f) -> p a x f", a=2, x=2)
                            for x in range(2):
                                for a in range(2):
                                    self.tt('dve', r25[:, :, a, x, :], ps5[:, :, a, 1 - x, :],
                                            s25[:, a, x, :].unsqueeze(1).broadcast_to([128, 4, 32]), ALU.mult,
                                            [r_bank[pb], r_tab], (), [r_rt[k]])
                            return rt1[k], rt2[k]

                        def transposes(srcs, dsts, k=k, tok0=tok0):
                            def f():
                                tb = 4 + (self._tb % 4)
                                self._tb += 1
                                n = 0
                                for (src, rs) in srcs:
                                    for j in range(4):
                                        self.tr(bank_bf(tb)[:, n * 128:(n + 1) * 128], src[:, j * 128:(j + 1) * 128],
                                                identb[:], [rs, r_const], r_bank[tb], n == 0)
                                        n += 1
                                self.cp('dve', trs[k][:, 0:n * 128], bank_bf(tb)[:, 0:n * 128], [r_bank[tb]], [r_trs[k]])
                                for i, dst in enumerate(dsts):
                                    P.dma('sp', dst.rearrange("h d t -> d h t"),
                                          trs[k][:, i * 512:(i + 1) * 512].rearrange("p (h t) -> p h t", h=4),
                                          reads=[r_trs[k]], swrites=[r_scr['Q']])
                            return f

                        if part in (0, 1):
                            self.cp('act', sb16[k][:], ps, [r_bank[pb]], [r_sb16[k]])
                            if part == 1 and not is_s:
                                self.cp('dve', of32[k][:], ps, [r_bank[pb]], [r_of32[k]])
                                P.dma('sp', d['o_nak'][seq, l, ptok:ptok + 128, cols], of32[k][:],
                                      reads=[r_of32[k]], swrites=[r_scr['OUT']])
                            dst = d['QTA' if part == 0 else 'KTA'][half * 4:half * 4 + 4, :, tok0:tok0 + 128]
                            pending.append(transposes([(sb16[k], r_sb16[k])], [dst]))
                        elif part in (2, 6, 10):
                            self.cp('act', sb16[k][:], ps, [r_bank[pb]], [r_sb16[k]])
                            nm = {2: 'VA', 6: 'VB', 10: 'VC'}[part]
                            P.dma('sp', d[nm][tok0:tok0 + 128, cols], sb16[k][:], reads=[r_sb16[k]],
                                  swrites=[r_scr['V']])
                            if not is_s and part in (2, 6):
                                self.cp('dve', of32[k][:], ps, [r_bank[pb]], [r_of32[k]])
                                P.dma('sp', d['o_nav' if part == 2 else 'o_dfv'][seq, l, ptok:ptok + 128, cols],
                                      of32[k][:], reads=[r_of32[k]], swrites=[r_scr['OUT']])
                        elif part in (3, 7, 11):
                            self.act(sb16[k][:], ps, AF.Silu, [r_bank[pb]], [r_sb16[k]])
                            gi_ = {3: 0, 7: 1, 11: 2}[part]
                            P.dma('sp', d['GT'][tok0:tok0 + 128, gi_ * GW + half * 512: gi_ * GW + (half + 1) * 512],
                                  sb16[k][:], reads=[r_sb16[k]], swrites=[r_scr['G']])
                        elif part in (4, 5):
                            if is_s:
                                a1, a2 = rope(None)
                                self.tt('dve', sb16[k][:], a1[:], a2[:], ALU.add, [r_rt[k]], [r_sb16[k]])
                            else:
                                self.cp('act', sb16[k][:], ps, [r_bank[pb]], [r_sb16[k]])
                                if part == 5:
                                    self.cp('dve', of32[k][:], ps, [r_bank[pb]], [r_of32[k]])
                                    P.dma('sp', d['o_dfk'][seq, l, ptok:ptok + 128, cols], of32[k][:],
                                          reads=[r_of32[k]], swrites=[r_scr['OUT']])
                            dst = d['QTB' if part == 4 else 'KTB'][half * 4:half * 4 + 4, :, tok0:tok0 + 128]
                            pending.append(transposes([(sb16[k], r_sb16[k])], [dst]))
                        elif part in (8, 9):
                            if is_s:
                                a1, a2 = rope(None)
                                self.tt('dve', sf32[k][:], a1[:], a2[:], ALU.add, [r_rt[k]], [r_sf32[k]])
                            else:
                                self.cp('act', sf32[k][:], ps, [r_bank[pb]], [r_sf32[k]])
                            base = 0 if part == 8 else 2
                            sfv = sf32[k][:].rearrange("p (h c) -> p h c", c=128)
                            for di, (dstt, rdst) in enumerate(((sb16[k], r_sb16[k]), (sb16b[k], r_sb16b[k]))):
                                dc = dec[:, base + di, half * 4:half * 4 + 4].unsqueeze(2).broadcast_to([128, 4, 128])
                                self.tt('dve', dstt[:].rearrange("p (h c) -> p h c", c=128), sfv, dc, ALU.mult,
                                        [r_sf32[k], r_tab], [rdst])
                            if part == 9:
                                P.dma('sp', d['KF'][tok0:tok0 + 128, cols], sb16[k][:], reads=[r_sb16[k]],
                                      swrites=[r_scr['V']])
                                P.dma('sp', d['KB'][tok0:tok0 + 128, cols], sb16b[k][:], reads=[r_sb16b[k]],
                                      swrites=[r_scr['V']])
                            n1, n2 = ('QFT', 'QBT') if part == 8 else ('KFT', 'KBT')
                            dst1 = d[n1][half * 4:half * 4 + 4, :, tok0:tok0 + 128]
                            dst2 = d[n2][half * 4:half * 4 + 4, :, tok0:tok0 + 128]
                            pending.append(transposes([(sb16[k], r_sb16[k]), (sb16b[k], r_sb16b[k])], [dst1, dst2]))
                    self.drain(wgen)
                else:
                    j = ui - 24
                    for f in pending:
                        f()
                    pending = []
                    wv = wbuf[ws][:].rearrange("p k (q c) -> p k q c", q=4)
                    chunks = [(0, 512), (512, 1024), (1024, 1280)]
                    for ci, (t0, t1) in enumerate(chunks):
                        n = t1 - t0
                        par = (j * 3 + ci) % 2
                        for q in range(4):
                            pb = par * 4 + q
                            for kc in range(32):
                                self.mm(bank(pb)[:, 0:n], wv[:, kc, q, :], hT[:, kc, t0:t1], kc == 0, kc == 31,
                                        [r_wbuf[ws]] + r_hT[t0 // 128:t1 // 128], r_bank[pb])
                        self.drain(wgen, 6)
                        k = par
                        b0 = par * 4
                        tt0 = grp[t0 // 128]
                        g0 = tt0 * 128
                        self.act(dtmp[k][:, 0, 0:n], bank(b0 + 3)[:, 0:n], AF.Silu, [r_bank[b0 + 3]], [r_dtmp[k]])
                        self.cp('act', dtmp[k][:, 1, 0:n], bank(b0 + 2)[:, 0:n], [r_bank[b0 + 2]], (), [r_dtmp[k]])
                        self.tt('dve', dstg[k][:, 0, 0:n], bank(b0 + 1)[:, 0:n], dtmp[k][:, 0, 0:n], ALU.mult,
                                [r_bank[b0 + 1], r_dtmp[k]], [r_dstg[k]])
                        self.tt('dve', dstg[k][:, 1, 0:n], bank(b0)[:, 0:n], dtmp[k][:, 1, 0:n], ALU.mult,
                                [r_bank[b0], r_dtmp[k]], (), [r_dstg[k]])
                        P.dma('sp', d['GDT'][j * 128:(j + 1) * 128, g0:g0 + n], dstg[k][:, 0, 0:n],
                              reads=[r_dstg[k]], swrites=[r_scr['UD']])
                        P.dma('sp', d['UT'][j * 128:(j + 1) * 128, g0:g0 + n], dstg[k][:, 1, 0:n],
                              reads=[r_dstg[k]], swrites=[r_scr['UD']])
                    self.drain(wgen)
            for f in pending:
                f()
            pending = []
            P.barrier()

    _tb = 0
    def phase_mixers(self, l):
        self.mixer_A(l)
        self.P.barrier()
        self.mixer_B(l)
        self.P.barrier()
        self.mixer_C(l)
        self.P.barrier()
        self.mixer_D(l)

    def mixer_A(self, l):
        nc, P, A = self.nc, self.P, self.A
        d = self.d
        bank, bank_bf, r_bank = self.bank, self.bank_bf, self.r_bank
        identb, r_const = self.identb, self.r_const
        A.reset()
        r_y = P.res('Yw')
        pr = A([120, 128], F32, 'pr'); r_pr = P.res('pr')
        self.memset('dve', pr[:], 0.0, writes=[r_pr])
        P.dma('sp', pr[:, 48:79], d['na_bias'][l], writes=[r_pr])
        r_pb = P.res('pbrep')
        P.dma('sp', d['PBREP'][:, :, :], pr[:].unsqueeze(1).broadcast_to([120, 64, 128]), reads=[r_pr], writes=[r_pb])
        masks = A([128, 5, 640], F32, 'masks'); r_masks = P.res('masks')
        P.dma('sp', masks[:], d['k_namask'].rearrange("t p k -> p t k"), writes=[r_masks])
        NB = 2
        qT = [A([128, LS], BF16, 'qT%d' % i) for i in range(NB)]
        kT = [A([128, LS], BF16, 'kT%d' % i) for i in range(NB)]
        V = [A([128, 16, 128], BF16, 'V%d' % i) for i in range(NB)]
        ckl = [A([128, 4, 128], BF16, 'ckl%d' % i) for i in range(NB)]
        ckT = [A([128, 512], BF16, 'ckT%d' % i) for i in range(NB)]
        cV = [A([128, 4, 128], BF16, 'cV%d' % i) for i in range(NB)]
        gate = [A([128, 16, 128], BF16, 'gate%d' % i) for i in range(NB)]
        TB2 = [A([128, 15, 64], F32, 'TB2_%d' % i) for i in range(NB)]
        BT = [A([128, 5, 640], F32, 'BT%d' % i) for i in range(NB)]
        yst = [A([128, 16, 128], BF16, 'yst%d' % i) for i in range(NB)]
        r_h = [P.res('hA%d' % i) for i in range(NB)]
        r_ckl = [P.res() for i in range(NB)]
        r_ckT = [P.res() for i in range(NB)]
        r_TB2 = [P.res() for i in range(NB)]
        r_BT = [P.res() for i in range(NB)]
        r_yst = [P.res() for i in range(NB)]
        sc = [A([128, 1152], F32, 'sc%d' % i) for i in range(2)]
        pbf = [A([128, 1152], BF16, 'pbf%d' % i) for i in range(2)]
        PT = [A([128, 9, 128], BF16, 'PT%d' % i) for i in range(2)]
        stt_ = [A([128, 4], F32, 'st%d' % i) for i in range(2)]
        r_sc = [P.res() for i in range(2)]
        r_pbf = [P.res() for i in range(2)]
        r_PT = [P.res() for i in range(2)]
        r_st = [P.res() for i in range(2)]
        r_tail = [P.res('tail', bank=4 + i) for i in range(2)]
        r_t9 = [P.res('t9', bank=4 + i) for i in range(2)]
        r_o = [P.res('o', bank=4 + i) for i in range(2)]
        self._u = 0

        def unit(qTb, rq, loc, nloc, bias, rbias, ctx, rctx, vlist, out_dst, r_out, gate_ap, rgate, first_out):
            u = self._u % 2
            self._u += 1
            bA, bC, bM, bT = 0 + u, 2 + u, 4 + u, 6 + u
            n1 = min(512, nloc)
            tot = nloc + (512 if ctx is not None else 0)
            nblk = tot // 128
            self.mm(bank(bA)[:, 0:n1], qTb, loc[:, 0:n1], True, True, [rq], r_bank[bA])
            if nloc > 512:
                self.mm(bank(bM)[:, 0:nloc - 512], qTb, loc[:, 512:nloc], True, True, [rq], r_tail[u])
            if ctx is not None:
                self.mm(bank(bC)[:, 0:512], qTb, ctx, True, True, [rq, rctx], r_bank[bC])
            if bias is not None:
                self.stt(sc[u][:, 0:n1], bank(bA)[:, 0:n1], SCALE, bias[:, 0:n1], ALU.mult, ALU.add,
                         [r_bank[bA], rbias], [r_sc[u]])
                self.stt(sc[u][:, 512:nloc], bank(bM)[:, 0:nloc - 512], SCALE, bias[:, 512:nloc], ALU.mult, ALU.add,
                         [r_tail[u], rbias], (), [r_sc[u]])
            else:
                P.add('act', lambda e, o=sc[u][:, 0:n1], i=bank(bA)[:, 0:n1]: e.mul(o, i, SCALE),
                      [r_bank[bA]], [r_sc[u]])
            if ctx is not None:
                P.add('act', lambda e, o=sc[u][:, nloc:tot], i=bank(bC)[:, 0:512]: e.mul(o, i, SCALE),
                      [r_bank[bC]], (), [r_sc[u]])
            self.memset('dve', stt_[u][:], 0.0, writes=[r_st[u]])
            P.add('dve', lambda e, o=stt_[u][:, 0:1], i=sc[u][:, 0:tot]: e.reduce_max(out=o, in_=i, axis=AX.X),
                  [r_sc[u]], (), [r_st[u]])
            self.ts('dve', stt_[u][:, 1:2], stt_[u][:, 0:1], -1.0, None, ALU.mult, None, [r_st[u]], (), [r_st[u]])
            self.act(pbf[u][:, 0:tot], sc[u][:, 0:tot], AF.Exp, [r_sc[u], r_st[u]], [r_pbf[u]],
                     bias=stt_[u][:, 1:2], scale=1.0, accum_out=stt_[u][:, 2:3])
            P.add('dve', lambda e, o=stt_[u][:, 3:4], i=stt_[u][:, 2:3]: e.reciprocal(out=o, in_=i),
                  [r_pbf[u], r_st[u]], (), [r_st[u]])
            nb8 = min(8, nblk)
            for b in range(nb8):
                self.tr(bank_bf(bT)[:, b * 128:(b + 1) * 128], pbf[u][:, b * 128:(b + 1) * 128], identb[:],
                        [r_pbf[u], r_const], r_bank[bT], b == 0)
            if nblk == 9:
                self.tr(bank_bf(bM, 256, 384), pbf[u][:, 1024:1152], identb[:], [r_pbf[u], r_const], r_t9[u], True)
            self.cp('act', PT[u][:, 0:nb8, :].rearrange("p b q -> p (b q)"), bank_bf(bT)[:, 0:nb8 * 128],
                    [r_bank[bT]], [r_PT[u]])
            if nblk == 9:
                self.cp('dve', PT[u][:, 8, :], bank_bf(bM, 256, 384), [r_t9[u]], (), [r_PT[u]])
            for b in range(nblk):
                self.mm(bank(bM)[:, 256:384], PT[u][:, b, :], vlist[b][0], b == 0, b == nblk - 1,
                        [r_PT[u], vlist[b][1]], r_o[u])
            self.stt(out_dst, bank(bM)[:, 256:384], stt_[u][:, 3:4], gate_ap, ALU.mult, ALU.mult,
                     [r_o[u], r_st[u], rgate], [r_out] if first_out else (), () if first_out else [r_out])

        cl = l
        for h in range(8):
            n = h % NB
            hs = slice(h * 128, (h + 1) * 128)
            P.dma('sp', qT[n][:], d['QTA'][h, :, 0:LS], writes=[r_h[n]])
            P.dma('sp', kT[n][:], d['KTA'][h, :, 0:LS], swrites=[r_h[n]])
            P.dma('sp', V[n][:], d['VA'][0:LS, hs].rearrange("(t p) e -> p t e", p=128), swrites=[r_h[n]])
            P.dma('sp', gate[n][:], d['GT'][0:LS, hs].rearrange("(t p) e -> p t e", p=128), swrites=[r_h[n]])
            P.dma('pool', cV[n][:], d['c_na_v'][cl, :, hs].rearrange("(t p) e -> p t e", p=128), swrites=[r_h[n]])
            P.dma('pool', ckl[n][:], d['c_na_k'][cl, :, hs].rearrange("(t p) e -> p t e", p=128), writes=[r_ckl[n]])
            bT = 6 + (h % 2)
            for t in range(4):
                self.tr(bank_bf(bT)[:, t * 128:(t + 1) * 128], ckl[n][:, t, :], identb[:], [r_ckl[n], r_const],
                        r_bank[bT], t == 0)
            self.cp('dve', ckT[n][:], bank_bf(bT)[:, 0:512], [r_bank[bT]], [r_ckT[n]])
            for half in range(2):
                src = bass.AP(tensor=d['PBREP'].tensor, offset=(h * 15) * 8192 + 63,
                              ap=[[127, 64], [8192, 15], [1, 64]])
                if half == 0:
                    P.dma('sp', TB2[n][0:64, :, :], src, reads=[r_pb], writes=[r_TB2[n]])
                else:
                    P.dma('sp', TB2[n][64:128, :, :], src, reads=[r_pb], swrites=[r_TB2[n]])
            self.cp('act', BT[n][:], masks[:], [r_masks], [r_BT[n]])
            for ty, qrel0 in enumerate((0, 2, 4, 6, 8)):
                for half in range(2):
                    qr = qrel0 + half
                    k0, k1 = max(0, qr - 7), min(9, qr + 7)
                    d0 = k0 - qr + 7
                    nk = k1 - k0 + 1
                    ps_ = slice(half * 64, (half + 1) * 64)
                    o = BT[n][ps_, ty, k0 * 64:(k1 + 1) * 64]
                    self.tt('dve', o, o, TB2[n][ps_, d0:d0 + nk, :].rearrange("p a b -> p (a b)"), ALU.add,
                            [r_TB2[n], r_BT[n]], (), [r_BT[n]])
            for qb in range(16):
                tw = min(max(qb - 2, 0), 11)
                ty = {0: 0, 1: 1, 14: 3, 15: 4}.get(qb, 2)
                vlist = [(V[n][:, tw + b, :], r_h[n]) for b in range(5)] + [(cV[n][:, b, :], r_h[n]) for b in range(4)]
                unit(qT[n][:, qb * 128:(qb + 1) * 128], r_h[n], kT[n][:, tw * 128:tw * 128 + 640], 640,
                     BT[n][:, ty, :], r_BT[n], ckT[n][:], r_ckT[n], vlist, yst[n][:, qb, :], r_yst[n],
                     gate[n][:, qb, :], r_h[n], qb == 0)
            P.dma('sp', d['Y'][0:LS, hs].rearrange("(t p) e -> p t e", p=128), yst[n][:], reads=[r_yst[n]],
                  swrites=[r_y])
        for sq in range(2):
            t0 = LS + sq * LP
            for h in range(8):
                n = h % NB
                hs = slice(h * 128, (h + 1) * 128)
                P.dma('sp', qT[n][:, 0:LP], d['QTA'][h, :, t0:t0 + LP], writes=[r_h[n]])
                P.dma('sp', kT[n][:, 0:LP], d['KTA'][h, :, t0:t0 + LP], swrites=[r_h[n]])
                P.dma('sp', V[n][:, 0:2, :], d['VA'][t0:t0 + LP, hs].rearrange("(t p) e -> p t e", p=128),
                      swrites=[r_h[n]])
                P.dma('sp', gate[n][:, 0:2, :], d['GT'][t0:t0 + LP, hs].rearrange("(t p) e -> p t e", p=128),
                      swrites=[r_h[n]])
                for qb in range(2):
                    vlist = [(V[n][:, b, :], r_h[n]) for b in range(2)]
                    unit(qT[n][:, qb * 128:(qb + 1) * 128], r_h[n], kT[n][:, 0:LP], LP, None, None, None, None,
                         vlist, yst[n][:, qb, :], r_yst[n], gate[n][:, qb, :], r_h[n], qb == 0)
                P.dma('sp', d['Y'][t0:t0 + LP, hs].rearrange("(t p) e -> p t e", p=128), yst[n][:, 0:2, :],
                      reads=[r_yst[n]], swrites=[r_y])

    def mixer_B(self, l):
        nc, P, A = self.nc, self.P, self.A
        d = self.d
        bank, bank_bf, r_bank = self.bank, self.bank_bf, self.r_bank
        identb, r_const = self.identb, self.r_const
        lam_init = 0.8 - 0.6 * math.exp(-0.3 * l)
        A.reset()
        r_y = P.res('Yw')
        lt = A([128, 4, 128], F32, 'lt'); r_lt = P.res('lt')
        P.dma('sp', lt[:].rearrange("p a b -> p (a b)"), d['diff_lam'][l:l + 1, :].broadcast_to([128, 512]),
              writes=[r_lt])
        lm = A([128, 2, 128], F32, 'lm')
        self.tt('dve', lm[:, 0, :], lt[:, 0, :], lt[:, 1, :], ALU.mult, [r_lt], [r_lt])
        self.tt('dve', lm[:, 1, :], lt[:, 2, :], lt[:, 3, :], ALU.mult, [r_lt], [r_lt])
        lam = A([128, 4], F32, 'lam')
        P.add('dve', lambda e: e.reduce_sum(out=lam[:, 0:2], in_=lm[:], axis=AX.X), [r_lt], [r_lt])
        self.act(lam[:, 0:2], lam[:, 0:2], AF.Exp, [r_lt], [r_lt])
        self.tt('dve', lam[:, 2:3], lam[:, 0:1], lam[:, 1:2], ALU.subtract, [r_lt], [r_lt])
        self.ts('dve', lam[:, 2:3], lam[:, 2:3], lam_init, None, ALU.add, None, [r_lt], [r_lt])
        wsub = A([128, 256], F32, 'wsub')
        P.dma('sp', wsub[:], d['diff_subln'][l:l + 1, :].broadcast_to([128, 256]), writes=[r_lt])
        self.ts('dve', wsub[:], wsub[:], 1.0 - lam_init, None, ALU.mult, None, [r_lt], [r_lt])

        NKB = (PAST + LS) // 128
        qT = [A([128, LS], BF16, 'bqT%d' % t) for t in range(2)]
        kT = [A([128, PAST + LS], BF16, 'bkT%d' % t) for t in range(2)]
        V = A([128, NKB, 256], BF16, 'bV')
        ckl = A([128, 4, 256], BF16, 'bckl')
        gate = A([128, 16, 256], BF16, 'bgate')
        yst = A([128, 16, 256], BF16, 'byst')
        r_h = P.res('hB'); r_ckl = P.res(); r_yst = P.res()
        exs = [[A([128, PAST + LS], F32, 'ex%d_%d' % (t, i)) for t in range(2)] for i in range(2)]
        r_exs = [[P.res() for t in range(2)] for i in range(2)]
        abfs = [A([128, PAST + LS], BF16, 'abf%d' % i) for i in range(2)]; r_abfs = [P.res() for i in range(2)]
        aTs = [A([128, NKB, 128], BF16, 'aT%d' % i) for i in range(2)]; r_aTs = [P.res() for i in range(2)]
        stt_ = [A([128, 32], F32, 'bst%d' % i) for i in range(2)]
        r_st = [P.res() for i in range(2)]
        otmps = [A([128, 256], F32, 'otmp%d' % i) for i in range(2)]; r_otmps = [P.res() for i in range(2)]
        junks = [A([128, 256], F32, 'junk%d' % i) for i in range(2)]
        r_o = P.res('bo')
        self._sb = 0
        self._ub = 0
        self._tbb = 0

        def unit(Lk, qcols, yout, first_out, gate_ap):
            u = self._ub % 2
            self._ub += 1
            st = stt_[u]
            ex, r_ex, abf, r_abf, aT, r_aT = exs[u], r_exs[u], abfs[u], r_abfs[u], aTs[u], r_aTs[u]
            otmp, r_otmp, junk = otmps[u], r_otmps[u], junks[u]
            nkb = Lk // 128
            nch = (Lk + 511) // 512
            self.memset('dve', st[:], 0.0, writes=[r_st[u]])
            chunks = []
            for t in range(2):
                for c in range(nch):
                    w = min(512, Lk - c * 512)
                    b = self._sb % 6
                    self._sb += 1
                    self.mm(bank(b)[:, 0:w], qT[t][:, qcols], kT[t][:, c * 512:c * 512 + w], True, True,
                            [r_h], r_bank[b])
                    P.add('dve', lambda e, o=st[:, t * 5 + c:t * 5 + c + 1], i=bank(b)[:, 0:w]:
                          e.reduce_max(out=o, in_=i, axis=AX.X), [r_bank[b]], (), [r_st[u]])
                    chunks.append((t, c, w, b))
                P.add('dve', lambda e, o=st[:, 10 + t:11 + t], i=st[:, t * 5:t * 5 + nch]:
                      e.reduce_max(out=o, in_=i, axis=AX.X), [r_st[u]], (), [r_st[u]])
                self.ts('dve', st[:, 12 + t:13 + t], st[:, 10 + t:11 + t], -SCALE, None, ALU.mult, None,
                        [r_st[u]], (), [r_st[u]])
                for (t_, c, w, b) in chunks[-nch:]:
                    self.act(ex[t][:, c * 512:c * 512 + w], bank(b)[:, 0:w], AF.Exp, [r_bank[b], r_st[u]],
                             [r_ex[t]] if c == 0 else (), () if c == 0 else [r_ex[t]],
                             scale=SCALE, bias=st[:, 12 + t:13 + t], accum_out=st[:, 14 + t * 5 + c:15 + t * 5 + c])
                P.add('dve', lambda e, o=st[:, 24 + t:25 + t], i=st[:, 14 + t * 5:14 + t * 5 + nch]:
                      e.reduce_sum(out=o, in_=i, axis=AX.X), [r_st[u], r_ex[t]], (), [r_st[u]])
            P.add('dve', lambda e, o=st[:, 26:28], i=st[:, 24:26]: e.reciprocal(out=o, in_=i), [r_st[u]], (), [r_st[u]])
            self.tt('dve', st[:, 28:29], st[:, 27:28], lam[:, 2:3], ALU.mult, [r_st[u], r_lt], (), [r_st[u]])
            self.act(ex[1][:, 0:Lk], ex[1][:, 0:Lk], AF.Identity, [r_ex[1], r_st[u]], [r_ex[1]], scale=st[:, 28:29])
            self.stt(abf[:, 0:Lk], ex[0][:, 0:Lk], st[:, 26:27], ex[1][:, 0:Lk], ALU.mult, ALU.subtract,
                     [r_ex[0], r_ex[1], r_st[u]], [r_abf])
            kb = 0
            first = True
            while kb < nkb:
                nb = min(8, nkb - kb)
                bT = 6 + (self._tbb % 2)
                self._tbb += 1
                for j in range(nb):
                    self.tr(bank_bf(bT)[:, j * 128:(j + 1) * 128], abf[:, (kb + j) * 128:(kb + j + 1) * 128], identb[:],
                            [r_abf, r_const], r_bank[bT], j == 0)
                self.cp('act' if (self._tbb % 2) else 'dve', aT[:, kb:kb + nb, :].rearrange("p b q -> p (b q)"),
                        bank_bf(bT)[:, 0:nb * 128], [r_bank[bT]], [r_aT] if first else (), () if first else [r_aT])
                first = False
                kb += nb
            bo = self._sb % 6
            self._sb += 1
            for b_ in range(nkb):
                self.mm(bank(bo)[:, 0:256], aT[:, b_, :], V[:, b_, :], b_ == 0, b_ == nkb - 1, [r_aT, r_h], r_bank[bo])
            self.act(junk[:], bank(bo)[:, 0:256], AF.Square, [r_bank[bo]], [r_otmp], accum_out=st[:, 29:30])
            self.ts('dve', st[:, 30:31], st[:, 29:30], 1.0 / 256.0, EPS, ALU.mult, ALU.add, [r_st[u], r_otmp], (), [r_st[u]])
            self.act(st[:, 30:31], st[:, 30:31], AF.Ln, [r_st[u]], (), [r_st[u]])
            self.act(st[:, 30:31], st[:, 30:31], AF.Exp, [r_st[u]], (), [r_st[u]], scale=-0.5)
            self.stt(otmp[:], bank(bo)[:, 0:256], st[:, 30:31], wsub[:], ALU.mult, ALU.mult,
                     [r_bank[bo], r_st[u], r_lt], [r_otmp])
            self.tt('dve', yout, otmp[:], gate_ap, ALU.mult, [r_otmp, r_h], [r_yst] if first_out else (),
                    () if first_out else [r_yst])

        for h in range(4):
            vs = slice(h * 256, (h + 1) * 256)
            P.dma('pool', ckl[:], d['c_df_k'][l, :, vs].rearrange("(t p) e -> p t e", p=128), writes=[r_ckl])
            first = True
            for t in range(2):
                P.dma('sp', qT[t][:], d['QTB'][h * 2 + t, :, 0:LS], writes=[r_h] if first else (),
                      swrites=() if first else [r_h])
                first = False
                P.dma('sp', kT[t][:, PAST:PAST + LS], d['KTB'][h * 2 + t, :, 0:LS], swrites=[r_h])
                bT = 6 + (t % 2)
                for tb in range(4):
                    self.tr(bank_bf(bT)[:, tb * 128:(tb + 1) * 128], ckl[:, tb, t * 128:(t + 1) * 128], identb[:],
                            [r_ckl, r_const], r_bank[bT], tb == 0)
                self.cp('dve', kT[t][:, 0:PAST], bank_bf(bT)[:, 0:512], [r_bank[bT]], (), [r_h])
            P.dma('pool', V[:, 0:4, :], d['c_df_v'][l, :, vs].rearrange("(t p) e -> p t e", p=128), swrites=[r_h])
            P.dma('sp', V[:, 4:20, :], d['VB'][0:LS, vs].rearrange("(t p) e -> p t e", p=128), swrites=[r_h])
            P.dma('sp', gate[:], d['GT'][0:LS, GW + h * 256:GW + (h + 1) * 256].rearrange("(t p) e -> p t e", p=128),
                  swrites=[r_h])
            for qb in range(16):
                unit(PAST + LS, slice(qb * 128, (qb + 1) * 128), yst[:, qb, :], qb == 0, gate[:, qb, :])
            P.dma('sp', d['Y'][0:LS, GW + h * 256:GW + (h + 1) * 256].rearrange("(t p) e -> p t e", p=128), yst[:],
                  reads=[r_yst], swrites=[r_y])
            for sq in range(2):
                t0 = LS + sq * LP
                first = True
                for t in range(2):
                    P.dma('sp', qT[t][:, 0:LP], d['QTB'][h * 2 + t, :, t0:t0 + LP], writes=[r_h] if first else (),
                          swrites=() if first else [r_h])
                    first = False
                    P.dma('sp', kT[t][:, 0:LP], d['KTB'][h * 2 + t, :, t0:t0 + LP], swrites=[r_h])
                P.dma('sp', V[:, 0:2, :], d['VB'][t0:t0 + LP, vs].rearrange("(t p) e -> p t e", p=128), swrites=[r_h])
                P.dma('sp', gate[:, 0:2, :],
                      d['GT'][t0:t0 + LP, GW + h * 256:GW + (h + 1) * 256].rearrange("(t p) e -> p t e", p=128),
                      swrites=[r_h])
                for qb in range(2):
                    unit(LP, slice(qb * 128, (qb + 1) * 128), yst[:, qb, :], qb == 0, gate[:, qb, :])
                P.dma('sp', d['Y'][t0:t0 + LP, GW + h * 256:GW + (h + 1) * 256].rearrange("(t p) e -> p t e", p=128),
                      yst[:, 0:2, :], reads=[r_yst], swrites=[r_y])
    def mixer_C(self, l):
        nc, P, A = self.nc, self.P, self.A
        d = self.d
        bank, bank_bf, r_bank = self.bank, self.bank_bf, self.r_bank
        cdt, r_tab = self.tabs['cdt'], self.tabs['r_tab']
        A.reset()
        r_y = P.res('Yw')
        rmask = A([128, 2, 128], F32, 'rmask'); r_rm = P.res('rmask')
        P.dma('sp', rmask[:], d['k_rmask'].rearrange("a j i -> j a i"), writes=[r_rm])
        NS = 2
        names = ('QFT', 'QBT', 'KFT', 'KBT')
        fT = [[A([128, LS], BF16, 'c%s%d' % (nm, i)) for nm in names] for i in range(NS)]
        tk = [[A([128, 16, 128], BF16, 'c%s%d' % (nm, i)) for nm in ('KF', 'KB', 'VC')] for i in range(NS)]
        gate = [A([128, 16, 128], BF16, 'cg%d' % i) for i in range(NS)]
        oacc = [A([128, 16, 128], F32, 'oacc%d' % i) for i in range(NS)]
        tmp1 = [A([128, 16, 128], F32, 'ctmp%d' % i) for i in range(NS)]
        yst = [A([128, 16, 128], BF16, 'cy%d' % i) for i in range(NS)]
        S = [[A([128, 128], F32, 'S%d_%d' % (i, dr)) for dr in range(2)] for i in range(NS)]
        Sb = [[A([128, 128], BF16, 'Sb%d_%d' % (i, dr)) for dr in range(2)] for i in range(NS)]
        atm = [[A([128, 128], BF16, 'atm%d_%d' % (i, dr)) for dr in range(2)] for i in range(NS)]
        stat = [A([128, 4, 16], F32, 'cst%d' % i) for i in range(NS)]
        r_h = [P.res() for i in range(NS)]
        r_o = [P.res() for i in range(NS)]
        r_S = [[P.res() for dr in range(2)] for i in range(NS)]
        r_Sb = [[P.res() for dr in range(2)] for i in range(NS)]
        r_atm = [[P.res() for dr in range(2)] for i in range(NS)]
        r_stat = [P.res() for i in range(NS)]
        r_yst = [P.res() for i in range(NS)]
        r_pa = [[P.res('pa', bank=i * 2 + dr) for dr in range(2)] for i in range(NS)]
        r_po = [[P.res('po', bank=i * 2 + dr) for dr in range(2)] for i in range(NS)]
        r_pd = [[P.res('pd', bank=i * 2 + dr) for dr in range(2)] for i in range(NS)]

        seqs = [(0, LS, True, None)] + [(LS + sq * LP, LP, False, sq) for sq in range(2)]
        for (t0, L, is_s, sq) in seqs:
            ncn = L // 128
            for hg in range(0, 8, NS):
                for i in range(NS):
                    h = hg + i
                    hs = slice(h * 128, (h + 1) * 128)
                    first = True
                    for j, nm in enumerate(names):
                        P.dma('sp', fT[i][j][:, 0:L], d[nm][h, :, t0:t0 + L], writes=[r_h[i]] if first else (),
                              swrites=() if first else [r_h[i]])
                        first = False
                    for j, nm in enumerate(('KF', 'KB', 'VC')):
                        P.dma('sp', tk[i][j][:, 0:ncn, :], d[nm][t0:t0 + L, hs].rearrange("(t p) e -> p t e", p=128),
                              swrites=[r_h[i]])
                    P.dma('sp', gate[i][:, 0:ncn, :],
                          d['GT'][t0:t0 + L, 2 * GW + h * 128:2 * GW + (h + 1) * 128].rearrange("(t p) e -> p t e", p=128),
                          swrites=[r_h[i]])
                    for dr in range(2):
                        if is_s:
                            P.dma('sp', S[i][dr][:], d['st_ret'][l, dr, h], writes=[r_S[i][dr]])
                        else:
                            self.memset('dve', S[i][dr][:], 0.0, writes=[r_S[i][dr]])
                        self.cp('act', Sb[i][dr][:], S[i][dr][:], [r_S[i][dr]], [r_Sb[i][dr]])
                for step in range(ncn):
                    for i in range(NS):
                        h = hg + i
                        for dr in range(2):
                            c = step if dr == 0 else ncn - 1 - step
                            cs = slice(c * 128, (c + 1) * 128)
                            pb = i * 2 + dr
                            qTt, kTt = fT[i][dr], fT[i][2 + dr]
                            ktok, vtok = tk[i][dr], tk[i][2]
                            self.mm(bank(pb)[:, 0:128], kTt[:, cs], qTt[:, cs], True, True, [r_h[i]], r_pa[i][dr])
                            self.tt('dve', atm[i][dr][:], bank(pb)[:, 0:128], rmask[:, dr, :], ALU.mult,
                                    [r_pa[i][dr], r_rm], [r_atm[i][dr]])
                            self.mm(bank(pb)[:, 128:256], atm[i][dr][:], vtok[:, c, :], True, False,
                                    [r_atm[i][dr], r_h[i]], r_po[i][dr], first=True)
                            self.mm(bank(pb)[:, 128:256], qTt[:, cs], Sb[i][dr][:], False, True,
                                    [r_h[i], r_Sb[i][dr]], r_po[i][dr], first=False)
                            first_touch = (step < (ncn + 1) // 2) if ncn > 1 else (dr == 0)
                            if ncn % 2 == 1 and step == ncn // 2:
                                first_touch = (dr == 0)
                            if first_touch:
                                self.cp('act', oacc[i][:, c, :], bank(pb)[:, 128:256], [r_po[i][dr]], (), [r_o[i]])
                            else:
                                self.tt('dve', oacc[i][:, c, :], oacc[i][:, c, :], bank(pb)[:, 128:256], ALU.add,
                                        [r_po[i][dr], r_o[i]], [r_o[i]])
                            self.mm(bank(pb)[:, 256:384], ktok[:, c, :], vtok[:, c, :], True, True, [r_h[i]], r_pd[i][dr])
                            cd = cdt[:, dr * 8 + h:dr * 8 + h + 1]
                            self.ts('dve', S[i][dr][:], S[i][dr][:], cd, None, ALU.mult, None, [r_S[i][dr], r_tab],
                                    [r_S[i][dr]])
                            self.stt(S[i][dr][:], bank(pb)[:, 256:384], cd, S[i][dr][:], ALU.mult, ALU.add,
                                     [r_pd[i][dr], r_S[i][dr], r_tab], [r_S[i][dr]])
                            self.cp('act', Sb[i][dr][:], S[i][dr][:], [r_S[i][dr]], [r_Sb[i][dr]])
                for i in range(NS):
                    h = hg + i
                    o3 = oacc[i][:, 0:ncn, :]
                    t3 = tmp1[i][:, 0:ncn, :]
                    sm, sq2, mean, rstd = (stat[i][:, k, 0:ncn] for k in range(4))
                    P.add('dve', lambda e, o=sm, i_=o3: e.reduce_sum(out=o, in_=i_, axis=AX.X), [r_o[i]], [r_stat[i]])
                    self.act(t3, o3, AF.Square, [r_o[i]], [r_yst[i]])
                    P.add('dve', lambda e, o=sq2, i_=t3: e.reduce_sum(out=o, in_=i_, axis=AX.X), [r_yst[i]], (),
                          [r_stat[i]])
                    self.ts('dve', mean, sm, 1.0 / 128.0, None, ALU.mult, None, [r_stat[i]], [r_stat[i]])
                    self.tt('dve', sm, mean, mean, ALU.mult, [r_stat[i]], [r_stat[i]])
                    self.stt(sq2, sq2, 1.0 / 128.0, sm, ALU.mult, ALU.subtract, [r_stat[i]], [r_stat[i]])
                    self.rstd_from(rstd, sq2, r_stat[i])
                    self.tt('dve', t3, o3, mean.unsqueeze(2).broadcast_to([128, ncn, 128]), ALU.subtract,
                            [r_o[i], r_stat[i]], [r_yst[i]])
                    self.tt('dve', t3, t3, rstd.unsqueeze(2).broadcast_to([128, ncn, 128]), ALU.mult,
                            [r_yst[i], r_stat[i]], [r_yst[i]])
                    self.tt('dve', yst[i][:, 0:ncn, :], t3, gate[i][:, 0:ncn, :], ALU.mult, [r_yst[i], r_h[i]],
                            [r_yst[i]])
                    P.dma('sp', d['Y'][t0:t0 + L, 2 * GW + h * 128:2 * GW + (h + 1) * 128].rearrange("(t p) e -> p t e", p=128),
                          yst[i][:, 0:ncn, :], reads=[r_yst[i]], swrites=[r_y])
                    if not is_s:
                        for dr in range(2):
                            P.dma('sp', d['o_st'][sq, l, dr, h], S[i][dr][:], reads=[r_S[i][dr]],
                                  swrites=[self.r_scr['OUT']])

    def mixer_D(self, l):
        nc, P, A = self.nc, self.P, self.A
        d = self.d
        A.reset()
        r_y = P.res('Yw')
        NB = 2
        Lmax = LS
        u = [A([128, Lmax + 2], F32, 'du%d' % i) for i in range(NB)]
        gd = [A([128, Lmax], F32, 'dgd%d' % i) for i in range(NB)]
        tacc = [A([128, Lmax], F32, 'dt%d' % i) for i in range(NB)]
        yb = [A([128, Lmax], BF16, 'dy%d' % i) for i in range(NB)]
        cw = [A([128, 3], F32, 'cw%d' % i) for i in range(NB)]
        r_in = [P.res() for i in range(NB)]
        r_t = [P.res() for i in range(NB)]
        r_yb = [P.res() for i in range(NB)]
        n = 0
        for j in range(8):
            rows = slice(j * 128, (j + 1) * 128)
            for (t0, L) in ((0, LS), (LS, LP), (LS + LP, LP)):
                b = n % NB
                n += 1
                self.memset('dve', u[b][:, 0:1], 0.0, writes=[r_in[b]])
                self.memset('dve', u[b][:, L + 1:L + 2], 0.0, swrites=[r_in[b]])
                P.dma('sp', u[b][:, 1:L + 1], d['UT'][rows, t0:t0 + L], swrites=[r_in[b]])
                P.dma('sp', gd[b][:, 0:L], d['GDT'][rows, t0:t0 + L], swrites=[r_in[b]])
                P.dma('sp', cw[b][:], d['conv_w'][l, rows, :], swrites=[r_in[b]])
                self.act(tacc[b][:, 0:L], u[b][:, 0:L], AF.Identity, [r_in[b]], [r_t[b]], scale=cw[b][:, 0:1])
                self.stt(tacc[b][:, 0:L], u[b][:, 1:L + 1], cw[b][:, 1:2], tacc[b][:, 0:L], ALU.mult, ALU.add,
                         [r_in[b], r_t[b]], [r_t[b]])
                self.stt(tacc[b][:, 0:L], u[b][:, 2:L + 2], cw[b][:, 2:3], tacc[b][:, 0:L], ALU.mult, ALU.add,
                         [r_in[b], r_t[b]], [r_t[b]])
                self.tt('dve', yb[b][:, 0:L], tacc[b][:, 0:L], gd[b][:, 0:L], ALU.mult, [r_t[b], r_in[b]], [r_yb[b]])
                P.dma('sp', d['YTD'][rows, t0:t0 + L], yb[b][:, 0:L], reads=[r_yb[b]], swrites=[r_y])

    def phase_EF(self, l):
        nc, P, A = self.nc, self.P, self.A
        d = self.d
        bank, bank_bf, r_bank = self.bank, self.bank_bf, self.r_bank
        identb, r_const = self.identb, self.r_const
        A.off = self.base_mark
        r_z = P.res('Zw')
        gbc = A([128, 2, D], F32, 'gbc'); r_g = P.res('gbc')
        for cv in range(2):
            P.dma('sp', gbc[:, cv, :], d['MADA'][l, cv:cv + 1, 2 * D:3 * D].broadcast_to([128, D]),
                  writes=[r_g] if cv == 0 else (), swrites=() if cv == 0 else [r_g])
        yT = A([128, 32, 1280], BF16, 'yT')
        r_yT = [P.res() for s in range(10)]
        wbuf = [A([128, 32, 512], BF16, 'wo%d' % i) for i in range(2)]
        r_wbuf = [P.res() for i in range(2)]
        ytile = [A([128, 3 * GW], BF16, 'ytile0')] * 2
        r_ytile = [P.res()] * 2
        xch = [A([128, 512], F32, 'xch%d' % i) for i in range(2)] + [None]
        xch[2] = xch[0]
        r_xch = [P.res() for i in range(2)]
        r_xch.append(r_xch[0])
        zst = [A([128, 512], F32, 'zst%d' % i) for i in range(2)] + [None]
        zst[2] = zst[0]
        r_zst = [P.res() for i in range(2)]
        r_zst.append(r_zst[0])
        self.alloc_stg()
        w_o = d['w_out'][l].rearrange("(kc p) n -> p kc n", p=128)
        for gi, grp in enumerate(self.groups):
            tbi = 0
            for s, tt in enumerate(grp):
                b = s % 2
                tok0 = tt * 128
                P.dma('sp', ytile[b][:], d['Y'][tok0:tok0 + 128, :], writes=[r_ytile[b]])
                P.dma('sp', yT[:, 24:32, s * 128:(s + 1) * 128],
                      d['YTD'][:, tok0:tok0 + 128].rearrange("(j p) t -> p j t", p=128), swrites=[r_yT[s]])
                for k8 in range(3):
                    bT = 4 + (tbi % 4)
                    tbi += 1
                    for j in range(8):
                        kc = k8 * 8 + j
                        self.tr(bank_bf(bT)[:, j * 128:(j + 1) * 128], ytile[b][:, kc * 128:(kc + 1) * 128], identb[:],
                                [r_ytile[b], r_const], r_bank[bT], j == 0)
                    self.cp('act' if k8 % 2 == 0 else 'dve', yT[:, k8 * 8:(k8 + 1) * 8, s * 128:(s + 1) * 128],
                            bank_bf(bT)[:, 0:1024].rearrange("p (j t) -> p j t", j=8), [r_bank[bT]], (), [r_yT[s]])
            self.drain(self.w_pieces(wbuf[0][:], r_wbuf[0], w_o[:, :, 0:512]))
            acc_i = 0
            xi = 0
            for oc in range(8):
                ws = oc % 2
                wgen = None
                if oc + 1 < 8:
                    wgen = self.w_pieces(wbuf[(oc + 1) % 2][:], r_wbuf[(oc + 1) % 2],
                                         w_o[:, :, (oc + 1) * 512:(oc + 2) * 512])
                cols = slice(oc * 512, (oc + 1) * 512)
                for s, tt in enumerate(grp):
                    pb = acc_i % 4
                    acc_i += 1
                    k = xi % 2
                    xi += 1
                    cv = 0 if tt < 16 else 1
                    P.dma('sp', xch[k][:], self.x_src(l, tt, oc * 512, (oc + 1) * 512), writes=[r_xch[k]])
                    for kc in range(32):
                        self.mm(bank(pb), yT[:, kc, s * 128:(s + 1) * 128], wbuf[ws][:, kc, :], kc == 0, kc == 31,
                                [r_yT[s], r_wbuf[ws]], r_bank[pb])
                    self.tt('dve', zst[k][:], bank(pb), gbc[:, cv, cols], ALU.mult, [r_bank[pb], r_g], [r_zst[k]])
                    self.stt(zst[k][:], xch[k][:], ALPHA, zst[k][:], ALU.mult, ALU.add, [r_xch[k], r_zst[k]],
                             [r_zst[k]], eng='dve')
                    P.dma('sp', d['Z'][tt * 128:(tt + 1) * 128, cols], zst[k][:], reads=[r_zst[k]], swrites=[r_z])
                    self.drain(wgen, 2)
                self.drain(wgen)
            P.barrier()
        A.off = self.base_mark
        gb = A([128, 2, D], F32, 'lngb'); r_gb = P.res('lngb')
        P.dma('sp', gb[:, 0, :], d['ln_g'][l:l + 1, :].broadcast_to([128, D]), writes=[r_gb])
        P.dma('sp', gb[:, 1, :], d['ln_b'][l:l + 1, :].broadcast_to([128, D]), swrites=[r_gb])
        zt = [A([128, D], F32, 'zt%d' % i) for i in range(2)]
        r_zt = [P.res() for i in range(2)]
        st = [A([128, 8, 6], F32, 'fst%d' % i) for i in range(2)]
        mv = [A([128, 4], F32, 'fmv%d' % i) for i in range(2)]
        r_mv = [P.res() for i in range(2)]
        r_x = P.res('Xw')
        for tt in range(NT):
            b = tt % 2
            P.dma('sp', zt[b][:], d['Z'][tt * 128:(tt + 1) * 128, :], writes=[r_zt[b]])
            for c8 in range(8):
                P.add('dve', lambda e, o=st[b][:, c8, :], i=zt[b][:, c8 * 512:(c8 + 1) * 512]: e.bn_stats(out=o, in_=i),
                      reads=[r_zt[b]], writes=[r_mv[b]] if c8 == 0 else (), swrites=() if c8 == 0 else [r_mv[b]])
            P.add('dve', lambda e, o=mv[b][:, 0:2], i=st[b][:].rearrange("p a b -> p (a b)"): e.bn_aggr(out=o, in_=i),
                  reads=[r_mv[b]], writes=[r_mv[b]])
            self.rstd_from(mv[b][:, 2:3], mv[b][:, 1:2], r_mv[b])
            self.stt(mv[b][:, 3:4], mv[b][:, 0:1], -1.0, mv[b][:, 2:3], ALU.mult, ALU.mult, [r_mv[b]], [r_mv[b]])
            self.act(zt[b][:], zt[b][:], AF.Identity, [r_zt[b], r_mv[b]], [r_zt[b]], scale=mv[b][:, 2:3],
                     bias=mv[b][:, 3:4])
            self.tt('dve', zt[b][:], zt[b][:], gb[:, 0, :], ALU.mult, [r_zt[b], r_gb], [r_zt[b]])
            self.tt('dve', zt[b][:], zt[b][:], gb[:, 1, :], ALU.add, [r_zt[b], r_gb], [r_zt[b]])
            if l == DEPTH - 1:
                if tt < 16:
                    dst = d['o_ys'][tt * 128:(tt + 1) * 128, :]
                else:
                    dst = d['o_yp'][(tt - 16) * 128:(tt - 15) * 128, :]
            else:
                dst = d['XRES'][tt * 128:(tt + 1) * 128, :]
            P.dma('sp', dst, zt[b][:], reads=[r_zt[b]], swrites=[r_x])


def _consts():
    k = {}
    k['k_ident'] = np.eye(128, dtype=np.float32)
    t = np.arange(LS)
    nf = 32
    inv = (10000.0 ** (-np.arange(nf, dtype=np.float32) / nf)).astype(np.float32)
    ang_r = ((t // 64).astype(np.float32)[:, None] * inv).astype(np.float32)
    ang_c = ((t % 64).astype(np.float32)[:, None] * inv).astype(np.float32)
    cos = np.zeros((LS, 2, 2, 32), np.float32)
    sin = np.zeros((LS, 2, 2, 32), np.float32)
    for a, ang in enumerate((ang_r, ang_c)):
        cos[:, a, 0] = np.cos(ang); cos[:, a, 1] = np.cos(ang)
        sin[:, a, 0] = -np.sin(ang); sin[:, a, 1] = np.sin(ang)
    k['k_cos'] = cos.reshape(LS, 128)
    k['k_sin'] = sin.reshape(LS, 128)
    rows = LS // 64
    m = np.full((5, 128, 640), NEG, np.float32)
    col = np.arange(64)
    c0 = np.clip(col - 8, 0, 48)
    col_ok = (col[None, :] >= c0[:, None]) & (col[None, :] < c0[:, None] + 16)
    for ty, qb in enumerate((0, 1, 5, 14, 15)):
        tw = min(max(qb - 2, 0), 11)
        for half in range(2):
            r = 2 * qb + half
            w0 = min(max(r - 4, 0), rows - 8)
            for kr in range(10):
                ra = 2 * tw + kr
                if w0 <= ra < w0 + 8:
                    blk = np.where(col_ok, 0.0, NEG).astype(np.float32)
                    m[ty, half * 64:(half + 1) * 64, kr * 64:(kr + 1) * 64] = blk
    k['k_namask'] = m
    j = np.arange(128)[:, None]
    i = np.arange(128)[None, :]
    k['k_rmask'] = np.stack([(j <= i), (j >= i)]).astype(np.float32)
    ii = np.arange(128, dtype=np.float32)
    k['k_coef'] = np.stack([ii + 1.0, -(ii + 1.0), 128.0 - ii, -(128.0 - ii)], 1).astype(np.float32)
    return k


_CACHE = {}


def _get_builder(debug=False, stop_after=None):
    key = (debug, stop_after)
    if key not in _CACHE:
        b = Builder(debug=debug, stop_after=stop_after)
        b.stats = b.build()
        _CACHE[key] = b
    return _CACHE[key]


def make_in_maps(inputs, cores):
    f = lambda a: np.ascontiguousarray(np.asarray(a, dtype=np.float32))
    k = _consts()
    shared = dict(
        w_ada=f(inputs['w_ada']), b_ada=f(inputs['b_ada']), w_in=f(inputs['w_in']), w_out=f(inputs['w_out']),
        ln_g=f(inputs['ln_g']), ln_b=f(inputs['ln_b']),
        na_bias=f(inputs['na_bias']).reshape(DEPTH, 120, 31),
        diff_lam=f(inputs['diff_lam']).reshape(DEPTH, 512), diff_subln=f(inputs['diff_subln']),
        ret_decay=f(inputs['ret_decay']).reshape(DEPTH, 16), conv_w=f(inputs['conv_w']), **k)
    xs, xp = f(inputs['x_sample']), f(inputs['x_prompt'])
    maps = []
    for i in cores:
        m = dict(shared)
        m['x_s'] = xs[i]
        m['x_p'] = xp[2 * i:2 * i + 2].reshape(2 * LP, D)
        m['c_na_k'] = f(inputs['cache_na_k'][i]).reshape(DEPTH, PAST, 1024)
        m['c_na_v'] = f(inputs['cache_na_v'][i]).reshape(DEPTH, PAST, 1024)
        m['c_df_k'] = f(inputs['cache_diff_k'][i]).reshape(DEPTH, PAST, 1024)
        m['c_df_v'] = f(inputs['cache_diff_v'][i]).reshape(DEPTH, PAST, 1024)
        m['st_ret'] = f(inputs['state_ret'][i])
        m['cvec'] = np.stack([f(inputs['c'])[i], f(inputs['c_ctx'])])
        maps.append(m)
    return maps


def kernel(**inputs):
    n = 8
    b = _get_builder()
    maps = make_in_maps(inputs, range(n))
    res = run_bass_kernel_spmd(b.nc, maps, core_ids=list(range(n)))
    R = res.results
    y_s = np.stack([R[i]['o_ys'] for i in range(n)])
    y_p = np.concatenate([R[i]['o_yp'].reshape(2, LP, D) for i in range(n)])

    def cat(nm, shape):
        return np.concatenate([R[i][nm].reshape((2, DEPTH, LP) + shape) for i in range(n)])
    nak = cat('o_nak', (8, HD))
    nav = cat('o_nav', (8, HD))
    dfk = cat('o_dfk', (4, 2, HD))
    dfv = cat('o_dfv', (4, 2 * HD))
    st = np.concatenate([R[i]['o_st'] for i in range(n)])
    return (y_p.astype(np.float32), y_s.astype(np.float32), nak, nav, dfk, dfv, st)
```

```python
import math
import contextlib
import numpy as np
import concourse.bass as bass
import concourse.mybir as mybir
from concourse.bass_utils import run_bass_kernel_spmd

F32 = mybir.dt.float32
BF16 = mybir.dt.bfloat16
AF = mybir.ActivationFunctionType
ALU = mybir.AluOpType
AX = mybir.AxisListType

D = 4096
DEPTH = 2
LS = 2048
LP = 256
NTOK = LS + 2 * LP
NT = NTOK // 128
PAST = 512
HD = 128
GW = 1024
DIN = 16384
ALPHA = (2 * DEPTH) ** 0.25
EPS = 1e-6
SCALE = HD ** -0.5
NEG = -30000.0
SB_LO = 16512
SB_HI = 229344

ENGS = ('pe', 'act', 'dve', 'pool', 'sp')


class Res:
    __slots__ = ('name', 'wc', 'wd', 'rc', 'rd', 'war_c', 'war_d', 'bank', 'gen')

    def __init__(self, name, bank=None):
        self.name = name
        self.bank = bank
        self.gen = None
        self.wc = {}
        self.wd = []
        self.rc = {}
        self.rd = []
        self.war_c = {}
        self.war_d = []


class Prog:
    def __init__(self, nc, n_sp=24, n_pool=16, n_act=4):
        self.nc = nc
        self.ins = []
        self.ring_sizes = {'sp': n_sp, 'pool': n_pool, 'act': n_act}
        self.all_res = []
        self.locks = {}

    def res(self, name='', bank=None):
        r = Res(name, bank)
        self.all_res.append(r)
        return r

    def add(self, eng, fn, reads=(), writes=(), swrites=(), dma=False):
        idx = len(self.ins)
        deps = set()
        for r in reads:
            deps.update(r.wc.values())
            deps.update(r.wd)
        for r in writes:
            deps.update(r.wc.values()); deps.update(r.wd)
            deps.update(r.rc.values()); deps.update(r.rd)
            deps.update(r.war_c.values()); deps.update(r.war_d)
        for r in swrites:
            if r.rc or r.rd:
                r.war_c, r.war_d = r.rc, r.rd
                r.rc, r.rd = {}, []
                r.wc, r.wd = {}, []
                r.gen = None
            deps.update(r.war_c.values()); deps.update(r.war_d)
            if r.gen is not None:
                deps.add(r.gen)
        for r in reads:
            if dma:
                r.rd.append(idx)
            else:
                r.rc[eng] = idx
        for r in writes:
            wc_ = dict(r.wc)
            for k_, v_ in r.rc.items():
                if wc_.get(k_, -1) < v_:
                    wc_[k_] = v_
            r.war_c, r.war_d = wc_, r.rd + r.wd
            r.rc, r.rd = {}, []
            r.gen = idx
            if dma:
                r.wc, r.wd = {}, [idx]
            else:
                r.wc, r.wd = {eng: idx}, []
        for r in swrites:
            if dma:
                r.wd.append(idx)
            else:
                r.wc[eng] = idx
        banks = None
        for grp_ in (reads, writes, swrites):
            for r in grp_:
                if r.bank is not None:
                    if banks is None:
                        banks = set()
                    banks.add(r.bank)
        if banks:
            for b_ in banks:
                L = self.locks.setdefault(b_, {})
                for e2, i2 in L.items():
                    if e2 != eng:
                        deps.add(i2)
                L[eng] = idx
        deps.discard(idx)
        self.ins.append([eng, fn, deps, dma])
        return idx

    def dma(self, q, out, in_, reads=(), writes=(), swrites=(), **kw):
        def fn(e, out=out, in_=in_, kw=kw):
            return e.dma_start(out=out, in_=in_, **kw)
        return self.add(q, fn, reads, writes, swrites, dma=True)

    def barrier(self):
        allr = self.res('barrier')
        deps = set()
        for r in self.all_res:
            deps.update(r.wc.values()); deps.update(r.wd)
            deps.update(r.rc.values()); deps.update(r.rd)
            deps.update(r.war_c.values()); deps.update(r.war_d)
        first = True
        for rnd in range(2):
            for eng in ENGS:
                def fn(e):
                    return e.nop()
                if rnd == 0:
                    i = self.add(eng, fn, writes=[allr])
                    if first:
                        self.ins[i][2].update(deps)
                        first = False
                else:
                    self.add(eng, fn, reads=[allr])
        for r in self.all_res:
            if r is allr:
                continue
            r.wc, r.wd, r.rc, r.rd, r.war_c, r.war_d = {}, [], {}, [], {}, []
            r.gen = None
        self.locks = {}
        self.all_res = [r for r in self.all_res if r is allr or getattr(r, 'name', '') != 'barrier']

    def emit(self):
        nc = self.nc
        ins = self.ins
        n = len(ins)
        engobj = {'pe': nc.tensor, 'act': nc.scalar, 'dve': nc.vector, 'pool': nc.gpsimd, 'sp': nc.sync}
        needed = [False] * n
        for i in range(n):
            eng, fn, deps, dma = ins[i]
            keep = []
            for d in deps:
                deng, _, _, ddma = ins[d]
                if (not dma) and (not ddma) and deng == eng and eng == 'pe':
                    continue
                keep.append(d)
            ins[i][2] = keep
            for d in keep:
                needed[d] = True
        self._stack = contextlib.ExitStack()
        sems = {e: self._stack.enter_context(nc.semaphore('s_' + e)) for e in ENGS}
        rings = {q: [self._stack.enter_context(nc.semaphore('r_%s%d' % (q, j))) for j in range(k)]
                 for q, k in self.ring_sizes.items()}
        ring_pos = {q: 0 for q in rings}
        ring_val = {q: [0] * len(rings[q]) for q in rings}
        ring_used = {q: [False] * len(rings[q]) for q in rings}
        tick = {e: 0 for e in ENGS}
        ev = [None] * n
        waited = {e: {} for e in engobj}
        nwaits = 0
        for i in range(n):
            eng, fn, deps, dma = ins[i]
            e = engobj[eng]
            need = {}
            if dma:
                q = eng
                pos = ring_pos[q]
                ring_pos[q] = (pos + 1) % len(rings[q])
                if ring_used[q][pos]:
                    need[('r', q, pos)] = ring_val[q][pos]
            for d in deps:
                k, v = ev[d]
                if need.get(k, -1) < v:
                    need[k] = v
            w = waited[eng]
            for k, v in need.items():
                if w.get(k, -1) >= v:
                    continue
                w[k] = v
                so = sems[k[1]] if k[0] == 'c' else rings[k[1]][k[2]]
                e.wait_ge(so, v)
                nwaits += 1
            inst = fn(e)
            if dma:
                ring_val[q][pos] += 16
                ring_used[q][pos] = True
                inst.then_inc(rings[q][pos], 16)
                ev[i] = (('r', q, pos), ring_val[q][pos])
            else:
                if needed[i]:
                    tick[eng] += 1
                    inst.then_inc(sems[eng], 1)
                    ev[i] = (('c', eng), tick[eng])
            ins[i][1] = None
        self.stats = dict(n=n, nwaits=nwaits, ticks=dict(tick))
        return self.stats


class Alloc:
    def __init__(self, nc):
        self.nc = nc
        self.off = SB_LO
        self.cnt = 0
        self.mark_ = SB_LO

    def __call__(self, shape, dt, name='t'):
        per = 1
        for s in shape[1:]:
            per *= s
        nbytes = per * (4 if dt == F32 else 2)
        nbytes = (nbytes + 63) // 64 * 64
        assert self.off + nbytes <= SB_HI, ('SBUF overflow', name, self.off, nbytes)
        self.cnt += 1
        t = self.nc.alloc_sbuf_tensor_at('%s_%d' % (name, self.cnt), list(shape), dt, offset=self.off)
        self.off += nbytes
        return t

    def mark(self):
        self.mark_ = self.off

    def reset(self):
        self.off = self.mark_


class Builder:
    def __init__(self, debug=False, stop_after=None, lite=()):
        self.debug = debug
        self.stop_after = stop_after
        self.lite = lite
        nc = self.nc = bass.Bass("TRN2", target_bir_lowering=False)
        self.P = Prog(nc)
        self.A = Alloc(nc)
        self.inputs = {}
        self.outputs = {}
        self.dbg_names = []

    def din(self, name, shape, dt=F32):
        if name in self.lite:
            shape = [DEPTH, 128, 512]
        t = self.nc.dram_tensor(name, list(shape), dt, kind="ExternalInput").ap()
        self.inputs[name] = t
        return t

    def dout(self, name, shape, dt=F32):
        t = self.nc.dram_tensor(name, list(shape), dt, kind="ExternalOutput").ap()
        self.outputs[name] = t
        return t

    def dscr(self, name, shape, dt=F32):
        if self.debug:
            t = self.nc.dram_tensor(name, list(shape), dt, kind="ExternalOutput").ap()
            self.dbg_names.append(name)
        else:
            t = self.nc.dram_tensor(name, list(shape), dt).ap()
        return t

    def mm(self, out, lhsT, rhs, start, stop, reads, wres, first=None):
        first = start if first is None else first
        fn = lambda e: e.matmul(out, lhsT=lhsT, rhs=rhs, start=start, stop=stop)
        if first:
            self.P.add('pe', fn, reads=reads, writes=[wres])
        else:
            self.P.add('pe', fn, reads=reads, swrites=[wres])

    def tr(self, out, in_, ident, reads, wres, excl):
        fn = lambda e: e.transpose(out=out, in_=in_, identity=ident)
        if excl:
            self.P.add('pe', fn, reads=reads, writes=[wres])
        else:
            self.P.add('pe', fn, reads=reads, swrites=[wres])

    def act(self, out, in_, func, reads, writes=(), swrites=(), **kw):
        self.P.add('act', lambda e: e.activation(out=out, in_=in_, func=func, **kw), reads, writes, swrites)

    def ts(self, eng, out, in0, s1, s2, op0, op1, reads, writes=(), swrites=(), **kw):
        if op1 is None:
            fn = lambda e: e.tensor_scalar(out=out, in0=in0, scalar1=s1, scalar2=None, op0=op0, **kw)
        else:
            fn = lambda e: e.tensor_scalar(out=out, in0=in0, scalar1=s1, scalar2=s2, op0=op0, op1=op1, **kw)
        self.P.add(eng, fn, reads, writes, swrites)

    def tt(self, eng, out, in0, in1, op, reads, writes=(), swrites=()):
        self.P.add(eng, lambda e: e.tensor_tensor(out=out, in0=in0, in1=in1, op=op), reads, writes, swrites)

    def stt(self, out, in0, scalar, in1, op0, op1, reads, writes=(), swrites=(), eng='dve'):
        self.P.add(eng, lambda e: e.scalar_tensor_tensor(out=out, in0=in0, scalar=scalar, in1=in1, op0=op0, op1=op1),
                   reads, writes, swrites)

    def cp(self, eng, out, in_, reads, writes=(), swrites=()):
        if eng == 'act':
            self.P.add('act', lambda e: e.copy(out=out, in_=in_), reads, writes, swrites)
        else:
            self.P.add(eng, lambda e: e.tensor_copy(out=out, in_=in_), reads, writes, swrites)

    def memset(self, eng, out, val, writes=(), swrites=()):
        self.P.add(eng, lambda e: e.memset(out, val), (), writes, swrites)


    NSTG = 4
    NSTG_A = 8

    def alloc_stg(self, n=4):
        self.NSTG = n
        self.stg = [self.A([128, 2, 512], F32, 'stg%d' % i) for i in range(self.NSTG)]
        self.r_stg = [self.P.res('stg%d' % i) for i in range(self.NSTG)]
        self._wst = 0

    def w_pieces(self, dst, r_dst, src, engs=('pool',)):
        P = self.P
        first = True
        for pc in range(16):
            slot = self._wst % self.NSTG
            self._wst += 1
            P.dma('sp', self.stg[slot][:], src[:, 2 * pc:2 * pc + 2, :], writes=[self.r_stg[slot]])
            self.cp(engs[pc % len(engs)], dst[:, 2 * pc:2 * pc + 2, :], self.stg[slot][:], [self.r_stg[slot]],
                    [r_dst] if first else (), () if first else [r_dst])
            first = False
            yield

    def w_pieces_d(self, dst, r_dst, srcs):
        P = self.P
        first = True
        dv = dst.rearrange("p k (q c) -> p k q c", q=4)
        for q in range(4):
            for k8 in range(4):
                slot = self._wst % self.NSTG
                self._wst += 1
                sv = self.stg[slot][:].rearrange("p a (b c) -> p (a b) c", c=128)
                P.dma('sp', sv, srcs[q][:, k8 * 8:(k8 + 1) * 8, :], writes=[self.r_stg[slot]])
                self.cp('pool', dv[:, k8 * 8:(k8 + 1) * 8, q, :], sv, [self.r_stg[slot]],
                        [r_dst] if first else (), () if first else [r_dst])
                first = False
                yield

    @staticmethod
    def drain(gen, n=None):
        if gen is None:
            return
        if n is None:
            for _ in gen:
                pass
        else:
            for _ in range(n):
                if next(gen, 'end') == 'end':
                    break

    def rstd_from(self, dst, src, res, mult=1.0):
        self.ts('dve', dst, src, mult, EPS, ALU.mult, ALU.add, [res], [res])
        self.act(dst, dst, AF.Ln, [res], [res])
        self.act(dst, dst, AF.Exp, [res], [res], scale=-0.5)

    def build(self):
        nc, P, A = self.nc, self.P, self.A
        d = self.d = {}
        d['x_s'] = self.din('x_s', [LS, D])
        d['x_p'] = self.din('x_p', [2 * LP, D])
        d['c_na_k'] = self.din('c_na_k', [DEPTH, PAST, 8 * HD])
        d['c_na_v'] = self.din('c_na_v', [DEPTH, PAST, 8 * HD])
        d['c_df_k'] = self.din('c_df_k', [DEPTH, PAST, 8 * HD])
        d['c_df_v'] = self.din('c_df_v', [DEPTH, PAST, 4 * 2 * HD])
        d['st_ret'] = self.din('st_ret', [DEPTH, 2, 8, HD, HD])
        d['cvec'] = self.din('cvec', [2, D])
        d['w_ada'] = self.din('w_ada', [DEPTH, D, 3 * D])
        d['b_ada'] = self.din('b_ada', [DEPTH, 3 * D])
        d['w_in'] = self.din('w_in', [DEPTH, D, DIN])
        d['w_out'] = self.din('w_out', [DEPTH, D, D])
        d['ln_g'] = self.din('ln_g', [DEPTH, D])
        d['ln_b'] = self.din('ln_b', [DEPTH, D])
        d['na_bias'] = self.din('na_bias', [DEPTH, 8 * 15, 31])
        d['diff_lam'] = self.din('diff_lam', [DEPTH, 4 * HD])
        d['diff_subln'] = self.din('diff_subln', [DEPTH, 2 * HD])
        d['ret_decay'] = self.din('ret_decay', [DEPTH, 16])
        d['conv_w'] = self.din('conv_w', [DEPTH, GW, 3])
        d['k_ident'] = self.din('k_ident', [128, 128])
        d['k_cos'] = self.din('k_cos', [LS, 128])
        d['k_sin'] = self.din('k_sin', [LS, 128])
        d['k_namask'] = self.din('k_namask', [5, 128, 640])
        d['k_rmask'] = self.din('k_rmask', [2, 128, 128])
        d['k_coef'] = self.din('k_coef', [128, 4])
        d['o_ys'] = self.dout('o_ys', [LS, D])
        d['o_yp'] = self.dout('o_yp', [2 * LP, D])
        d['o_nak'] = self.dout('o_nak', [2, DEPTH, LP, GW])
        d['o_nav'] = self.dout('o_nav', [2, DEPTH, LP, GW])
        d['o_dfk'] = self.dout('o_dfk', [2, DEPTH, LP, GW])
        d['o_dfv'] = self.dout('o_dfv', [2, DEPTH, LP, GW])
        d['o_st'] = self.dout('o_st', [2, DEPTH, 2, 8, HD, HD])
        d['MADA'] = self.dscr('MADA', [DEPTH, 2, 3 * D])
        d['XRES'] = self.dscr('XRES', [NTOK, D])
        for nm in ('QTA', 'KTA', 'QTB', 'KTB', 'QFT', 'QBT', 'KFT', 'KBT'):
            d[nm] = self.dscr(nm, [8, HD, NTOK], BF16)
        for nm in ('VA', 'VB', 'VC', 'KF', 'KB'):
            d[nm] = self.dscr(nm, [NTOK, GW], BF16)
        d['GT'] = self.dscr('GT', [NTOK, 3 * GW], BF16)
        d['UT'] = self.dscr('UT', [GW, NTOK])
        d['GDT'] = self.dscr('GDT', [GW, NTOK])
        d['Y'] = self.dscr('Y', [NTOK, 3 * GW], BF16)
        d['YTD'] = self.dscr('YTD', [GW, NTOK], BF16)
        d['Z'] = self.dscr('Z', [NTOK, D])
        d['PBREP'] = self.dscr('PBREP', [120, 64, 128])

        self._es = contextlib.ExitStack()
        PS = self._es.enter_context(nc.psum_tensor("psum_all", [128, 4096], F32))
        self.PS = PS

        def bank(b, lo=0, hi=512):
            return PS[:, b * 512 + lo: b * 512 + hi]

        def bank_bf(b, lo=0, hi=1024):
            return PS[:, b * 512 + lo // 2: b * 512 + hi // 2].bitcast(BF16)
        self.bank, self.bank_bf = bank, bank_bf
        self.r_bank = [P.res('bank%d' % b, bank=b) for b in range(8)]
        r_bank = self.r_bank

        self.r_const = r_const = P.res('const')
        self.ident = ident = A([128, 128], F32, 'ident')
        self.identb = identb = A([128, 128], BF16, 'identb')
        P.dma('sp', ident[:], d['k_ident'][:, :], swrites=[r_const])
        P.dma('pool', identb[:], d['k_ident'][:, :], swrites=[r_const])
        self.coef = coef = A([128, 4], F32, 'coef')
        P.dma('sp', coef[:], d['k_coef'][:, :], swrites=[r_const])
        A.mark()
        self.base_mark = A.mark_

        cT = A([128, 32, 2], F32, 'cT'); r_cT = P.res('cT')
        cTb = A([128, 32, 2], BF16, 'cTb'); r_cTb = P.res('cTb')
        wbuf = [A([128, 32, 512], BF16, 'wbuf%d' % i) for i in range(2)]
        r_wbuf = [P.res('wbuf%d' % i) for i in range(2)]
        mrow = [A([2, 512], F32, 'mrow%d' % i) for i in range(2)]
        brow = [A([2, 512], F32, 'brow%d' % i) for i in range(2)]
        r_mrow = [P.res() for i in range(2)]
        r_brow = [P.res() for i in range(2)]
        self.alloc_stg(8)
        for cv in range(2):
            P.dma('sp', cT[:, :, cv], d['cvec'][cv].rearrange("(kc p) -> p kc", p=128), swrites=[r_cT],
                  allow_slow_non_contiguous=True)
        self.act(cTb[:], cT[:], AF.Silu, [r_cT], [r_cTb])
        wcnt = 0
        r_mada = P.res('MADA')
        for l in range(DEPTH if 'w_ada' not in self.lite else 0):
            wl = d['w_ada'][l].rearrange("(kc p) n -> p kc n", p=128)
            for ch in range(24):
                s = wcnt % 2
                wcnt += 1
                self.drain(self.w_pieces(wbuf[s][:], r_wbuf[s], wl[:, :, ch * 512:(ch + 1) * 512],
                                          engs=('pool', 'act', 'dve', 'act', 'dve')))
                P.dma('sp', brow[s][:], d['b_ada'][l:l + 1, ch * 512:(ch + 1) * 512].broadcast_to([2, 512]),
                      writes=[r_brow[s]])
                pb = ch % 4
                for kc in range(32):
                    self.mm(bank(pb)[0:2, :], cTb[:, kc, :], wbuf[s][:, kc, :], kc == 0, kc == 31,
                            [r_cTb, r_wbuf[s]], r_bank[pb])
                self.tt('dve', mrow[s][:], bank(pb)[0:2, :], brow[s][:], ALU.add,
                        [r_bank[pb], r_brow[s]], [r_mrow[s]])
                P.dma('sp', d['MADA'][l, :, ch * 512:(ch + 1) * 512], mrow[s][:], reads=[r_mrow[s]],
                      swrites=[r_mada])
        P.barrier()
        A.reset()
        if self.stop_after == 'A':
            return self.finish()

        self.groups = [list(range(0, 8)) + [16, 17], list(range(8, 16)) + [18, 19]]

        for l in range(DEPTH):
            self.layer(l)
            if self.stopped:
                break
        return self.finish()

    stopped = False

    def finish(self):
        self.P.barrier()
        return self.P.emit()

    def x_src(self, l, tt, c0=0, c1=D):
        d = self.d
        if l == 0:
            if tt < 16:
                return d['x_s'][tt * 128:(tt + 1) * 128, c0:c1]
            return d['x_p'][(tt - 16) * 128:(tt - 15) * 128, c0:c1]
        return d['XRES'][tt * 128:(tt + 1) * 128, c0:c1]

    def layer(self, l):
        nc, P, A = self.nc, self.P, self.A
        d = self.d
        coef = self.coef
        A.mark_ = self.base_mark
        A.reset()
        r_tab = P.res('tab')
        mod = A([128, 2, 2, 32], F32, 'mod')
        for cv in range(2):
            for wh in range(2):
                P.dma('sp', mod[:, cv, wh, :],
                      d['MADA'][l, cv, wh * D:(wh + 1) * D].rearrange("(kc p) -> p kc", p=128),
                      swrites=[r_tab], allow_slow_non_contiguous=True)
        self.ts('dve', mod[:, :, 1, :], mod[:, :, 1, :], 1.0, None, ALU.add, None, [r_tab], [r_tab])
        cos2 = A([128, 16, 128], F32, 'cos2')
        sin2 = A([128, 16, 128], F32, 'sin2')
        P.dma('sp', cos2[:], d['k_cos'].rearrange("(t p) c -> p t c", p=128), swrites=[r_tab])
        P.dma('sp', sin2[:], d['k_sin'].rearrange("(t p) c -> p t c", p=128), swrites=[r_tab])
        ld = A([128, 16], F32, 'ld')
        P.dma('sp', ld[:], d['ret_decay'][l:l + 1, :].broadcast_to([128, 16]), writes=[r_tab])
        self.act(ld[:], ld[:], AF.Exp, [r_tab], [r_tab], scale=-1.0)
        self.ts('dve', ld[:], ld[:], 1.0, None, ALU.add, None, [r_tab], [r_tab])
        self.act(ld[:], ld[:], AF.Ln, [r_tab], [r_tab])
        self.ts('dve', ld[:], ld[:], -1.0, None, ALU.mult, None, [r_tab], [r_tab])
        dec = A([128, 4, 8], F32, 'dec')
        self.act(dec[:, 0, :], ld[:, 0:8], AF.Exp, [r_tab], [r_tab], scale=coef[:, 0:1])
        self.act(dec[:, 1, :], ld[:, 8:16], AF.Exp, [r_tab], [r_tab], scale=coef[:, 2:3])
        self.act(dec[:, 2, :], ld[:, 0:8], AF.Exp, [r_tab], [r_tab], scale=coef[:, 1:2])
        self.act(dec[:, 3, :], ld[:, 8:16], AF.Exp, [r_tab], [r_tab], scale=coef[:, 3:4])
        self.ts('dve', dec[:, 2:4, :], dec[:, 2:4, :], SCALE, None, ALU.mult, None, [r_tab], [r_tab])
        cdt = A([128, 16], F32, 'cdt')
        self.act(cdt[:], ld[:], AF.Exp, [r_tab], [r_tab], scale=128.0)
        A.mark()
        self.layer_mark = A.mark_
        self.tabs = dict(mod=mod, cos2=cos2, sin2=sin2, dec=dec, cdt=cdt, r_tab=r_tab)

        if self.stop_after == 'TAB%d' % l:
            self.stopped = True
            return
        self.phase_BC(l)
        if self.stopped:
            return
        A.mark_ = self.layer_mark
        P.barrier()
        if self.stop_after == 'BC%d' % l:
            self.stopped = True
            return
        self.phase_mixers(l)
        P.barrier()
        if self.stop_after == 'MIX%d' % l:
            self.stopped = True
            return
        self.phase_EF(l)
        P.barrier()
        if self.stop_after == 'L%d' % l:
            self.stopped = True

    def phase_BC(self, l):
        nc, P, A = self.nc, self.P, self.A
        d = self.d
        bank, bank_bf, r_bank = self.bank, self.bank_bf, self.r_bank
        ident, identb, r_const = self.ident, self.identb, self.r_const
        T = self.tabs
        r_tab = T['r_tab']
        mod, cos2, sin2, dec = T['mod'], T['cos2'], T['sin2'], T['dec']
        A.reset()
        hT = A([128, 32, 1280], BF16, 'hT')
        r_hT = [P.res('hT%d' % s) for s in range(10)]
        wbuf = [A([128, 32, 512], BF16, 'wb%d' % i) for i in range(2)]
        r_wbuf = [P.res('wb%d' % i) for i in range(2)]
        A.mark()
        w_l = d['w_in'][l]
        w_tok = w_l.rearrange("(kc p) n -> p kc n", p=128)
        r_scr = self.r_scr = getattr(self, 'r_scr', None) or {k: P.res(k) for k in
                                                             ('Q', 'V', 'G', 'UD', 'OUT', 'Y', 'Z', 'X')}

        for gi, grp in enumerate(self.groups):
            A.reset()
            xt = [A([128, D], F32, 'xt%d' % i) for i in range(2)]
            r_xt = [P.res() for i in range(2)]
            st = [A([128, 8, 6], F32, 'st%d' % i) for i in range(2)]
            mv = [A([128, 4], F32, 'mv%d' % i) for i in range(2)]
            r_mv = [P.res() for i in range(2)]
            tb_i = 0
            ev_i = 0
            for s, tt in enumerate(grp):
                b = s % 2
                cv = 0 if tt < 16 else 1
                P.dma('sp', xt[b][:], self.x_src(l, tt), writes=[r_xt[b]])
                for c8 in range(8):
                    P.add('dve', lambda e, o=st[b][:, c8, :], i=xt[b][:, c8 * 512:(c8 + 1) * 512]: e.bn_stats(out=o, in_=i),
                          reads=[r_xt[b]], writes=[r_mv[b]] if c8 == 0 else (), swrites=() if c8 == 0 else [r_mv[b]])
                P.add('dve', lambda e, o=mv[b][:, 0:2], i=st[b][:].rearrange("p a b -> p (a b)"): e.bn_aggr(out=o, in_=i),
                      reads=[r_mv[b]], writes=[r_mv[b]])
                self.rstd_from(mv[b][:, 2:3], mv[b][:, 1:2], r_mv[b])
                self.ts('dve', xt[b][:], xt[b][:], mv[b][:, 0:1], mv[b][:, 2:3], ALU.subtract, ALU.mult,
                        [r_mv[b], r_xt[b]], [r_xt[b]])
                for kq in range(8):
                    pb = (tb_i % 4)
                    tb_i += 1
                    for j in range(4):
                        kc = kq * 4 + j
                        self.tr(bank(pb)[:, j * 128:(j + 1) * 128], xt[b][:, kc * 128:(kc + 1) * 128], ident[:],
                                [r_xt[b], r_const], r_bank[pb], j == 0)
                    for j in range(4):
                        kc = kq * 4 + j
                        o = hT[:, kc, s * 128:(s + 1) * 128]
                        i_ = bank(pb)[:, j * 128:(j + 1) * 128]
                        if kq % 2 == 0:
                            self.act(o, i_, AF.Identity, [r_bank[pb], r_tab], (), [r_hT[s]],
                                     scale=mod[:, cv, 1, kc:kc + 1], bias=mod[:, cv, 0, kc:kc + 1])
                        else:
                            self.ts('dve', o, i_, mod[:, cv, 1, kc:kc + 1], mod[:, cv, 0, kc:kc + 1],
                                    ALU.mult, ALU.add, [r_bank[pb], r_tab], (), [r_hT[s]])
                        ev_i += 1
            P.barrier()
            if self.stop_after == 'B%d' % l:
                self.stopped = True
                return
            A.reset()
            sb16 = [A([128, 512], BF16, 'sb16_%d' % i) for i in range(2)]
            sb16b = [A([128, 512], BF16, 'sb16b_%d' % i) for i in range(2)]
            sf32 = [A([128, 512], F32, 'sf32_%d' % i) for i in range(2)]
            of32 = [A([128, 512], F32, 'of32_0')] * 2
            rt1 = [A([128, 512], F32, 'rt1_%d' % i) for i in range(2)]
            rt2 = [A([128, 512], F32, 'rt2_%d' % i) for i in range(2)]
            trs = [A([128, 1024], BF16, 'trs%d' % i) for i in range(2)]
            r_sb16 = [P.res() for i in range(2)]
            r_sb16b = [P.res() for i in range(2)]
            r_sf32 = [P.res() for i in range(2)]
            r_of32 = [P.res()] * 2
            r_rt = [P.res() for i in range(2)]
            r_trs = [P.res() for i in range(2)]
            dstg0 = A([128, 2, 512], F32, 'dstg')
            dtmp0 = A([128, 2, 512], F32, 'dtmp')
            dstg, dtmp = [dstg0, dstg0], [dtmp0, dtmp0]
            r_dstg0, r_dtmp0 = P.res(), P.res()
            r_dstg, r_dtmp = [r_dstg0, r_dstg0], [r_dtmp0, r_dtmp0]
            self.alloc_stg()

            nunits = 24 + 8
            wslot = [0]

            def load_w(ui, ws):
                if ui < 24:
                    return self.w_pieces(wbuf[ws][:], r_wbuf[ws], w_tok[:, :, ui * 512:(ui + 1) * 512])
                j = ui - 24
                srcs = [w_tok[:, :, (12 + q) * GW + j * 128:(12 + q) * GW + (j + 1) * 128] for q in range(4)]
                return self.w_pieces_d(wbuf[ws][:], r_wbuf[ws], srcs)

            self.drain(load_w(0, 0))
            pending = []
            acc_i = 0
            u_i = 0
            for ui in range(nunits):
                ws = ui % 2
                wgen = load_w(ui + 1, (ui + 1) % 2) if ui + 1 < nunits else None
                if ui < 24:
                    cc = ui
                    part, half = cc // 2, cc % 2
                    for s, tt in enumerate(grp):
                        pb = acc_i % 4
                        acc_i += 1
                        for kc in range(32):
                            self.mm(bank(pb), hT[:, kc, s * 128:(s + 1) * 128], wbuf[ws][:, kc, :], kc == 0, kc == 31,
                                    [r_hT[s], r_wbuf[ws]], r_bank[pb])
                        self.drain(wgen, 2)
                        for f in pending:
                            f()
                        pending = []
                        k = u_i % 2
                        u_i += 1
                        is_s = tt < 16
                        tok0 = tt * 128
                        ps = bank(pb)
                        cols = slice(half * 512, (half + 1) * 512)
                        if not is_s:
                            seq = (tt - 16) // 2
                            ptok = ((tt - 16) % 2) * 128

                        def rope(dst, k=k, ps=ps, pb=pb, tt=tt):
                            c2 = cos2[:, tt, :].unsqueeze(1).broadcast_to([128, 4, 128])
                            psv = ps.rearrange("p (b c) -> p b c", c=128)
                            self.tt('dve', rt1[k][:].rearrange("p (b c) -> p b c", c=128), psv, c2, ALU.mult,
                                    [r_bank[pb], r_tab], [r_rt[k]])
                            ps5 = ps.rearrange("p (b a x f) -> p b a x f", a=2, x=2, f=32)
                            r25 = rt2[k][:].rearrange("p (b a x f) -> p b a x f", a=2, x=2, f=32)
                            s25 = sin2[:, tt, :].rearrange("p (a x f) -> p a x f", a=2, x=2)
                            for x in range(2):
                                for a in range(2):
                                    self.tt('dve', r25[:, :, a, x, :], ps5[:, :, a, 1 - x, :],
                                            s25[:, a, x, :].unsqueeze(1).broadcast_to([128, 4, 32]), ALU.mult,
                                            [r_bank[pb], r_tab], (), [r_rt[k]])
                            return rt1[k], rt2[k]

                        def transposes(srcs, dsts, k=k, tok0=tok0):
                            def f():
                                tb = 4 + (self._tb % 4)
                                self._tb += 1
                                n = 0
                                for (src, rs) in srcs:
                                    for j in range(4):
                                        self.tr(bank_bf(tb)[:, n * 128:(n + 1) * 128], src[:, j * 128:(j + 1) * 128],
                                                identb[:], [rs, r_const], r_bank[tb], n == 0)
                                        n += 1
                                self.cp('dve', trs[k][:, 0:n * 128], bank_bf(tb)[:, 0:n * 128], [r_bank[tb]], [r_trs[k]])
                                for i, dst in enumerate(dsts):
                                    P.dma('sp', dst.rearrange("h d t -> d h t"),
                                          trs[k][:, i * 512:(i + 1) * 512].rearrange("p (h t) -> p h t", h=4),
                                          reads=[r_trs[k]], swrites=[r_scr['Q']])
                            return f

                        if part in (0, 1):
                            self.cp('act', sb16[k][:], ps, [r_bank[pb]], [r_sb16[k]])
                            if part == 1 and not is_s:
                                self.cp('dve', of32[k][:], ps, [r_bank[pb]], [r_of32[k]])
                                P.dma('sp', d['o_nak'][seq, l, ptok:ptok + 128, cols], of32[k][:],
                                      reads=[r_of32[k]], swrites=[r_scr['OUT']])
                            dst = d['QTA' if part == 0 else 'KTA'][half * 4:half * 4 + 4, :, tok0:tok0 + 128]
                            pending.append(transposes([(sb16[k], r_sb16[k])], [dst]))
                        elif part in (2, 6, 10):
                            self.cp('act', sb16[k][:], ps, [r_bank[pb]], [r_sb16[k]])
                            nm = {2: 'VA', 6: 'VB', 10: 'VC'}[part]
                            P.dma('sp', d[nm][tok0:tok0 + 128, cols], sb16[k][:], reads=[r_sb16[k]],
                                  swrites=[r_scr['V']])
                            if not is_s and part in (2, 6):
                                self.cp('dve', of32[k][:], ps, [r_bank[pb]], [r_of32[k]])
                                P.dma('sp', d['o_nav' if part == 2 else 'o_dfv'][seq, l, ptok:ptok + 128, cols],
                                      of32[k][:], reads=[r_of32[k]], swrites=[r_scr['OUT']])
                        elif part in (3, 7, 11):
                            self.act(sb16[k][:], ps, AF.Silu, [r_bank[pb]], [r_sb16[k]])
                            gi_ = {3: 0, 7: 1, 11: 2}[part]
                            P.dma('sp', d['GT'][tok0:tok0 + 128, gi_ * GW + half * 512: gi_ * GW + (half + 1) * 512],
                                  sb16[k][:], reads=[r_sb16[k]], swrites=[r_scr['G']])
                        elif part in (4, 5):
                            if is_s:
                                a1, a2 = rope(None)
                                self.tt('dve', sb16[k][:], a1[:], a2[:], ALU.add, [r_rt[k]], [r_sb16[k]])
                            else:
                                self.cp('act', sb16[k][:], ps, [r_bank[pb]], [r_sb16[k]])
                                if part == 5:
                                    self.cp('dve', of32[k][:], ps, [r_bank[pb]], [r_of32[k]])
                                    P.dma('sp', d['o_dfk'][seq, l, ptok:ptok + 128, cols], of32[k][:],
                                          reads=[r_of32[k]], swrites=[r_scr['OUT']])
                            dst = d['QTB' if part == 4 else 'KTB'][half * 4:half * 4 + 4, :, tok0:tok0 + 128]
                            pending.append(transposes([(sb16[k], r_sb16[k])], [dst]))
                        elif part in (8, 9):
                            if is_s:
                                a1, a2 = rope(None)
                                self.tt('dve', sf32[k][:], a1[:], a2[:], ALU.add, [r_rt[k]], [r_sf32[k]])
                            else:
                                self.cp('act', sf32[k][:], ps, [r_bank[pb]], [r_sf32[k]])
                            base = 0 if part == 8 else 2
                            sfv = sf32[k][:].rearrange("p (h c) -> p h c", c=128)
                            for di, (dstt, rdst) in enumerate(((sb16[k], r_sb16[k]), (sb16b[k], r_sb16b[k]))):
                                dc = dec[:, base + di, half * 4:half * 4 + 4].unsqueeze(2).broadcast_to([128, 4, 128])
                                self.tt('dve', dstt[:].rearrange("p (h c) -> p h c", c=128), sfv, dc, ALU.mult,
                                        [r_sf32[k], r_tab], [rdst])
                            if part == 9:
                                P.dma('sp', d['KF'][tok0:tok0 + 128, cols], sb16[k][:], reads=[r_sb16[k]],
                                      swrites=[r_scr['V']])
                                P.dma('sp', d['KB'][tok0:tok0 + 128, cols], sb16b[k][:], reads=[r_sb16b[k]],
                                      swrites=[r_scr['V']])
                            n1, n2 = ('QFT', 'QBT') if part == 8 else ('KFT', 'KBT')
                            dst1 = d[n1][half * 4:half * 4 + 4, :, tok0:tok0 + 128]
                            dst2 = d[n2][half * 4:half * 4 + 4, :, tok0:tok0 + 128]
                            pending.append(transposes([(sb16[k], r_sb16[k]), (sb16b[k], r_sb16b[k])], [dst1, dst2]))
                    self.drain(wgen)
                else:
                    j = ui - 24
                    for f in pending:
                        f()
                    pending = []
                    wv = wbuf[ws][:].rearrange("p k (q c) -> p k q c", q=4)
                    chunks = [(0, 512), (512, 1024), (1024, 1280)]
                    for ci, (t0, t1) in enumerate(chunks):
                        n = t1 - t0
                        par = (j * 3 + ci) % 2
                        for q in range(4):
                            pb = par * 4 + q
                            for kc in range(32):
                                self.mm(bank(pb)[:, 0:n], wv[:, kc, q, :], hT[:, kc, t0:t1], kc == 0, kc == 31,
                                        [r_wbuf[ws]] + r_hT[t0 // 128:t1 // 128], r_bank[pb])
                        self.drain(wgen, 6)
                        k = par
                        b0 = par * 4
                        tt0 = grp[t0 // 128]
                        g0 = tt0 * 128
                        self.act(dtmp[k][:, 0, 0:n], bank(b0 + 3)[:, 0:n], AF.Silu, [r_bank[b0 + 3]], [r_dtmp[k]])
                        self.cp('act', dtmp[k][:, 1, 0:n], bank(b0 + 2)[:, 0:n], [r_bank[b0 + 2]], (), [r_dtmp[k]])
                        self.tt('dve', dstg[k][:, 0, 0:n], bank(b0 + 1)[:, 0:n], dtmp[k][:, 0, 0:n], ALU.mult,
                                [r_bank[b0 + 1], r_dtmp[k]], [r_dstg[k]])
                        self.tt('dve', dstg[k][:, 1, 0:n], bank(b0)[:, 0:n], dtmp[k][:, 1, 0:n], ALU.mult,
                                [r_bank[b0], r_dtmp[k]], (), [r_dstg[k]])
                        P.dma('sp', d['GDT'][j * 128:(j + 1) * 128, g0:g0 + n], dstg[k][:, 0, 0:n],
                              reads=[r_dstg[k]], swrites=[r_scr['UD']])
                        P.dma('sp', d['UT'][j * 128:(j + 1) * 128, g0:g0 + n], dstg[k][:, 1, 0:n],
                              reads=[r_dstg[k]], swrites=[r_scr['UD']])
                    self.drain(wgen)
            for f in pending:
                f()
            pending = []
            P.barrier()

    _tb = 0
    def phase_mixers(self, l):
        self.mixer_A(l)
        self.P.barrier()
        self.mixer_B(l)
        self.P.barrier()
        self.mixer_C(l)
        self.P.barrier()
        self.mixer_D(l)

    def mixer_A(self, l):
        nc, P, A = self.nc, self.P, self.A
        d = self.d
        bank, bank_bf, r_bank = self.bank, self.bank_bf, self.r_bank
        identb, r_const = self.identb, self.r_const
        A.reset()
        r_y = P.res('Yw')
        pr = A([120, 128], F32, 'pr'); r_pr = P.res('pr')
        self.memset('dve', pr[:], 0.0, writes=[r_pr])
        P.dma('sp', pr[:, 48:79], d['na_bias'][l], writes=[r_pr])
        r_pb = P.res('pbrep')
        P.dma('sp', d['PBREP'][:, :, :], pr[:].unsqueeze(1).broadcast_to([120, 64, 128]), reads=[r_pr], writes=[r_pb])
        masks = A([128, 5, 640], F32, 'masks'); r_masks = P.res('masks')
        P.dma('sp', masks[:], d['k_namask'].rearrange("t p k -> p t k"), writes=[r_masks])
        NB = 2
        qT = [A([128, LS], BF16, 'qT%d' % i) for i in range(NB)]
        kT = [A([128, LS], BF16, 'kT%d' % i) for i in range(NB)]
        V = [A([128, 16, 128], BF16, 'V%d' % i) for i in range(NB)]
        ckl = [A([128, 4, 128], BF16, 'ckl%d' % i) for i in range(NB)]
        ckT = [A([128, 512], BF16, 'ckT%d' % i) for i in range(NB)]
        cV = [A([128, 4, 128], BF16, 'cV%d' % i) for i in range(NB)]
        gate = [A([128, 16, 128], BF16, 'gate%d' % i) for i in range(NB)]
        TB2 = [A([128, 15, 64], F32, 'TB2_%d' % i) for i in range(NB)]
        BT = [A([128, 5, 640], F32, 'BT%d' % i) for i in range(NB)]
        yst = [A([128, 16, 128], BF16, 'yst%d' % i) for i in range(NB)]
        r_h = [P.res('hA%d' % i) for i in range(NB)]
        r_ckl = [P.res() for i in range(NB)]
        r_ckT = [P.res() for i in range(NB)]
        r_TB2 = [P.res() for i in range(NB)]
        r_BT = [P.res() for i in range(NB)]
        r_yst = [P.res() for i in range(NB)]
        sc = [A([128, 1152], F32, 'sc%d' % i) for i in range(2)]
        pbf = [A([128, 1152], BF16, 'pbf%d' % i) for i in range(2)]
        PT = [A([128, 9, 128], BF16, 'PT%d' % i) for i in range(2)]
        stt_ = [A([128, 4], F32, 'st%d' % i) for i in range(2)]
        r_sc = [P.res() for i in range(2)]
        r_pbf = [P.res() for i in range(2)]
        r_PT = [P.res() for i in range(2)]
        r_st = [P.res() for i in range(2)]
        r_tail = [P.res('tail', bank=4 + i) for i in range(2)]
        r_t9 = [P.res('t9', bank=4 + i) for i in range(2)]
        r_o = [P.res('o', bank=4 + i) for i in range(2)]
        self._u = 0

        def unit(qTb, rq, loc, nloc, bias, rbias, ctx, rctx, vlist, out_dst, r_out, gate_ap, rgate, first_out):
            u = self._u % 2
            self._u += 1
            bA, bC, bM, bT = 0 + u, 2 + u, 4 + u, 6 + u
            n1 = min(512, nloc)
            tot = nloc + (512 if ctx is not None else 0)
            nblk = tot // 128
            self.mm(bank(bA)[:, 0:n1], qTb, loc[:, 0:n1], True, True, [rq], r_bank[bA])
            if nloc > 512:
                self.mm(bank(bM)[:, 0:nloc - 512], qTb, loc[:, 512:nloc], True, True, [rq], r_tail[u])
            if ctx is not None:
                self.mm(bank(bC)[:, 0:512], qTb, ctx, True, True, [rq, rctx], r_bank[bC])
            if bias is not None:
                self.stt(sc[u][:, 0:n1], bank(bA)[:, 0:n1], SCALE, bias[:, 0:n1], ALU.mult, ALU.add,
                         [r_bank[bA], rbias], [r_sc[u]])
                self.stt(sc[u][:, 512:nloc], bank(bM)[:, 0:nloc - 512], SCALE, bias[:, 512:nloc], ALU.mult, ALU.add,
                         [r_tail[u], rbias], (), [r_sc[u]])
            else:
                P.add('act', lambda e, o=sc[u][:, 0:n1], i=bank(bA)[:, 0:n1]: e.mul(o, i, SCALE),
                      [r_bank[bA]], [r_sc[u]])
            if ctx is not None:
                P.add('act', lambda e, o=sc[u][:, nloc:tot], i=bank(bC)[:, 0:512]: e.mul(o, i, SCALE),
                      [r_bank[bC]], (), [r_sc[u]])
            self.memset('dve', stt_[u][:], 0.0, writes=[r_st[u]])
            P.add('dve', lambda e, o=stt_[u][:, 0:1], i=sc[u][:, 0:tot]: e.reduce_max(out=o, in_=i, axis=AX.X),
                  [r_sc[u]], (), [r_st[u]])
            self.ts('dve', stt_[u][:, 1:2], stt_[u][:, 0:1], -1.0, None, ALU.mult, None, [r_st[u]], (), [r_st[u]])
            self.act(pbf[u][:, 0:tot], sc[u][:, 0:tot], AF.Exp, [r_sc[u], r_st[u]], [r_pbf[u]],
                     bias=stt_[u][:, 1:2], scale=1.0, accum_out=stt_[u][:, 2:3])
            P.add('dve', lambda e, o=stt_[u][:, 3:4], i=stt_[u][:, 2:3]: e.reciprocal(out=o, in_=i),
                  [r_pbf[u], r_st[u]], (), [r_st[u]])
            nb8 = min(8, nblk)

            def back():
                for b in range(nb8):
                    self.tr(bank_bf(bT)[:, b * 128:(b + 1) * 128], pbf[u][:, b * 128:(b + 1) * 128], identb[:],
                            [r_pbf[u], r_const], r_bank[bT], b == 0)
                if nblk == 9:
                    self.tr(bank_bf(bM, 256, 384), pbf[u][:, 1024:1152], identb[:], [r_pbf[u], r_const], r_t9[u], True)
                self.cp('act', PT[u][:, 0:nb8, :].rearrange("p b q -> p (b q)"), bank_bf(bT)[:, 0:nb8 * 128],
                        [r_bank[bT]], [r_PT[u]])
                if nblk == 9:
                    self.cp('dve', PT[u][:, 8, :], bank_bf(bM, 256, 384), [r_t9[u]], (), [r_PT[u]])
                for b in range(nblk):
                    self.mm(bank(bM)[:, 256:384], PT[u][:, b, :], vlist[b][0], b == 0, b == nblk - 1,
                            [r_PT[u], vlist[b][1]], r_o[u])
                self.stt(out_dst, bank(bM)[:, 256:384], stt_[u][:, 3:4], gate_ap, ALU.mult, ALU.mult,
                         [r_o[u], r_st[u], rgate], [r_out] if first_out else (), () if first_out else [r_out])
            return back

        cl = l
        for h in range(8):
            n = h % NB
            hs = slice(h * 128, (h + 1) * 128)
            P.dma('sp', qT[n][:], d['QTA'][h, :, 0:LS], writes=[r_h[n]])
            P.dma('sp', kT[n][:], d['KTA'][h, :, 0:LS], swrites=[r_h[n]])
            P.dma('sp', V[n][:], d['VA'][0:LS, hs].rearrange("(t p) e -> p t e", p=128), swrites=[r_h[n]])
            P.dma('sp', gate[n][:], d['GT'][0:LS, hs].rearrange("(t p) e -> p t e", p=128), swrites=[r_h[n]])
            P.dma('pool', cV[n][:], d['c_na_v'][cl, :, hs].rearrange("(t p) e -> p t e", p=128), swrites=[r_h[n]])
            P.dma('pool', ckl[n][:], d['c_na_k'][cl, :, hs].rearrange("(t p) e -> p t e", p=128), writes=[r_ckl[n]])
            bT = 6 + (h % 2)
            for t in range(4):
                self.tr(bank_bf(bT)[:, t * 128:(t + 1) * 128], ckl[n][:, t, :], identb[:], [r_ckl[n], r_const],
                        r_bank[bT], t == 0)
            self.cp('dve', ckT[n][:], bank_bf(bT)[:, 0:512], [r_bank[bT]], [r_ckT[n]])
            for half in range(2):
                src = bass.AP(tensor=d['PBREP'].tensor, offset=(h * 15) * 8192 + 63,
                              ap=[[127, 64], [8192, 15], [1, 64]])
                if half == 0:
                    P.dma('sp', TB2[n][0:64, :, :], src, reads=[r_pb], writes=[r_TB2[n]])
                else:
                    P.dma('sp', TB2[n][64:128, :, :], src, reads=[r_pb], swrites=[r_TB2[n]])
            self.cp('act', BT[n][:], masks[:], [r_masks], [r_BT[n]])
            for ty, qrel0 in enumerate((0, 2, 4, 6, 8)):
                for half in range(2):
                    qr = qrel0 + half
                    k0, k1 = max(0, qr - 7), min(9, qr + 7)
                    d0 = k0 - qr + 7
                    nk = k1 - k0 + 1
                    ps_ = slice(half * 64, (half + 1) * 64)
                    o = BT[n][ps_, ty, k0 * 64:(k1 + 1) * 64]
                    self.tt('dve', o, o, TB2[n][ps_, d0:d0 + nk, :].rearrange("p a b -> p (a b)"), ALU.add,
                            [r_TB2[n], r_BT[n]], (), [r_BT[n]])
            pend = None
            for qb in range(16):
                tw = min(max(qb - 2, 0), 11)
                ty = {0: 0, 1: 1, 14: 3, 15: 4}.get(qb, 2)
                vlist = [(V[n][:, tw + b, :], r_h[n]) for b in range(5)] + [(cV[n][:, b, :], r_h[n]) for b in range(4)]
                bk = unit(qT[n][:, qb * 128:(qb + 1) * 128], r_h[n], kT[n][:, tw * 128:tw * 128 + 640], 640,
                          BT[n][:, ty, :], r_BT[n], ckT[n][:], r_ckT[n], vlist, yst[n][:, qb, :], r_yst[n],
                          gate[n][:, qb, :], r_h[n], qb == 0)
                if pend is not None:
                    pend()
                pend = bk
            pend()
            P.dma('sp', d['Y'][0:LS, hs].rearrange("(t p) e -> p t e", p=128), yst[n][:], reads=[r_yst[n]],
                  swrites=[r_y])
        for sq in range(2):
            t0 = LS + sq * LP
            for h in range(8):
                n = h % NB
                hs = slice(h * 128, (h + 1) * 128)
                P.dma('sp', qT[n][:, 0:LP], d['QTA'][h, :, t0:t0 + LP], writes=[r_h[n]])
                P.dma('sp', kT[n][:, 0:LP], d['KTA'][h, :, t0:t0 + LP], swrites=[r_h[n]])
                P.dma('sp', V[n][:, 0:2, :], d['VA'][t0:t0 + LP, hs].rearrange("(t p) e -> p t e", p=128),
                      swrites=[r_h[n]])
                P.dma('sp', gate[n][:, 0:2, :], d['GT'][t0:t0 + LP, hs].rearrange("(t p) e -> p t e", p=128),
                      swrites=[r_h[n]])
                pend = None
                for qb in range(2):
                    vlist = [(V[n][:, b, :], r_h[n]) for b in range(2)]
                    bk = unit(qT[n][:, qb * 128:(qb + 1) * 128], r_h[n], kT[n][:, 0:LP], LP, None, None, None, None,
                              vlist, yst[n][:, qb, :], r_yst[n], gate[n][:, qb, :], r_h[n], qb == 0)
                    if pend is not None:
                        pend()
                    pend = bk
                pend()
                P.dma('sp', d['Y'][t0:t0 + LP, hs].rearrange("(t p) e -> p t e", p=128), yst[n][:, 0:2, :],
                      reads=[r_yst[n]], swrites=[r_y])

    def mixer_B(self, l):
        nc, P, A = self.nc, self.P, self.A
        d = self.d
        bank, bank_bf, r_bank = self.bank, self.bank_bf, self.r_bank
        identb, r_const = self.identb, self.r_const
        lam_init = 0.8 - 0.6 * math.exp(-0.3 * l)
        A.reset()
        r_y = P.res('Yw')
        lt = A([128, 4, 128], F32, 'lt'); r_lt = P.res('lt')
        P.dma('sp', lt[:].rearrange("p a b -> p (a b)"), d['diff_lam'][l:l + 1, :].broadcast_to([128, 512]),
              writes=[r_lt])
        lm = A([128, 2, 128], F32, 'lm')
        self.tt('dve', lm[:, 0, :], lt[:, 0, :], lt[:, 1, :], ALU.mult, [r_lt], [r_lt])
        self.tt('dve', lm[:, 1, :], lt[:, 2, :], lt[:, 3, :], ALU.mult, [r_lt], [r_lt])
        lam = A([128, 4], F32, 'lam')
        P.add('dve', lambda e: e.reduce_sum(out=lam[:, 0:2], in_=lm[:], axis=AX.X), [r_lt], [r_lt])
        self.act(lam[:, 0:2], lam[:, 0:2], AF.Exp, [r_lt], [r_lt])
        self.tt('dve', lam[:, 2:3], lam[:, 0:1], lam[:, 1:2], ALU.subtract, [r_lt], [r_lt])
        self.ts('dve', lam[:, 2:3], lam[:, 2:3], lam_init, None, ALU.add, None, [r_lt], [r_lt])
        wsub = A([128, 256], F32, 'wsub')
        P.dma('sp', wsub[:], d['diff_subln'][l:l + 1, :].broadcast_to([128, 256]), writes=[r_lt])
        self.ts('dve', wsub[:], wsub[:], 1.0 - lam_init, None, ALU.mult, None, [r_lt], [r_lt])

        NKB = (PAST + LS) // 128
        qT = [A([128, LS], BF16, 'bqT%d' % t) for t in range(2)]
        kT = [A([128, PAST + LS], BF16, 'bkT%d' % t) for t in range(2)]
        V = A([128, NKB, 256], BF16, 'bV')
        ckl = A([128, 4, 256], BF16, 'bckl')
        gate = A([128, 16, 256], BF16, 'bgate')
        yst = A([128, 16, 256], BF16, 'byst')
        r_h = P.res('hB'); r_ckl = P.res(); r_yst = P.res()
        exs = [[A([128, PAST + LS], F32, 'ex%d_%d' % (t, i)) for t in range(2)] for i in range(2)]
        r_exs = [[P.res() for t in range(2)] for i in range(2)]
        abfs = [A([128, PAST + LS], BF16, 'abf%d' % i) for i in range(2)]; r_abfs = [P.res() for i in range(2)]
        aTs = [A([128, NKB, 128], BF16, 'aT%d' % i) for i in range(2)]; r_aTs = [P.res() for i in range(2)]
        stt_ = [A([128, 32], F32, 'bst%d' % i) for i in range(2)]
        r_st = [P.res() for i in range(2)]
        otmps = [A([128, 256], F32, 'otmp%d' % i) for i in range(2)]; r_otmps = [P.res() for i in range(2)]
        junks = [A([128, 256], F32, 'junk%d' % i) for i in range(2)]
        r_o = P.res('bo')
        self._sb = 0
        self._ub = 0
        self._tbb = 0

        def unit(Lk, qcols, yout, first_out, gate_ap):
            u = self._ub % 2
            self._ub += 1
            st = stt_[u]
            ex, r_ex, abf, r_abf, aT, r_aT = exs[u], r_exs[u], abfs[u], r_abfs[u], aTs[u], r_aTs[u]
            otmp, r_otmp, junk = otmps[u], r_otmps[u], junks[u]
            nkb = Lk // 128
            nch = (Lk + 511) // 512
            self.memset('dve', st[:], 0.0, writes=[r_st[u]])
            chunks = []
            for t in range(2):
                for c in range(nch):
                    w = min(512, Lk - c * 512)
                    b = self._sb % 6
                    self._sb += 1
                    self.mm(bank(b)[:, 0:w], qT[t][:, qcols], kT[t][:, c * 512:c * 512 + w], True, True,
                            [r_h], r_bank[b])
                    P.add('dve', lambda e, o=st[:, t * 5 + c:t * 5 + c + 1], i=bank(b)[:, 0:w]:
                          e.reduce_max(out=o, in_=i, axis=AX.X), [r_bank[b]], (), [r_st[u]])
                    chunks.append((t, c, w, b))
                P.add('dve', lambda e, o=st[:, 10 + t:11 + t], i=st[:, t * 5:t * 5 + nch]:
                      e.reduce_max(out=o, in_=i, axis=AX.X), [r_st[u]], (), [r_st[u]])
                self.ts('dve', st[:, 12 + t:13 + t], st[:, 10 + t:11 + t], -SCALE, None, ALU.mult, None,
                        [r_st[u]], (), [r_st[u]])
                for (t_, c, w, b) in chunks[-nch:]:
                    self.act(ex[t][:, c * 512:c * 512 + w], bank(b)[:, 0:w], AF.Exp, [r_bank[b], r_st[u]],
                             [r_ex[t]] if c == 0 else (), () if c == 0 else [r_ex[t]],
                             scale=SCALE, bias=st[:, 12 + t:13 + t], accum_out=st[:, 14 + t * 5 + c:15 + t * 5 + c])
                P.add('dve', lambda e, o=st[:, 24 + t:25 + t], i=st[:, 14 + t * 5:14 + t * 5 + nch]:
                      e.reduce_sum(out=o, in_=i, axis=AX.X), [r_st[u], r_ex[t]], (), [r_st[u]])
            P.add('dve', lambda e, o=st[:, 26:28], i=st[:, 24:26]: e.reciprocal(out=o, in_=i), [r_st[u]], (), [r_st[u]])
            self.tt('dve', st[:, 28:29], st[:, 27:28], lam[:, 2:3], ALU.mult, [r_st[u], r_lt], (), [r_st[u]])
            self.act(ex[1][:, 0:Lk], ex[1][:, 0:Lk], AF.Identity, [r_ex[1], r_st[u]], [r_ex[1]], scale=st[:, 28:29])
            self.stt(abf[:, 0:Lk], ex[0][:, 0:Lk], st[:, 26:27], ex[1][:, 0:Lk], ALU.mult, ALU.subtract,
                     [r_ex[0], r_ex[1], r_st[u]], [r_abf])
            def back():
                self._b_back(Lk, nkb, u, st, abf, r_abf, aT, r_aT, otmp, r_otmp, junk, V, r_h, wsub, r_lt, r_st,
                             yout, first_out, gate_ap, r_yst)
            return back

        def _unused():
            kb = 0
            first = True
            while kb < nkb:
                nb = min(8, nkb - kb)
                bT = 6 + (self._tbb % 2)
                self._tbb += 1
                for j in range(nb):
                    self.tr(bank_bf(bT)[:, j * 128:(j + 1) * 128], abf[:, (kb + j) * 128:(kb + j + 1) * 128], identb[:],
                            [r_abf, r_const], r_bank[bT], j == 0)
                self.cp('act' if (self._tbb % 2) else 'dve', aT[:, kb:kb + nb, :].rearrange("p b q -> p (b q)"),
                        bank_bf(bT)[:, 0:nb * 128], [r_bank[bT]], [r_aT] if first else (), () if first else [r_aT])
                first = False
                kb += nb
            bo = self._sb % 6
            self._sb += 1
            for b_ in range(nkb):
                self.mm(bank(bo)[:, 0:256], aT[:, b_, :], V[:, b_, :], b_ == 0, b_ == nkb - 1, [r_aT, r_h], r_bank[bo])
            self.act(junk[:], bank(bo)[:, 0:256], AF.Square, [r_bank[bo]], [r_otmp], accum_out=st[:, 29:30])
            self.ts('dve', st[:, 30:31], st[:, 29:30], 1.0 / 256.0, EPS, ALU.mult, ALU.add, [r_st[u], r_otmp], (), [r_st[u]])
            self.act(st[:, 30:31], st[:, 30:31], AF.Ln, [r_st[u]], (), [r_st[u]])
            self.act(st[:, 30:31], st[:, 30:31], AF.Exp, [r_st[u]], (), [r_st[u]], scale=-0.5)
            self.stt(otmp[:], bank(bo)[:, 0:256], st[:, 30:31], wsub[:], ALU.mult, ALU.mult,
                     [r_bank[bo], r_st[u], r_lt], [r_otmp])
            self.tt('dve', yout, otmp[:], gate_ap, ALU.mult, [r_otmp, r_h], [r_yst] if first_out else (),
                    () if first_out else [r_yst])

        for h in range(4):
            vs = slice(h * 256, (h + 1) * 256)
            P.dma('pool', ckl[:], d['c_df_k'][l, :, vs].rearrange("(t p) e -> p t e", p=128), writes=[r_ckl])
            first = True
            for t in range(2):
                P.dma('sp', qT[t][:], d['QTB'][h * 2 + t, :, 0:LS], writes=[r_h] if first else (),
                      swrites=() if first else [r_h])
                first = False
                P.dma('sp', kT[t][:, PAST:PAST + LS], d['KTB'][h * 2 + t, :, 0:LS], swrites=[r_h])
                bT = 6 + (t % 2)
                for tb in range(4):
                    self.tr(bank_bf(bT)[:, tb * 128:(tb + 1) * 128], ckl[:, tb, t * 128:(t + 1) * 128], identb[:],
                            [r_ckl, r_const], r_bank[bT], tb == 0)
                self.cp('dve', kT[t][:, 0:PAST], bank_bf(bT)[:, 0:512], [r_bank[bT]], (), [r_h])
            P.dma('pool', V[:, 0:4, :], d['c_df_v'][l, :, vs].rearrange("(t p) e -> p t e", p=128), swrites=[r_h])
            P.dma('sp', V[:, 4:20, :], d['VB'][0:LS, vs].rearrange("(t p) e -> p t e", p=128), swrites=[r_h])
            P.dma('sp', gate[:], d['GT'][0:LS, GW + h * 256:GW + (h + 1) * 256].rearrange("(t p) e -> p t e", p=128),
                  swrites=[r_h])
            pend = None
            for qb in range(16):
                bk = unit(PAST + LS, slice(qb * 128, (qb + 1) * 128), yst[:, qb, :], qb == 0, gate[:, qb, :])
                if pend is not None:
                    pend()
                pend = bk
            pend()
            P.dma('sp', d['Y'][0:LS, GW + h * 256:GW + (h + 1) * 256].rearrange("(t p) e -> p t e", p=128), yst[:],
                  reads=[r_yst], swrites=[r_y])
            for sq in range(2):
                t0 = LS + sq * LP
                first = True
                for t in range(2):
                    P.dma('sp', qT[t][:, 0:LP], d['QTB'][h * 2 + t, :, t0:t0 + LP], writes=[r_h] if first else (),
                          swrites=() if first else [r_h])
                    first = False
                    P.dma('sp', kT[t][:, 0:LP], d['KTB'][h * 2 + t, :, t0:t0 + LP], swrites=[r_h])
                P.dma('sp', V[:, 0:2, :], d['VB'][t0:t0 + LP, vs].rearrange("(t p) e -> p t e", p=128), swrites=[r_h])
                P.dma('sp', gate[:, 0:2, :],
                      d['GT'][t0:t0 + LP, GW + h * 256:GW + (h + 1) * 256].rearrange("(t p) e -> p t e", p=128),
                      swrites=[r_h])
                pend = None
                for qb in range(2):
                    bk = unit(LP, slice(qb * 128, (qb + 1) * 128), yst[:, qb, :], qb == 0, gate[:, qb, :])
                    if pend is not None:
                        pend()
                    pend = bk
                pend()
                P.dma('sp', d['Y'][t0:t0 + LP, GW + h * 256:GW + (h + 1) * 256].rearrange("(t p) e -> p t e", p=128),
                      yst[:, 0:2, :], reads=[r_yst], swrites=[r_y])
    def _b_back(self, Lk, nkb, u, st, abf, r_abf, aT, r_aT, otmp, r_otmp, junk, V, r_h, wsub, r_lt, r_st,
                yout, first_out, gate_ap, r_yst):
        P = self.P
        bank, bank_bf, r_bank = self.bank, self.bank_bf, self.r_bank
        identb, r_const = self.identb, self.r_const
        kb = 0
        first = True
        while kb < nkb:
            nb = min(8, nkb - kb)
            bT = 6 + (self._tbb % 2)
            self._tbb += 1
            for j in range(nb):
                self.tr(bank_bf(bT)[:, j * 128:(j + 1) * 128], abf[:, (kb + j) * 128:(kb + j + 1) * 128], identb[:],
                        [r_abf, r_const], r_bank[bT], j == 0)
            self.cp('act' if (self._tbb % 2) else 'dve', aT[:, kb:kb + nb, :].rearrange("p b q -> p (b q)"),
                    bank_bf(bT)[:, 0:nb * 128], [r_bank[bT]], [r_aT] if first else (), () if first else [r_aT])
            first = False
            kb += nb
        bo = self._sb % 6
        self._sb += 1
        for b_ in range(nkb):
            self.mm(bank(bo)[:, 0:256], aT[:, b_, :], V[:, b_, :], b_ == 0, b_ == nkb - 1, [r_aT, r_h], r_bank[bo])
        self.act(junk[:], bank(bo)[:, 0:256], AF.Square, [r_bank[bo]], [r_otmp], accum_out=st[:, 29:30])
        self.ts('dve', st[:, 30:31], st[:, 29:30], 1.0 / 256.0, EPS, ALU.mult, ALU.add, [r_st[u], r_otmp], (), [r_st[u]])
        self.act(st[:, 30:31], st[:, 30:31], AF.Ln, [r_st[u]], (), [r_st[u]])
        self.act(st[:, 30:31], st[:, 30:31], AF.Exp, [r_st[u]], (), [r_st[u]], scale=-0.5)
        self.stt(otmp[:], bank(bo)[:, 0:256], st[:, 30:31], wsub[:], ALU.mult, ALU.mult,
                 [r_bank[bo], r_st[u], r_lt], [r_otmp])
        self.tt('dve', yout, otmp[:], gate_ap, ALU.mult, [r_otmp, r_h], [r_yst] if first_out else (),
                () if first_out else [r_yst])

    def mixer_C(self, l):
        nc, P, A = self.nc, self.P, self.A
        d = self.d
        bank, bank_bf, r_bank = self.bank, self.bank_bf, self.r_bank
        cdt, r_tab = self.tabs['cdt'], self.tabs['r_tab']
        A.reset()
        r_y = P.res('Yw')
        rmask = A([128, 2, 128], F32, 'rmask'); r_rm = P.res('rmask')
        P.dma('sp', rmask[:], d['k_rmask'].rearrange("a j i -> j a i"), writes=[r_rm])
        NS = 2
        names = ('QFT', 'QBT', 'KFT', 'KBT')
        fT = [[A([128, LS], BF16, 'c%s%d' % (nm, i)) for nm in names] for i in range(NS)]
        tk = [[A([128, 16, 128], BF16, 'c%s%d' % (nm, i)) for nm in ('KF', 'KB', 'VC')] for i in range(NS)]
        gate = [A([128, 16, 128], BF16, 'cg%d' % i) for i in range(NS)]
        oacc = [A([128, 16, 128], F32, 'oacc%d' % i) for i in range(NS)]
        tmp1 = [A([128, 16, 128], F32, 'ctmp%d' % i) for i in range(NS)]
        yst = [A([128, 16, 128], BF16, 'cy%d' % i) for i in range(NS)]
        S = [[A([128, 128], F32, 'S%d_%d' % (i, dr)) for dr in range(2)] for i in range(NS)]
        Sb = [[A([128, 128], BF16, 'Sb%d_%d' % (i, dr)) for dr in range(2)] for i in range(NS)]
        atm = [[A([128, 128], BF16, 'atm%d_%d' % (i, dr)) for dr in range(2)] for i in range(NS)]
        stat = [A([128, 4, 16], F32, 'cst%d' % i) for i in range(NS)]
        r_h = [P.res() for i in range(NS)]
        r_o = [P.res() for i in range(NS)]
        r_S = [[P.res() for dr in range(2)] for i in range(NS)]
        r_Sb = [[P.res() for dr in range(2)] for i in range(NS)]
        r_atm = [[P.res() for dr in range(2)] for i in range(NS)]
        r_stat = [P.res() for i in range(NS)]
        r_yst = [P.res() for i in range(NS)]
        r_pa = [[P.res('pa', bank=i * 2 + dr) for dr in range(2)] for i in range(NS)]
        r_po = [[P.res('po', bank=i * 2 + dr) for dr in range(2)] for i in range(NS)]
        r_pd = [[P.res('pd', bank=i * 2 + dr) for dr in range(2)] for i in range(NS)]

        seqs = [(0, LS, True, None)] + [(LS + sq * LP, LP, False, sq) for sq in range(2)]
        for (t0, L, is_s, sq) in seqs:
            ncn = L // 128
            for hg in range(0, 8, NS):
                for i in range(NS):
                    h = hg + i
                    hs = slice(h * 128, (h + 1) * 128)
                    first = True
                    for j, nm in enumerate(names):
                        P.dma('sp', fT[i][j][:, 0:L], d[nm][h, :, t0:t0 + L], writes=[r_h[i]] if first else (),
                              swrites=() if first else [r_h[i]])
                        first = False
                    for j, nm in enumerate(('KF', 'KB', 'VC')):
                        P.dma('sp', tk[i][j][:, 0:ncn, :], d[nm][t0:t0 + L, hs].rearrange("(t p) e -> p t e", p=128),
                              swrites=[r_h[i]])
                    P.dma('sp', gate[i][:, 0:ncn, :],
                          d['GT'][t0:t0 + L, 2 * GW + h * 128:2 * GW + (h + 1) * 128].rearrange("(t p) e -> p t e", p=128),
                          swrites=[r_h[i]])
                    for dr in range(2):
                        if is_s:
                            P.dma('sp', S[i][dr][:], d['st_ret'][l, dr, h], writes=[r_S[i][dr]])
                        else:
                            self.memset('dve', S[i][dr][:], 0.0, writes=[r_S[i][dr]])
                        self.cp('act', Sb[i][dr][:], S[i][dr][:], [r_S[i][dr]], [r_Sb[i][dr]])
                for step in range(ncn):
                    for i in range(NS):
                        h = hg + i
                        for dr in range(2):
                            c = step if dr == 0 else ncn - 1 - step
                            cs = slice(c * 128, (c + 1) * 128)
                            pb = i * 2 + dr
                            qTt, kTt = fT[i][dr], fT[i][2 + dr]
                            ktok, vtok = tk[i][dr], tk[i][2]
                            self.mm(bank(pb)[:, 0:128], kTt[:, cs], qTt[:, cs], True, True, [r_h[i]], r_pa[i][dr])
                            self.tt('dve', atm[i][dr][:], bank(pb)[:, 0:128], rmask[:, dr, :], ALU.mult,
                                    [r_pa[i][dr], r_rm], [r_atm[i][dr]])
                            self.mm(bank(pb)[:, 128:256], atm[i][dr][:], vtok[:, c, :], True, False,
                                    [r_atm[i][dr], r_h[i]], r_po[i][dr], first=True)
                            self.mm(bank(pb)[:, 128:256], qTt[:, cs], Sb[i][dr][:], False, True,
                                    [r_h[i], r_Sb[i][dr]], r_po[i][dr], first=False)
                            first_touch = (step < (ncn + 1) // 2) if ncn > 1 else (dr == 0)
                            if ncn % 2 == 1 and step == ncn // 2:
                                first_touch = (dr == 0)
                            if first_touch:
                                self.cp('act', oacc[i][:, c, :], bank(pb)[:, 128:256], [r_po[i][dr]], (), [r_o[i]])
                            else:
                                self.tt('dve', oacc[i][:, c, :], oacc[i][:, c, :], bank(pb)[:, 128:256], ALU.add,
                                        [r_po[i][dr], r_o[i]], [r_o[i]])
                            self.mm(bank(pb)[:, 256:384], ktok[:, c, :], vtok[:, c, :], True, True, [r_h[i]], r_pd[i][dr])
                            cd = cdt[:, dr * 8 + h:dr * 8 + h + 1]
                            self.ts('dve', S[i][dr][:], S[i][dr][:], cd, None, ALU.mult, None, [r_S[i][dr], r_tab],
                                    [r_S[i][dr]])
                            self.stt(S[i][dr][:], bank(pb)[:, 256:384], cd, S[i][dr][:], ALU.mult, ALU.add,
                                     [r_pd[i][dr], r_S[i][dr], r_tab], [r_S[i][dr]])
                            self.cp('act', Sb[i][dr][:], S[i][dr][:], [r_S[i][dr]], [r_Sb[i][dr]])
                for i in range(NS):
                    h = hg + i
                    o3 = oacc[i][:, 0:ncn, :]
                    t3 = tmp1[i][:, 0:ncn, :]
                    sm, sq2, mean, rstd = (stat[i][:, k, 0:ncn] for k in range(4))
                    P.add('dve', lambda e, o=sm, i_=o3: e.reduce_sum(out=o, in_=i_, axis=AX.X), [r_o[i]], [r_stat[i]])
                    self.act(t3, o3, AF.Square, [r_o[i]], [r_yst[i]])
                    P.add('dve', lambda e, o=sq2, i_=t3: e.reduce_sum(out=o, in_=i_, axis=AX.X), [r_yst[i]], (),
                          [r_stat[i]])
                    self.ts('dve', mean, sm, 1.0 / 128.0, None, ALU.mult, None, [r_stat[i]], [r_stat[i]])
                    self.tt('dve', sm, mean, mean, ALU.mult, [r_stat[i]], [r_stat[i]])
                    self.stt(sq2, sq2, 1.0 / 128.0, sm, ALU.mult, ALU.subtract, [r_stat[i]], [r_stat[i]])
                    self.rstd_from(rstd, sq2, r_stat[i])
                    self.tt('dve', t3, o3, mean.unsqueeze(2).broadcast_to([128, ncn, 128]), ALU.subtract,
                            [r_o[i], r_stat[i]], [r_yst[i]])
                    self.tt('dve', t3, t3, rstd.unsqueeze(2).broadcast_to([128, ncn, 128]), ALU.mult,
                            [r_yst[i], r_stat[i]], [r_yst[i]])
                    self.tt('dve', yst[i][:, 0:ncn, :], t3, gate[i][:, 0:ncn, :], ALU.mult, [r_yst[i], r_h[i]],
                            [r_yst[i]])
                    P.dma('sp', d['Y'][t0:t0 + L, 2 * GW + h * 128:2 * GW + (h + 1) * 128].rearrange("(t p) e -> p t e", p=128),
                          yst[i][:, 0:ncn, :], reads=[r_yst[i]], swrites=[r_y])
                    if not is_s:
                        for dr in range(2):
                            P.dma('sp', d['o_st'][sq, l, dr, h], S[i][dr][:], reads=[r_S[i][dr]],
                                  swrites=[self.r_scr['OUT']])

    def mixer_D(self, l):
        nc, P, A = self.nc, self.P, self.A
        d = self.d
        A.reset()
        r_y = P.res('Yw')
        NB = 2
        Lmax = LS
        u = [A([128, Lmax + 2], F32, 'du%d' % i) for i in range(NB)]
        gd = [A([128, Lmax], F32, 'dgd%d' % i) for i in range(NB)]
        tacc = [A([128, Lmax], F32, 'dt%d' % i) for i in range(NB)]
        yb = [A([128, Lmax], BF16, 'dy%d' % i) for i in range(NB)]
        cw = [A([128, 3], F32, 'cw%d' % i) for i in range(NB)]
        r_in = [P.res() for i in range(NB)]
        r_t = [P.res() for i in range(NB)]
        r_yb = [P.res() for i in range(NB)]
        n = 0
        for j in range(8):
            rows = slice(j * 128, (j + 1) * 128)
            for (t0, L) in ((0, LS), (LS, LP), (LS + LP, LP)):
                b = n % NB
                n += 1
                self.memset('dve', u[b][:, 0:1], 0.0, writes=[r_in[b]])
                self.memset('dve', u[b][:, L + 1:L + 2], 0.0, swrites=[r_in[b]])
                P.dma('sp', u[b][:, 1:L + 1], d['UT'][rows, t0:t0 + L], swrites=[r_in[b]])
                P.dma('sp', gd[b][:, 0:L], d['GDT'][rows, t0:t0 + L], swrites=[r_in[b]])
                P.dma('sp', cw[b][:], d['conv_w'][l, rows, :], swrites=[r_in[b]])
                self.act(tacc[b][:, 0:L], u[b][:, 0:L], AF.Identity, [r_in[b]], [r_t[b]], scale=cw[b][:, 0:1])
                self.stt(tacc[b][:, 0:L], u[b][:, 1:L + 1], cw[b][:, 1:2], tacc[b][:, 0:L], ALU.mult, ALU.add,
                         [r_in[b], r_t[b]], [r_t[b]])
                self.stt(tacc[b][:, 0:L], u[b][:, 2:L + 2], cw[b][:, 2:3], tacc[b][:, 0:L], ALU.mult, ALU.add,
                         [r_in[b], r_t[b]], [r_t[b]])
                self.tt('dve', yb[b][:, 0:L], tacc[b][:, 0:L], gd[b][:, 0:L], ALU.mult, [r_t[b], r_in[b]], [r_yb[b]])
                P.dma('sp', d['YTD'][rows, t0:t0 + L], yb[b][:, 0:L], reads=[r_yb[b]], swrites=[r_y])

    def phase_EF(self, l):
        nc, P, A = self.nc, self.P, self.A
        d = self.d
        bank, bank_bf, r_bank = self.bank, self.bank_bf, self.r_bank
        identb, r_const = self.identb, self.r_const
        A.off = self.base_mark
        r_z = P.res('Zw')
        gbc = A([128, 2, D], F32, 'gbc'); r_g = P.res('gbc')
        for cv in range(2):
            P.dma('sp', gbc[:, cv, :], d['MADA'][l, cv:cv + 1, 2 * D:3 * D].broadcast_to([128, D]),
                  writes=[r_g] if cv == 0 else (), swrites=() if cv == 0 else [r_g])
        yT = A([128, 32, 1280], BF16, 'yT')
        r_yT = [P.res() for s in range(10)]
        wbuf = [A([128, 32, 512], BF16, 'wo%d' % i) for i in range(2)]
        r_wbuf = [P.res() for i in range(2)]
        ytile = [A([128, 3 * GW], BF16, 'ytile0')] * 2
        r_ytile = [P.res()] * 2
        xch = [A([128, 512], F32, 'xch%d' % i) for i in range(2)] + [None]
        xch[2] = xch[0]
        r_xch = [P.res() for i in range(2)]
        r_xch.append(r_xch[0])
        zst = [A([128, 512], F32, 'zst%d' % i) for i in range(2)] + [None]
        zst[2] = zst[0]
        r_zst = [P.res() for i in range(2)]
        r_zst.append(r_zst[0])
        self.alloc_stg()
        w_o = d['w_out'][l].rearrange("(kc p) n -> p kc n", p=128)
        for gi, grp in enumerate(self.groups):
            tbi = 0
            for s, tt in enumerate(grp):
                b = s % 2
                tok0 = tt * 128
                P.dma('sp', ytile[b][:], d['Y'][tok0:tok0 + 128, :], writes=[r_ytile[b]])
                P.dma('sp', yT[:, 24:32, s * 128:(s + 1) * 128],
                      d['YTD'][:, tok0:tok0 + 128].rearrange("(j p) t -> p j t", p=128), swrites=[r_yT[s]])
                for k8 in range(3):
                    bT = 4 + (tbi % 4)
                    tbi += 1
                    for j in range(8):
                        kc = k8 * 8 + j
                        self.tr(bank_bf(bT)[:, j * 128:(j + 1) * 128], ytile[b][:, kc * 128:(kc + 1) * 128], identb[:],
                                [r_ytile[b], r_const], r_bank[bT], j == 0)
                    self.cp('act' if k8 % 2 == 0 else 'dve', yT[:, k8 * 8:(k8 + 1) * 8, s * 128:(s + 1) * 128],
                            bank_bf(bT)[:, 0:1024].rearrange("p (j t) -> p j t", j=8), [r_bank[bT]], (), [r_yT[s]])
            self.drain(self.w_pieces(wbuf[0][:], r_wbuf[0], w_o[:, :, 0:512]))
            acc_i = 0
            xi = 0
            for oc in range(8):
                ws = oc % 2
                wgen = None
                if oc + 1 < 8:
                    wgen = self.w_pieces(wbuf[(oc + 1) % 2][:], r_wbuf[(oc + 1) % 2],
                                         w_o[:, :, (oc + 1) * 512:(oc + 2) * 512])
                cols = slice(oc * 512, (oc + 1) * 512)
                for s, tt in enumerate(grp):
                    pb = acc_i % 4
                    acc_i += 1
                    k = xi % 2
                    xi += 1
                    cv = 0 if tt < 16 else 1
                    P.dma('sp', xch[k][:], self.x_src(l, tt, oc * 512, (oc + 1) * 512), writes=[r_xch[k]])
                    for kc in range(32):
                        self.mm(bank(pb), yT[:, kc, s * 128:(s + 1) * 128], wbuf[ws][:, kc, :], kc == 0, kc == 31,
                                [r_yT[s], r_wbuf[ws]], r_bank[pb])
                    self.tt('dve', zst[k][:], bank(pb), gbc[:, cv, cols], ALU.mult, [r_bank[pb], r_g], [r_zst[k]])
                    self.stt(zst[k][:], xch[k][:], ALPHA, zst[k][:], ALU.mult, ALU.add, [r_xch[k], r_zst[k]],
                             [r_zst[k]], eng='dve')
                    P.dma('sp', d['Z'][tt * 128:(tt + 1) * 128, cols], zst[k][:], reads=[r_zst[k]], swrites=[r_z])
                    self.drain(wgen, 2)
                self.drain(wgen)
            P.barrier()
        A.off = self.base_mark
        gb = A([128, 2, D], F32, 'lngb'); r_gb = P.res('lngb')
        P.dma('sp', gb[:, 0, :], d['ln_g'][l:l + 1, :].broadcast_to([128, D]), writes=[r_gb])
        P.dma('sp', gb[:, 1, :], d['ln_b'][l:l + 1, :].broadcast_to([128, D]), swrites=[r_gb])
        zt = [A([128, D], F32, 'zt%d' % i) for i in range(4)]
        r_zt = [P.res() for i in range(4)]
        st = [A([128, 8, 6], F32, 'fst%d' % i) for i in range(4)]
        mv = [A([128, 4], F32, 'fmv%d' % i) for i in range(4)]
        r_mv = [P.res() for i in range(4)]
        r_x = P.res('Xw')
        for tt in range(NT):
            b = tt % 4
            P.dma('sp', zt[b][:], d['Z'][tt * 128:(tt + 1) * 128, :], writes=[r_zt[b]])
            for c8 in range(8):
                P.add('dve', lambda e, o=st[b][:, c8, :], i=zt[b][:, c8 * 512:(c8 + 1) * 512]: e.bn_stats(out=o, in_=i),
                      reads=[r_zt[b]], writes=[r_mv[b]] if c8 == 0 else (), swrites=() if c8 == 0 else [r_mv[b]])
            P.add('dve', lambda e, o=mv[b][:, 0:2], i=st[b][:].rearrange("p a b -> p (a b)"): e.bn_aggr(out=o, in_=i),
                  reads=[r_mv[b]], writes=[r_mv[b]])
            self.rstd_from(mv[b][:, 2:3], mv[b][:, 1:2], r_mv[b])
            self.stt(mv[b][:, 3:4], mv[b][:, 0:1], -1.0, mv[b][:, 2:3], ALU.mult, ALU.mult, [r_mv[b]], [r_mv[b]])
            self.act(zt[b][:], zt[b][:], AF.Identity, [r_zt[b], r_mv[b]], [r_zt[b]], scale=mv[b][:, 2:3],
                     bias=mv[b][:, 3:4])
            self.tt('dve', zt[b][:], zt[b][:], gb[:, 0, :], ALU.mult, [r_zt[b], r_gb], [r_zt[b]])
            self.tt('dve', zt[b][:], zt[b][:], gb[:, 1, :], ALU.add, [r_zt[b], r_gb], [r_zt[b]])
            if l == DEPTH - 1:
                if tt < 16:
                    dst = d['o_ys'][tt * 128:(tt + 1) * 128, :]
                else:
                    dst = d['o_yp'][(tt - 16) * 128:(tt - 15) * 128, :]
            else:
                dst = d['XRES'][tt * 128:(tt + 1) * 128, :]
            P.dma('sp', dst, zt[b][:], reads=[r_zt[b]], swrites=[r_x])


def _consts():
    k = {}
    k['k_ident'] = np.eye(128, dtype=np.float32)
    t = np.arange(LS)
    nf = 32
    inv = (10000.0 ** (-np.arange(nf, dtype=np.float32) / nf)).astype(np.float32)
    ang_r = ((t // 64).astype(np.float32)[:, None] * inv).astype(np.float32)
    ang_c = ((t % 64).astype(np.float32)[:, None] * inv).astype(np.float32)
    cos = np.zeros((LS, 2, 2, 32), np.float32)
    sin = np.zeros((LS, 2, 2, 32), np.float32)
    for a, ang in enumerate((ang_r, ang_c)):
        cos[:, a, 0] = np.cos(ang); cos[:, a, 1] = np.cos(ang)
        sin[:, a, 0] = -np.sin(ang); sin[:, a, 1] = np.sin(ang)
    k['k_cos'] = cos.reshape(LS, 128)
    k['k_sin'] = sin.reshape(LS, 128)
    rows = LS // 64
    m = np.full((5, 128, 640), NEG, np.float32)
    col = np.arange(64)
    c0 = np.clip(col - 8, 0, 48)
    col_ok = (col[None, :] >= c0[:, None]) & (col[None, :] < c0[:, None] + 16)
    for ty, qb in enumerate((0, 1, 5, 14, 15)):
        tw = min(max(qb - 2, 0), 11)
        for half in range(2):
            r = 2 * qb + half
            w0 = min(max(r - 4, 0), rows - 8)
            for kr in range(10):
                ra = 2 * tw + kr
                if w0 <= ra < w0 + 8:
                    blk = np.where(col_ok, 0.0, NEG).astype(np.float32)
                    m[ty, half * 64:(half + 1) * 64, kr * 64:(kr + 1) * 64] = blk
    k['k_namask'] = m
    j = np.arange(128)[:, None]
    i = np.arange(128)[None, :]
    k['k_rmask'] = np.stack([(j <= i), (j >= i)]).astype(np.float32)
    ii = np.arange(128, dtype=np.float32)
    k['k_coef'] = np.stack([ii + 1.0, -(ii + 1.0), 128.0 - ii, -(128.0 - ii)], 1).astype(np.float32)
    return k


_CACHE = {}


def _get_builder(debug=False, stop_after=None):
    key = (debug, stop_after)
    if key not in _CACHE:
        b = Builder(debug=debug, stop_after=stop_after)
        b.stats = b.build()
        _CACHE[key] = b
    return _CACHE[key]


def make_in_maps(inputs, cores):
    f = lambda a: np.ascontiguousarray(np.asarray(a, dtype=np.float32))
    k = _consts()
    shared = dict(
        w_ada=f(inputs['w_ada']), b_ada=f(inputs['b_ada']), w_in=f(inputs['w_in']), w_out=f(inputs['w_out']),
        ln_g=f(inputs['ln_g']), ln_b=f(inputs['ln_b']),
        na_bias=f(inputs['na_bias']).reshape(DEPTH, 120, 31),
        diff_lam=f(inputs['diff_lam']).reshape(DEPTH, 512), diff_subln=f(inputs['diff_subln']),
        ret_decay=f(inputs['ret_decay']).reshape(DEPTH, 16), conv_w=f(inputs['conv_w']), **k)
    xs, xp = f(inputs['x_sample']), f(inputs['x_prompt'])
    maps = []
    for i in cores:
        m = dict(shared)
        m['x_s'] = xs[i]
        m['x_p'] = xp[2 * i:2 * i + 2].reshape(2 * LP, D)
        m['c_na_k'] = f(inputs['cache_na_k'][i]).reshape(DEPTH, PAST, 1024)
        m['c_na_v'] = f(inputs['cache_na_v'][i]).reshape(DEPTH, PAST, 1024)
        m['c_df_k'] = f(inputs['cache_diff_k'][i]).reshape(DEPTH, PAST, 1024)
        m['c_df_v'] = f(inputs['cache_diff_v'][i]).reshape(DEPTH, PAST, 1024)
        m['st_ret'] = f(inputs['state_ret'][i])
        m['cvec'] = np.stack([f(inputs['c'])[i], f(inputs['c_ctx'])])
        maps.append(m)
    return maps


def kernel(**inputs):
    n = 8
    b = _get_builder()
    maps = make_in_maps(inputs, range(n))
    res = run_bass_kernel_spmd(b.nc, maps, core_ids=list(range(n)))
    R = res.results
    y_s = np.stack([R[i]['o_ys'] for i in range(n)])
    y_p = np.concatenate([R[i]['o_yp'].reshape(2, LP, D) for i in range(n)])

    def cat(nm, shape):
        return np.concatenate([R[i][nm].reshape((2, DEPTH, LP) + shape) for i in range(n)])
    nak = cat('o_nak', (8, HD))
    nav = cat('o_nav', (8, HD))
    dfk = cat('o_dfk', (4, 2, HD))
    dfv = cat('o_dfv', (4, 2 * HD))
    st = np.concatenate([R[i]['o_st'] for i in range(n)])
    return (y_p.astype(np.float32), y_s.astype(np.float32), nak, nav, dfk, dfv, st)
```

```python
import math
import contextlib
import numpy as np
import concourse.bass as bass
import concourse.mybir as mybir
from concourse.bass_utils import run_bass_kernel_spmd

F32 = mybir.dt.float32
BF16 = mybir.dt.bfloat16
AF = mybir.ActivationFunctionType
ALU = mybir.AluOpType
AX = mybir.AxisListType

D = 4096
DEPTH = 2
LS = 2048
LP = 256
NTOK = LS + 2 * LP
NT = NTOK // 128
PAST = 512
HD = 128
GW = 1024
DIN = 16384
ALPHA = (2 * DEPTH) ** 0.25
EPS = 1e-6
SCALE = HD ** -0.5
NEG = -30000.0
SB_LO = 16512
SB_HI = 229344

ENGS = ('pe', 'act', 'dve', 'pool', 'sp')


class Res:
    __slots__ = ('name', 'wc', 'wd', 'rc', 'rd', 'war_c', 'war_d', 'bank', 'gen')

    def __init__(self, name, bank=None):
        self.name = name
        self.bank = bank
        self.gen = None
        self.wc = {}
        self.wd = []
        self.rc = {}
        self.rd = []
        self.war_c = {}
        self.war_d = []


class Prog:
    def __init__(self, nc, n_sp=24, n_pool=16, n_act=4):
        self.nc = nc
        self.ins = []
        self.ring_sizes = {'sp': n_sp, 'pool': n_pool, 'act': n_act}
        self.all_res = []
        self.locks = {}

    def res(self, name='', bank=None):
        r = Res(name, bank)
        self.all_res.append(r)
        return r

    def add(self, eng, fn, reads=(), writes=(), swrites=(), dma=False):
        idx = len(self.ins)
        deps = set()
        for r in reads:
            deps.update(r.wc.values())
            deps.update(r.wd)
        for r in writes:
            deps.update(r.wc.values()); deps.update(r.wd)
            deps.update(r.rc.values()); deps.update(r.rd)
            deps.update(r.war_c.values()); deps.update(r.war_d)
        for r in swrites:
            if r.rc or r.rd:
                r.war_c, r.war_d = r.rc, r.rd
                r.rc, r.rd = {}, []
                r.wc, r.wd = {}, []
                r.gen = None
            deps.update(r.war_c.values()); deps.update(r.war_d)
            if r.gen is not None:
                deps.add(r.gen)
        for r in reads:
            if dma:
                r.rd.append(idx)
            else:
                r.rc[eng] = idx
        for r in writes:
            wc_ = dict(r.wc)
            for k_, v_ in r.rc.items():
                if wc_.get(k_, -1) < v_:
                    wc_[k_] = v_
            r.war_c, r.war_d = wc_, r.rd + r.wd
            r.rc, r.rd = {}, []
            r.gen = idx
            if dma:
                r.wc, r.wd = {}, [idx]
            else:
                r.wc, r.wd = {eng: idx}, []
        for r in swrites:
            if dma:
                r.wd.append(idx)
            else:
                r.wc[eng] = idx
        banks = None
        for grp_ in (reads, writes, swrites):
            for r in grp_:
                if r.bank is not None:
                    if banks is None:
                        banks = set()
                    banks.add(r.bank)
        if banks:
            for b_ in banks:
                L = self.locks.setdefault(b_, {})
                for e2, i2 in L.items():
                    if e2 != eng:
                        deps.add(i2)
                L[eng] = idx
        deps.discard(idx)
        self.ins.append([eng, fn, deps, dma])
        return idx

    def dma(self, q, out, in_, reads=(), writes=(), swrites=(), **kw):
        def fn(e, out=out, in_=in_, kw=kw):
            return e.dma_start(out=out, in_=in_, **kw)
        return self.add(q, fn, reads, writes, swrites, dma=True)

    def barrier(self):
        allr = self.res('barrier')
        deps = set()
        for r in self.all_res:
            deps.update(r.wc.values()); deps.update(r.wd)
            deps.update(r.rc.values()); deps.update(r.rd)
            deps.update(r.war_c.values()); deps.update(r.war_d)
        first = True
        for rnd in range(2):
            for eng in ENGS:
                def fn(e):
                    return e.nop()
                if rnd == 0:
                    i = self.add(eng, fn, writes=[allr])
                    if first:
                        self.ins[i][2].update(deps)
                        first = False
                else:
                    self.add(eng, fn, reads=[allr])
        for r in self.all_res:
            if r is allr:
                continue
            r.wc, r.wd, r.rc, r.rd, r.war_c, r.war_d = {}, [], {}, [], {}, []
            r.gen = None
        self.locks = {}
        self.all_res = [r for r in self.all_res if r is allr or getattr(r, 'name', '') != 'barrier']

    def emit(self):
        nc = self.nc
        ins = self.ins
        n = len(ins)
        engobj = {'pe': nc.tensor, 'act': nc.scalar, 'dve': nc.vector, 'pool': nc.gpsimd, 'sp': nc.sync}
        needed = [False] * n
        for i in range(n):
            eng, fn, deps, dma = ins[i]
            keep = []
            for d in deps:
                deng, _, _, ddma = ins[d]
                if (not dma) and (not ddma) and deng == eng and eng == 'pe':
                    continue
                keep.append(d)
            ins[i][2] = keep
            for d in keep:
                needed[d] = True
        self._stack = contextlib.ExitStack()
        sems = {e: self._stack.enter_context(nc.semaphore('s_' + e)) for e in ENGS}
        rings = {q: [self._stack.enter_context(nc.semaphore('r_%s%d' % (q, j))) for j in range(k)]
                 for q, k in self.ring_sizes.items()}
        ring_pos = {q: 0 for q in rings}
        ring_val = {q: [0] * len(rings[q]) for q in rings}
        ring_used = {q: [False] * len(rings[q]) for q in rings}
        tick = {e: 0 for e in ENGS}
        ev = [None] * n
        waited = {e: {} for e in engobj}
        nwaits = 0
        for i in range(n):
            eng, fn, deps, dma = ins[i]
            e = engobj[eng]
            need = {}
            if dma:
                q = eng
                pos = ring_pos[q]
                ring_pos[q] = (pos + 1) % len(rings[q])
                if ring_used[q][pos]:
                    need[('r', q, pos)] = ring_val[q][pos]
            for d in deps:
                k, v = ev[d]
                if need.get(k, -1) < v:
                    need[k] = v
            w = waited[eng]
            for k, v in need.items():
                if w.get(k, -1) >= v:
                    continue
                w[k] = v
                so = sems[k[1]] if k[0] == 'c' else rings[k[1]][k[2]]
                e.wait_ge(so, v)
                nwaits += 1
            inst = fn(e)
            if dma:
                ring_val[q][pos] += 16
                ring_used[q][pos] = True
                inst.then_inc(rings[q][pos], 16)
                ev[i] = (('r', q, pos), ring_val[q][pos])
            else:
                if needed[i]:
                    tick[eng] += 1
                    inst.then_inc(sems[eng], 1)
                    ev[i] = (('c', eng), tick[eng])
            ins[i][1] = None
        self.stats = dict(n=n, nwaits=nwaits, ticks=dict(tick))
        return self.stats


class Alloc:
    def __init__(self, nc):
        self.nc = nc
        self.off = SB_LO
        self.cnt = 0
        self.mark_ = SB_LO

    def __call__(self, shape, dt, name='t'):
        per = 1
        for s in shape[1:]:
            per *= s
        nbytes = per * (4 if dt == F32 else 2)
        nbytes = (nbytes + 63) // 64 * 64
        assert self.off + nbytes <= SB_HI, ('SBUF overflow', name, self.off, nbytes)
        self.cnt += 1
        t = self.nc.alloc_sbuf_tensor_at('%s_%d' % (name, self.cnt), list(shape), dt, offset=self.off)
        self.off += nbytes
        return t

    def mark(self):
        self.mark_ = self.off

    def reset(self):
        self.off = self.mark_


class Builder:
    def __init__(self, debug=False, stop_after=None, lite=()):
        self.debug = debug
        self.stop_after = stop_after
        self.lite = lite
        nc = self.nc = bass.Bass("TRN2", target_bir_lowering=False)
        self.P = Prog(nc)
        self.A = Alloc(nc)
        self.inputs = {}
        self.outputs = {}
        self.dbg_names = []

    def din(self, name, shape, dt=F32):
        if name in self.lite:
            shape = [DEPTH, 128, 512]
        t = self.nc.dram_tensor(name, list(shape), dt, kind="ExternalInput").ap()
        self.inputs[name] = t
        return t

    def dout(self, name, shape, dt=F32):
        t = self.nc.dram_tensor(name, list(shape), dt, kind="ExternalOutput").ap()
        self.outputs[name] = t
        return t

    def dscr(self, name, shape, dt=F32):
        if self.debug:
            t = self.nc.dram_tensor(name, list(shape), dt, kind="ExternalOutput").ap()
            self.dbg_names.append(name)
        else:
            t = self.nc.dram_tensor(name, list(shape), dt).ap()
        return t

    def mm(self, out, lhsT, rhs, start, stop, reads, wres, first=None):
        first = start if first is None else first
        fn = lambda e: e.matmul(out, lhsT=lhsT, rhs=rhs, start=start, stop=stop)
        if first:
            self.P.add('pe', fn, reads=reads, writes=[wres])
        else:
            self.P.add('pe', fn, reads=reads, swrites=[wres])

    def tr(self, out, in_, ident, reads, wres, excl):
        fn = lambda e: e.transpose(out=out, in_=in_, identity=ident)
        if excl:
            self.P.add('pe', fn, reads=reads, writes=[wres])
        else:
            self.P.add('pe', fn, reads=reads, swrites=[wres])

    def act(self, out, in_, func, reads, writes=(), swrites=(), **kw):
        self.P.add('act', lambda e: e.activation(out=out, in_=in_, func=func, **kw), reads, writes, swrites)

    def ts(self, eng, out, in0, s1, s2, op0, op1, reads, writes=(), swrites=(), **kw):
        if op1 is None:
            fn = lambda e: e.tensor_scalar(out=out, in0=in0, scalar1=s1, scalar2=None, op0=op0, **kw)
        else:
            fn = lambda e: e.tensor_scalar(out=out, in0=in0, scalar1=s1, scalar2=s2, op0=op0, op1=op1, **kw)
        self.P.add(eng, fn, reads, writes, swrites)

    def tt(self, eng, out, in0, in1, op, reads, writes=(), swrites=()):
        self.P.add(eng, lambda e: e.tensor_tensor(out=out, in0=in0, in1=in1, op=op), reads, writes, swrites)

    def stt(self, out, in0, scalar, in1, op0, op1, reads, writes=(), swrites=(), eng='dve'):
        self.P.add(eng, lambda e: e.scalar_tensor_tensor(out=out, in0=in0, scalar=scalar, in1=in1, op0=op0, op1=op1),
                   reads, writes, swrites)

    def cp(self, eng, out, in_, reads, writes=(), swrites=()):
        if eng == 'act':
            self.P.add('act', lambda e: e.copy(out=out, in_=in_), reads, writes, swrites)
        else:
            self.P.add(eng, lambda e: e.tensor_copy(out=out, in_=in_), reads, writes, swrites)

    def memset(self, eng, out, val, writes=(), swrites=()):
        self.P.add(eng, lambda e: e.memset(out, val), (), writes, swrites)


    NSTG = 4
    NSTG_A = 8

    def alloc_stg(self, n=4):
        self.NSTG = n
        self.stg = [self.A([128, 2, 512], F32, 'stg%d' % i) for i in range(self.NSTG)]
        self.r_stg = [self.P.res('stg%d' % i) for i in range(self.NSTG)]
        self._wst = 0

    def w_pieces(self, dst, r_dst, src, engs=('pool',)):
        P = self.P
        first = True
        for pc in range(16):
            slot = self._wst % self.NSTG
            self._wst += 1
            P.dma('sp', self.stg[slot][:], src[:, 2 * pc:2 * pc + 2, :], writes=[self.r_stg[slot]])
            self.cp(engs[pc % len(engs)], dst[:, 2 * pc:2 * pc + 2, :], self.stg[slot][:], [self.r_stg[slot]],
                    [r_dst] if first else (), () if first else [r_dst])
            first = False
            yield

    def w_pieces_d(self, dst, r_dst, srcs):
        P = self.P
        first = True
        dv = dst.rearrange("p k (q c) -> p k q c", q=4)
        for q in range(4):
            for k8 in range(4):
                slot = self._wst % self.NSTG
                self._wst += 1
                sv = self.stg[slot][:].rearrange("p a (b c) -> p (a b) c", c=128)
                P.dma('sp', sv, srcs[q][:, k8 * 8:(k8 + 1) * 8, :], writes=[self.r_stg[slot]])
                self.cp('pool', dv[:, k8 * 8:(k8 + 1) * 8, q, :], sv, [self.r_stg[slot]],
                        [r_dst] if first else (), () if first else [r_dst])
                first = False
                yield

    @staticmethod
    def drain(gen, n=None):
        if gen is None:
            return
        if n is None:
            for _ in gen:
                pass
        else:
            for _ in range(n):
                if next(gen, 'end') == 'end':
                    break

    def rstd_from(self, dst, src, res, mult=1.0):
        self.ts('dve', dst, src, mult, EPS, ALU.mult, ALU.add, [res], [res])
        self.act(dst, dst, AF.Ln, [res], [res])
        self.act(dst, dst, AF.Exp, [res], [res], scale=-0.5)

    def build(self):
        nc, P, A = self.nc, self.P, self.A
        d = self.d = {}
        d['x_s'] = self.din('x_s', [LS, D])
        d['x_p'] = self.din('x_p', [2 * LP, D])
        d['c_na_k'] = self.din('c_na_k', [DEPTH, PAST, 8 * HD])
        d['c_na_v'] = self.din('c_na_v', [DEPTH, PAST, 8 * HD])
        d['c_df_k'] = self.din('c_df_k', [DEPTH, PAST, 8 * HD])
        d['c_df_v'] = self.din('c_df_v', [DEPTH, PAST, 4 * 2 * HD])
        d['st_ret'] = self.din('st_ret', [DEPTH, 2, 8, HD, HD])
        d['cvec'] = self.din('cvec', [2, D])
        d['w_ada'] = self.din('w_ada', [DEPTH, D, 3 * D])
        d['b_ada'] = self.din('b_ada', [DEPTH, 3 * D])
        d['w_in'] = self.din('w_in', [DEPTH, D, DIN])
        d['w_out'] = self.din('w_out', [DEPTH, D, D])
        d['ln_g'] = self.din('ln_g', [DEPTH, D])
        d['ln_b'] = self.din('ln_b', [DEPTH, D])
        d['na_bias'] = self.din('na_bias', [DEPTH, 8 * 15, 31])
        d['diff_lam'] = self.din('diff_lam', [DEPTH, 4 * HD])
        d['diff_subln'] = self.din('diff_subln', [DEPTH, 2 * HD])
        d['ret_decay'] = self.din('ret_decay', [DEPTH, 16])
        d['conv_w'] = self.din('conv_w', [DEPTH, GW, 3])
        d['k_ident'] = self.din('k_ident', [128, 128])
        d['k_cos'] = self.din('k_cos', [LS, 128])
        d['k_sin'] = self.din('k_sin', [LS, 128])
        d['k_namask'] = self.din('k_namask', [5, 128, 640])
        d['k_rmask'] = self.din('k_rmask', [2, 128, 128])
        d['k_coef'] = self.din('k_coef', [128, 4])
        d['o_ys'] = self.dout('o_ys', [LS, D])
        d['o_yp'] = self.dout('o_yp', [2 * LP, D])
        d['o_nak'] = self.dout('o_nak', [2, DEPTH, LP, GW])
        d['o_nav'] = self.dout('o_nav', [2, DEPTH, LP, GW])
        d['o_dfk'] = self.dout('o_dfk', [2, DEPTH, LP, GW])
        d['o_dfv'] = self.dout('o_dfv', [2, DEPTH, LP, GW])
        d['o_st'] = self.dout('o_st', [2, DEPTH, 2, 8, HD, HD])
        d['MADA'] = self.dscr('MADA', [DEPTH, 2, 3 * D])
        d['XRES'] = self.dscr('XRES', [NTOK, D])
        for nm in ('QTA', 'KTA', 'QTB', 'KTB', 'QFT', 'QBT', 'KFT', 'KBT'):
            d[nm] = self.dscr(nm, [8, HD, NTOK], BF16)
        for nm in ('VA', 'VB', 'VC', 'KF', 'KB'):
            d[nm] = self.dscr(nm, [NTOK, GW], BF16)
        d['GT'] = self.dscr('GT', [NTOK, 3 * GW], BF16)
        d['UT'] = self.dscr('UT', [GW, NTOK])
        d['GDT'] = self.dscr('GDT', [GW, NTOK])
        d['Y'] = self.dscr('Y', [NTOK, 3 * GW], BF16)
        d['YTD'] = self.dscr('YTD', [GW, NTOK], BF16)
        d['Z'] = self.dscr('Z', [NTOK, D])
        d['PBREP'] = self.dscr('PBREP', [120, 64, 128])

        self._es = contextlib.ExitStack()
        PS = self._es.enter_context(nc.psum_tensor("psum_all", [128, 4096], F32))
        self.PS = PS

        def bank(b, lo=0, hi=512):
            return PS[:, b * 512 + lo: b * 512 + hi]

        def bank_bf(b, lo=0, hi=1024):
            return PS[:, b * 512 + lo // 2: b * 512 + hi // 2].bitcast(BF16)
        self.bank, self.bank_bf = bank, bank_bf
        self.r_bank = [P.res('bank%d' % b, bank=b) for b in range(8)]
        r_bank = self.r_bank

        self.r_const = r_const = P.res('const')
        self.ident = ident = A([128, 128], F32, 'ident')
        self.identb = identb = A([128, 128], BF16, 'identb')
        P.dma('sp', ident[:], d['k_ident'][:, :], swrites=[r_const])
        P.dma('pool', identb[:], d['k_ident'][:, :], swrites=[r_const])
        self.coef = coef = A([128, 4], F32, 'coef')
        P.dma('sp', coef[:], d['k_coef'][:, :], swrites=[r_const])
        A.mark()
        self.base_mark = A.mark_

        cT = A([128, 32, 2], F32, 'cT'); r_cT = P.res('cT')
        cTb = A([128, 32, 2], BF16, 'cTb'); r_cTb = P.res('cTb')
        wbuf = [A([128, 32, 512], BF16, 'wbuf%d' % i) for i in range(2)]
        r_wbuf = [P.res('wbuf%d' % i) for i in range(2)]
        mrow = [A([2, 512], F32, 'mrow%d' % i) for i in range(2)]
        brow = [A([2, 512], F32, 'brow%d' % i) for i in range(2)]
        r_mrow = [P.res() for i in range(2)]
        r_brow = [P.res() for i in range(2)]
        self.alloc_stg(8)
        for cv in range(2):
            P.dma('sp', cT[:, :, cv], d['cvec'][cv].rearrange("(kc p) -> p kc", p=128), swrites=[r_cT],
                  allow_slow_non_contiguous=True)
        self.act(cTb[:], cT[:], AF.Silu, [r_cT], [r_cTb])
        wcnt = 0
        r_mada = P.res('MADA')
        for l in range(DEPTH if 'w_ada' not in self.lite else 0):
            wl = d['w_ada'][l].rearrange("(kc p) n -> p kc n", p=128)
            for ch in range(24):
                s = wcnt % 2
                wcnt += 1
                self.drain(self.w_pieces(wbuf[s][:], r_wbuf[s], wl[:, :, ch * 512:(ch + 1) * 512],
                                          engs=('pool', 'act', 'dve', 'act', 'dve')))
                P.dma('sp', brow[s][:], d['b_ada'][l:l + 1, ch * 512:(ch + 1) * 512].broadcast_to([2, 512]),
                      writes=[r_brow[s]])
                pb = ch % 4
                for kc in range(32):
                    self.mm(bank(pb)[0:2, :], cTb[:, kc, :], wbuf[s][:, kc, :], kc == 0, kc == 31,
                            [r_cTb, r_wbuf[s]], r_bank[pb])
                self.tt('dve', mrow[s][:], bank(pb)[0:2, :], brow[s][:], ALU.add,
                        [r_bank[pb], r_brow[s]], [r_mrow[s]])
                P.dma('sp', d['MADA'][l, :, ch * 512:(ch + 1) * 512], mrow[s][:], reads=[r_mrow[s]],
                      swrites=[r_mada])
        P.barrier()
        A.reset()
        if self.stop_after == 'A':
            return self.finish()

        self.groups = [list(range(0, 8)) + [16, 17], list(range(8, 16)) + [18, 19]]

        for l in range(DEPTH):
            self.layer(l)
            if self.stopped:
                break
        return self.finish()

    stopped = False

    def finish(self):
        self.P.barrier()
        return self.P.emit()

    def x_src(self, l, tt, c0=0, c1=D):
        d = self.d
        if l == 0:
            if tt < 16:
                return d['x_s'][tt * 128:(tt + 1) * 128, c0:c1]
            return d['x_p'][(tt - 16) * 128:(tt - 15) * 128, c0:c1]
        return d['XRES'][tt * 128:(tt + 1) * 128, c0:c1]

    def layer(self, l):
        nc, P, A = self.nc, self.P, self.A
        d = self.d
        coef = self.coef
        A.mark_ = self.base_mark
        A.reset()
        r_tab = P.res('tab')
        mod = A([128, 2, 2, 32], F32, 'mod')
        for cv in range(2):
            for wh in range(2):
                P.dma('sp', mod[:, cv, wh, :],
                      d['MADA'][l, cv, wh * D:(wh + 1) * D].rearrange("(kc p) -> p kc", p=128),
                      swrites=[r_tab], allow_slow_non_contiguous=True)
        self.ts('dve', mod[:, :, 1, :], mod[:, :, 1, :], 1.0, None, ALU.add, None, [r_tab], [r_tab])
        cos2 = A([128, 16, 128], F32, 'cos2')
        sin2 = A([128, 16, 128], F32, 'sin2')
        P.dma('sp', cos2[:], d['k_cos'].rearrange("(t p) c -> p t c", p=128), swrites=[r_tab])
        P.dma('sp', sin2[:], d['k_sin'].rearrange("(t p) c -> p t c", p=128), swrites=[r_tab])
        ld = A([128, 16], F32, 'ld')
        P.dma('sp', ld[:], d['ret_decay'][l:l + 1, :].broadcast_to([128, 16]), writes=[r_tab])
        self.act(ld[:], ld[:], AF.Exp, [r_tab], [r_tab], scale=-1.0)
        self.ts('dve', ld[:], ld[:], 1.0, None, ALU.add, None, [r_tab], [r_tab])
        self.act(ld[:], ld[:], AF.Ln, [r_tab], [r_tab])
        self.ts('dve', ld[:], ld[:], -1.0, None, ALU.mult, None, [r_tab], [r_tab])
        dec = A([128, 4, 8], F32, 'dec')
        self.act(dec[:, 0, :], ld[:, 0:8], AF.Exp, [r_tab], [r_tab], scale=coef[:, 0:1])
        self.act(dec[:, 1, :], ld[:, 8:16], AF.Exp, [r_tab], [r_tab], scale=coef[:, 2:3])
        self.act(dec[:, 2, :], ld[:, 0:8], AF.Exp, [r_tab], [r_tab], scale=coef[:, 1:2])
        self.act(dec[:, 3, :], ld[:, 8:16], AF.Exp, [r_tab], [r_tab], scale=coef[:, 3:4])
        self.ts('dve', dec[:, 2:4, :], dec[:, 2:4, :], SCALE, None, ALU.mult, None, [r_tab], [r_tab])
        cdt = A([128, 16], F32, 'cdt')
        self.act(cdt[:], ld[:], AF.Exp, [r_tab], [r_tab], scale=128.0)
        A.mark()
        self.layer_mark = A.mark_
        self.tabs = dict(mod=mod, cos2=cos2, sin2=sin2, dec=dec, cdt=cdt, r_tab=r_tab)

        if self.stop_after == 'TAB%d' % l:
            self.stopped = True
            return
        self.phase_BC(l)
        if self.stopped:
            return
        A.mark_ = self.layer_mark
        P.barrier()
        if self.stop_after == 'BC%d' % l:
            self.stopped = True
            return
        self.phase_mixers(l)
        P.barrier()
        if self.stop_after == 'MIX%d' % l:
            self.stopped = True
            return
        self.phase_EF(l)
        P.barrier()
        if self.stop_after == 'L%d' % l:
            self.stopped = True

    def phase_BC(self, l):
        nc, P, A = self.nc, self.P, self.A
        d = self.d
        bank, bank_bf, r_bank = self.bank, self.bank_bf, self.r_bank
        ident, identb, r_const = self.ident, self.identb, self.r_const
        T = self.tabs
        r_tab = T['r_tab']
        mod, cos2, sin2, dec = T['mod'], T['cos2'], T['sin2'], T['dec']
        A.reset()
        hT = A([128, 32, 1280], BF16, 'hT')
        r_hT = [P.res('hT%d' % s) for s in range(10)]
        wbuf = [A([128, 32, 512], BF16, 'wb%d' % i) for i in range(2)]
        r_wbuf = [P.res('wb%d' % i) for i in range(2)]
        A.mark()
        w_l = d['w_in'][l]
        w_tok = w_l.rearrange("(kc p) n -> p kc n", p=128)
        r_scr = self.r_scr = getattr(self, 'r_scr', None) or {k: P.res(k) for k in
                                                             ('Q', 'V', 'G', 'UD', 'OUT', 'Y', 'Z', 'X')}

        for gi, grp in enumerate(self.groups):
            A.reset()
            xt = [A([128, D], F32, 'xt%d' % i) for i in range(2)]
            r_xt = [P.res() for i in range(2)]
            st = [A([128, 8, 6], F32, 'st%d' % i) for i in range(2)]
            mv = [A([128, 4], F32, 'mv%d' % i) for i in range(2)]
            r_mv = [P.res() for i in range(2)]
            tb_i = 0
            ev_i = 0
            for s, tt in enumerate(grp):
                b = s % 2
                cv = 0 if tt < 16 else 1
                P.dma('sp', xt[b][:], self.x_src(l, tt), writes=[r_xt[b]])
                for c8 in range(8):
                    P.add('dve', lambda e, o=st[b][:, c8, :], i=xt[b][:, c8 * 512:(c8 + 1) * 512]: e.bn_stats(out=o, in_=i),
                          reads=[r_xt[b]], writes=[r_mv[b]] if c8 == 0 else (), swrites=() if c8 == 0 else [r_mv[b]])
                P.add('dve', lambda e, o=mv[b][:, 0:2], i=st[b][:].rearrange("p a b -> p (a b)"): e.bn_aggr(out=o, in_=i),
                      reads=[r_mv[b]], writes=[r_mv[b]])
                self.rstd_from(mv[b][:, 2:3], mv[b][:, 1:2], r_mv[b])
                self.ts('dve', xt[b][:], xt[b][:], mv[b][:, 0:1], mv[b][:, 2:3], ALU.subtract, ALU.mult,
                        [r_mv[b], r_xt[b]], [r_xt[b]])
                for kq in range(8):
                    pb = (tb_i % 4)
                    tb_i += 1
                    for j in range(4):
                        kc = kq * 4 + j
                        self.tr(bank(pb)[:, j * 128:(j + 1) * 128], xt[b][:, kc * 128:(kc + 1) * 128], ident[:],
                                [r_xt[b], r_const], r_bank[pb], j == 0)
                    for j in range(4):
                        kc = kq * 4 + j
                        o = hT[:, kc, s * 128:(s + 1) * 128]
                        i_ = bank(pb)[:, j * 128:(j + 1) * 128]
                        if kq % 2 == 0:
                            self.act(o, i_, AF.Identity, [r_bank[pb], r_tab], (), [r_hT[s]],
                                     scale=mod[:, cv, 1, kc:kc + 1], bias=mod[:, cv, 0, kc:kc + 1])
                        else:
                            self.ts('dve', o, i_, mod[:, cv, 1, kc:kc + 1], mod[:, cv, 0, kc:kc + 1],
                                    ALU.mult, ALU.add, [r_bank[pb], r_tab], (), [r_hT[s]])
                        ev_i += 1
            P.barrier()
            if self.stop_after == 'B%d' % l:
                self.stopped = True
                return
            A.reset()
            sb16 = [A([128, 512], BF16, 'sb16_%d' % i) for i in range(2)]
            sb16b = [A([128, 512], BF16, 'sb16b_%d' % i) for i in range(2)]
            sf32 = [A([128, 512], F32, 'sf32_%d' % i) for i in range(2)]
            of32 = [A([128, 512], F32, 'of32_0')] * 2
            rt1 = [A([128, 512], F32, 'rt1_%d' % i) for i in range(2)]
            rt2 = [A([128, 512], F32, 'rt2_%d' % i) for i in range(2)]
            trs = [A([128, 1024], BF16, 'trs%d' % i) for i in range(2)]
            r_sb16 = [P.res() for i in range(2)]
            r_sb16b = [P.res() for i in range(2)]
            r_sf32 = [P.res() for i in range(2)]
            r_of32 = [P.res()] * 2
            r_rt = [P.res() for i in range(2)]
            r_trs = [P.res() for i in range(2)]
            dstg0 = A([128, 2, 512], F32, 'dstg')
            dtmp0 = A([128, 2, 512], F32, 'dtmp')
            dstg, dtmp = [dstg0, dstg0], [dtmp0, dtmp0]
            r_dstg0, r_dtmp0 = P.res(), P.res()
            r_dstg, r_dtmp = [r_dstg0, r_dstg0], [r_dtmp0, r_dtmp0]
            self.alloc_stg()

            nunits = 24 + 8
            wslot = [0]

            def load_w(ui, ws):
                if ui < 24:
                    return self.w_pieces(wbuf[ws][:], r_wbuf[ws], w_tok[:, :, ui * 512:(ui + 1) * 512])
                j = ui - 24
                srcs = [w_tok[:, :, (12 + q) * GW + j * 128:(12 + q) * GW + (j + 1) * 128] for q in range(4)]
                return self.w_pieces_d(wbuf[ws][:], r_wbuf[ws], srcs)

            self.drain(load_w(0, 0))
            pending = []
            acc_i = 0
            u_i = 0
            for ui in range(nunits):
                ws = ui % 2
                wgen = load_w(ui + 1, (ui + 1) % 2) if ui + 1 < nunits else None
                if ui < 24:
                    cc = ui
                    part, half = cc // 2, cc % 2
                    for s, tt in enumerate(grp):
                        pb = acc_i % 4
                        acc_i += 1
                        for kc in range(32):
                            self.mm(bank(pb), hT[:, kc, s * 128:(s + 1) * 128], wbuf[ws][:, kc, :], kc == 0, kc == 31,
                                    [r_hT[s], r_wbuf[ws]], r_bank[pb])
                        self.drain(wgen, 2)
                        for f in pending:
                            f()
                        pending = []
                        k = u_i % 2
                        u_i += 1
                        is_s = tt < 16
                        tok0 = tt * 128
                        ps = bank(pb)
                        cols = slice(half * 512, (half + 1) * 512)
                        if not is_s:
                            seq = (tt - 16) // 2
                            ptok = ((tt - 16) % 2) * 128

                        def rope(dst, k=k, ps=ps, pb=pb, tt=tt):
                            c2 = cos2[:, tt, :].unsqueeze(1).broadcast_to([128, 4, 128])
                            psv = ps.rearrange("p (b c) -> p b c", c=128)
                            self.tt('dve', rt1[k][:].rearrange("p (b c) -> p b c", c=128), psv, c2, ALU.mult,
                                    [r_bank[pb], r_tab], [r_rt[k]])
                            ps5 = ps.rearrange("p (b a x f) -> p b a x f", a=2, x=2, f=32)
                            r25 = rt2[k][:].rearrange("p (b a x f) -> p b a x f", a=2, x=2, f=32)
                            s25 = sin2[:, tt, :].rearrange("p (a x f) -> p a x f", a=2, x=2)
                            for x in range(2):
                                for a in range(2):
                                    self.tt('dve', r25[:, :, a, x, :], ps5[:, :, a, 1 - x, :],
                                            s25[:, a, x, :].unsqueeze(1).broadcast_to([128, 4, 32]), ALU.mult,
                                            [r_bank[pb], r_tab], (), [r_rt[k]])
                            return rt1[k], rt2[k]

                        def transposes(srcs, dsts, k=k, tok0=tok0):
                            def f():
                                tb = 4 + (self._tb % 4)
                                self._tb += 1
                                n = 0
                                for (src, rs) in srcs:
                                    for j in range(4):
                                        self.tr(bank_bf(tb)[:, n * 128:(n + 1) * 128], src[:, j * 128:(j + 1) * 128],
                                                identb[:], [rs, r_const], r_bank[tb], n == 0)
                                        n += 1
                                self.cp('dve', trs[k][:, 0:n * 128], bank_bf(tb)[:, 0:n * 128], [r_bank[tb]], [r_trs[k]])
                                for i, dst in enumerate(dsts):
                                    P.dma('sp', dst.rearrange("h d t -> d h t"),
                                          trs[k][:, i * 512:(i + 1) * 512].rearrange("p (h t) -> p h t", h=4),
                                          reads=[r_trs[k]], swrites=[r_scr['Q']])
                            return f

                        if part in (0, 1):
                            self.cp('act', sb16[k][:], ps, [r_bank[pb]], [r_sb16[k]])
                            if part == 1 and not is_s:
                                self.cp('dve', of32[k][:], ps, [r_bank[pb]], [r_of32[k]])
                                P.dma('sp', d['o_nak'][seq, l, ptok:ptok + 128, cols], of32[k][:],
                                      reads=[r_of32[k]], swrites=[r_scr['OUT']])
                            dst = d['QTA' if part == 0 else 'KTA'][half * 4:half * 4 + 4, :, tok0:tok0 + 128]
                            pending.append(transposes([(sb16[k], r_sb16[k])], [dst]))
                        elif part in (2, 6, 10):
                            self.cp('act', sb16[k][:], ps, [r_bank[pb]], [r_sb16[k]])
                            nm = {2: 'VA', 6: 'VB', 10: 'VC'}[part]
                            P.dma('sp', d[nm][tok0:tok0 + 128, cols], sb16[k][:], reads=[r_sb16[k]],
                                  swrites=[r_scr['V']])
                            if not is_s and part in (2, 6):
                                self.cp('dve', of32[k][:], ps, [r_bank[pb]], [r_of32[k]])
                                P.dma('sp', d['o_nav' if part == 2 else 'o_dfv'][seq, l, ptok:ptok + 128, cols],
                                      of32[k][:], reads=[r_of32[k]], swrites=[r_scr['OUT']])
                        elif part in (3, 7, 11):
                            self.act(sb16[k][:], ps, AF.Silu, [r_bank[pb]], [r_sb16[k]])
                            gi_ = {3: 0, 7: 1, 11: 2}[part]
                            P.dma('sp', d['GT'][tok0:tok0 + 128, gi_ * GW + half * 512: gi_ * GW + (half + 1) * 512],
                                  sb16[k][:], reads=[r_sb16[k]], swrites=[r_scr['G']])
                        elif part in (4, 5):
                            if is_s:
                                a1, a2 = rope(None)
                                self.tt('dve', sb16[k][:], a1[:], a2[:], ALU.add, [r_rt[k]], [r_sb16[k]])
                            else:
                                self.cp('act', sb16[k][:], ps, [r_bank[pb]], [r_sb16[k]])
                                if part == 5:
                                    self.cp('dve', of32[k][:], ps, [r_bank[pb]], [r_of32[k]])
                                    P.dma('sp', d['o_dfk'][seq, l, ptok:ptok + 128, cols], of32[k][:],
                                          reads=[r_of32[k]], swrites=[r_scr['OUT']])
                            dst = d['QTB' if part == 4 else 'KTB'][half * 4:half * 4 + 4, :, tok0:tok0 + 128]
                            pending.append(transposes([(sb16[k], r_sb16[k])], [dst]))
                        elif part in (8, 9):
                            if is_s:
                                a1, a2 = rope(None)
                                self.tt('dve', sf32[k][:], a1[:], a2[:], ALU.add, [r_rt[k]], [r_sf32[k]])
                            else:
                                self.cp('act', sf32[k][:], ps, [r_bank[pb]], [r_sf32[k]])
                            base = 0 if part == 8 else 2
                            sfv = sf32[k][:].rearrange("p (h c) -> p h c", c=128)
                            for di, (dstt, rdst) in enumerate(((sb16[k], r_sb16[k]), (sb16b[k], r_sb16b[k]))):
                                dc = dec[:, base + di, half * 4:half * 4 + 4].unsqueeze(2).broadcast_to([128, 4, 128])
                                self.tt('dve', dstt[:].rearrange("p (h c) -> p h c", c=128), sfv, dc, ALU.mult,
                                        [r_sf32[k], r_tab], [rdst])
                            if part == 9:
                                P.dma('sp', d['KF'][tok0:tok0 + 128, cols], sb16[k][:], reads=[r_sb16[k]],
                                      swrites=[r_scr['V']])
                                P.dma('sp', d['KB'][tok0:tok0 + 128, cols], sb16b[k][:], reads=[r_sb16b[k]],
                                      swrites=[r_scr['V']])
                            n1, n2 = ('QFT', 'QBT') if part == 8 else ('KFT', 'KBT')
                            dst1 = d[n1][half * 4:half * 4 + 4, :, tok0:tok0 + 128]
                            dst2 = d[n2][half * 4:half * 4 + 4, :, tok0:tok0 + 128]
                            pending.append(transposes([(sb16[k], r_sb16[k]), (sb16b[k], r_sb16b[k])], [dst1, dst2]))
                    self.drain(wgen)
                else:
                    j = ui - 24
                    for f in pending:
                        f()
                    pending = []
                    wv = wbuf[ws][:].rearrange("p k (q c) -> p k q c", q=4)
                    chunks = [(0, 512), (512, 1024), (1024, 1280)]
                    for ci, (t0, t1) in enumerate(chunks):
                        n = t1 - t0
                        par = (j * 3 + ci) % 2
                        for q in range(4):
                            pb = par * 4 + q
                            for kc in range(32):
                                self.mm(bank(pb)[:, 0:n], wv[:, kc, q, :], hT[:, kc, t0:t1], kc == 0, kc == 31,
                                        [r_wbuf[ws]] + r_hT[t0 // 128:t1 // 128], r_bank[pb])
                        self.drain(wgen, 6)
                        k = par
                        b0 = par * 4
                        tt0 = grp[t0 // 128]
                        g0 = tt0 * 128
                        self.act(dtmp[k][:, 0, 0:n], bank(b0 + 3)[:, 0:n], AF.Silu, [r_bank[b0 + 3]], [r_dtmp[k]])
                        self.cp('act', dtmp[k][:, 1, 0:n], bank(b0 + 2)[:, 0:n], [r_bank[b0 + 2]], (), [r_dtmp[k]])
                        self.tt('dve', dstg[k][:, 0, 0:n], bank(b0 + 1)[:, 0:n], dtmp[k][:, 0, 0:n], ALU.mult,
                                [r_bank[b0 + 1], r_dtmp[k]], [r_dstg[k]])
                        self.tt('dve', dstg[k][:, 1, 0:n], bank(b0)[:, 0:n], dtmp[k][:, 1, 0:n], ALU.mult,
                                [r_bank[b0], r_dtmp[k]], (), [r_dstg[k]])
                        P.dma('sp', d['GDT'][j * 128:(j + 1) * 128, g0:g0 + n], dstg[k][:, 0, 0:n],
                              reads=[r_dstg[k]], swrites=[r_scr['UD']])
                        P.dma('sp', d['UT'][j * 128:(j + 1) * 128, g0:g0 + n], dstg[k][:, 1, 0:n],
                              reads=[r_dstg[k]], swrites=[r_scr['UD']])
                    self.drain(wgen)
            for f in pending:
                f()
            pending = []
            P.barrier()

    _tb = 0
    def phase_mixers(self, l):
        self.mixer_A(l)
        self.P.barrier()
        self.mixer_B(l)
        self.P.barrier()
        self.mixer_C(l)
        self.P.barrier()
        self.mixer_D(l)

    def mixer_A(self, l):
        nc, P, A = self.nc, self.P, self.A
        d = self.d
        bank, bank_bf, r_bank = self.bank, self.bank_bf, self.r_bank
        identb, r_const = self.identb, self.r_const
        A.reset()
        r_y = P.res('Yw')
        pr = A([120, 128], F32, 'pr'); r_pr = P.res('pr')
        self.memset('dve', pr[:], 0.0, writes=[r_pr])
        P.dma('sp', pr[:, 48:79], d['na_bias'][l], writes=[r_pr])
        r_pb = P.res('pbrep')
        P.dma('sp', d['PBREP'][:, :, :], pr[:].unsqueeze(1).broadcast_to([120, 64, 128]), reads=[r_pr], writes=[r_pb])
        masks = A([128, 5, 640], F32, 'masks'); r_masks = P.res('masks')
        P.dma('sp', masks[:], d['k_namask'].rearrange("t p k -> p t k"), writes=[r_masks])
        NB = 2
        qT = [A([128, LS], BF16, 'qT%d' % i) for i in range(NB)]
        kT = [A([128, LS], BF16, 'kT%d' % i) for i in range(NB)]
        V = [A([128, 16, 128], BF16, 'V%d' % i) for i in range(NB)]
        ckl = [A([128, 4, 128], BF16, 'ckl%d' % i) for i in range(NB)]
        ckT = [A([128, 512], BF16, 'ckT%d' % i) for i in range(NB)]
        cV = [A([128, 4, 128], BF16, 'cV%d' % i) for i in range(NB)]
        gate = [A([128, 16, 128], BF16, 'gate%d' % i) for i in range(NB)]
        TB2 = [A([128, 15, 64], F32, 'TB2_%d' % i) for i in range(NB)]
        BT = [A([128, 5, 640], F32, 'BT%d' % i) for i in range(NB)]
        yst = [A([128, 16, 128], BF16, 'yst%d' % i) for i in range(NB)]
        r_h = [P.res('hA%d' % i) for i in range(NB)]
        r_ckl = [P.res() for i in range(NB)]
        r_ckT = [P.res() for i in range(NB)]
        r_TB2 = [P.res() for i in range(NB)]
        r_BT = [P.res() for i in range(NB)]
        r_yst = [P.res() for i in range(NB)]
        sc = [A([128, 1152], F32, 'sc%d' % i) for i in range(2)]
        pbf = [A([128, 1152], BF16, 'pbf%d' % i) for i in range(2)]
        PT = [A([128, 9, 128], BF16, 'PT%d' % i) for i in range(2)]
        stt_ = [A([128, 4], F32, 'st%d' % i) for i in range(2)]
        r_sc = [P.res() for i in range(2)]
        r_pbf = [P.res() for i in range(2)]
        r_PT = [P.res() for i in range(2)]
        r_st = [P.res() for i in range(2)]
        r_tail = [P.res('tail', bank=4 + i) for i in range(2)]
        r_t9 = [P.res('t9', bank=4 + i) for i in range(2)]
        r_o = [P.res('o', bank=4 + i) for i in range(2)]
        self._u = 0

        def unit(qTb, rq, loc, nloc, bias, rbias, ctx, rctx, vlist, out_dst, r_out, gate_ap, rgate, first_out):
            u = self._u % 2
            self._u += 1
            bA, bC, bM, bT = 0 + u, 2 + u, 4 + u, 6 + u
            n1 = min(512, nloc)
            tot = nloc + (512 if ctx is not None else 0)
            nblk = tot // 128
            self.mm(bank(bA)[:, 0:n1], qTb, loc[:, 0:n1], True, True, [rq], r_bank[bA])
            if nloc > 512:
                self.mm(bank(bM)[:, 0:nloc - 512], qTb, loc[:, 512:nloc], True, True, [rq], r_tail[u])
            if ctx is not None:
                self.mm(bank(bC)[:, 0:512], qTb, ctx, True, True, [rq, rctx], r_bank[bC])
            if bias is not None:
                self.stt(sc[u][:, 0:n1], bank(bA)[:, 0:n1], SCALE, bias[:, 0:n1], ALU.mult, ALU.add,
                         [r_bank[bA], rbias], [r_sc[u]])
                self.stt(sc[u][:, 512:nloc], bank(bM)[:, 0:nloc - 512], SCALE, bias[:, 512:nloc], ALU.mult, ALU.add,
                         [r_tail[u], rbias], (), [r_sc[u]])
            else:
                P.add('act', lambda e, o=sc[u][:, 0:n1], i=bank(bA)[:, 0:n1]: e.mul(o, i, SCALE),
                      [r_bank[bA]], [r_sc[u]])
            if ctx is not None:
                P.add('act', lambda e, o=sc[u][:, nloc:tot], i=bank(bC)[:, 0:512]: e.mul(o, i, SCALE),
                      [r_bank[bC]], (), [r_sc[u]])
            self.memset('dve', stt_[u][:], 0.0, writes=[r_st[u]])
            P.add('dve', lambda e, o=stt_[u][:, 0:1], i=sc[u][:, 0:tot]: e.reduce_max(out=o, in_=i, axis=AX.X),
                  [r_sc[u]], (), [r_st[u]])
            self.ts('dve', stt_[u][:, 1:2], stt_[u][:, 0:1], -1.0, None, ALU.mult, None, [r_st[u]], (), [r_st[u]])
            self.act(pbf[u][:, 0:tot], sc[u][:, 0:tot], AF.Exp, [r_sc[u], r_st[u]], [r_pbf[u]],
                     bias=stt_[u][:, 1:2], scale=1.0, accum_out=stt_[u][:, 2:3])
            P.add('dve', lambda e, o=stt_[u][:, 3:4], i=stt_[u][:, 2:3]: e.reciprocal(out=o, in_=i),
                  [r_pbf[u], r_st[u]], (), [r_st[u]])
            nb8 = min(8, nblk)

            def back():
                for b in range(nb8):
                    self.tr(bank_bf(bT)[:, b * 128:(b + 1) * 128], pbf[u][:, b * 128:(b + 1) * 128], identb[:],
                            [r_pbf[u], r_const], r_bank[bT], b == 0)
                if nblk == 9:
                    self.tr(bank_bf(bM, 256, 384), pbf[u][:, 1024:1152], identb[:], [r_pbf[u], r_const], r_t9[u], True)
                self.cp('act', PT[u][:, 0:nb8, :].rearrange("p b q -> p (b q)"), bank_bf(bT)[:, 0:nb8 * 128],
                        [r_bank[bT]], [r_PT[u]])
                if nblk == 9:
                    self.cp('dve', PT[u][:, 8, :], bank_bf(bM, 256, 384), [r_t9[u]], (), [r_PT[u]])
                for b in range(nblk):
                    self.mm(bank(bM)[:, 256:384], PT[u][:, b, :], vlist[b][0], b == 0, b == nblk - 1,
                            [r_PT[u], vlist[b][1]], r_o[u])
                self.stt(out_dst, bank(bM)[:, 256:384], stt_[u][:, 3:4], gate_ap, ALU.mult, ALU.mult,
                         [r_o[u], r_st[u], rgate], [r_out] if first_out else (), () if first_out else [r_out])
            return back

        cl = l
        for h in range(8):
            n = h % NB
            hs = slice(h * 128, (h + 1) * 128)
            P.dma('sp', qT[n][:], d['QTA'][h, :, 0:LS], writes=[r_h[n]])
            P.dma('sp', kT[n][:], d['KTA'][h, :, 0:LS], swrites=[r_h[n]])
            P.dma('sp', V[n][:], d['VA'][0:LS, hs].rearrange("(t p) e -> p t e", p=128), swrites=[r_h[n]])
            P.dma('sp', gate[n][:], d['GT'][0:LS, hs].rearrange("(t p) e -> p t e", p=128), swrites=[r_h[n]])
            P.dma('pool', cV[n][:], d['c_na_v'][cl, :, hs].rearrange("(t p) e -> p t e", p=128), swrites=[r_h[n]])
            P.dma('pool', ckl[n][:], d['c_na_k'][cl, :, hs].rearrange("(t p) e -> p t e", p=128), writes=[r_ckl[n]])
            bT = 6 + (h % 2)
            for t in range(4):
                self.tr(bank_bf(bT)[:, t * 128:(t + 1) * 128], ckl[n][:, t, :], identb[:], [r_ckl[n], r_const],
                        r_bank[bT], t == 0)
            self.cp('dve', ckT[n][:], bank_bf(bT)[:, 0:512], [r_bank[bT]], [r_ckT[n]])
            for half in range(2):
                src = bass.AP(tensor=d['PBREP'].tensor, offset=(h * 15) * 8192 + 63,
                              ap=[[127, 64], [8192, 15], [1, 64]])
                if half == 0:
                    P.dma('sp', TB2[n][0:64, :, :], src, reads=[r_pb], writes=[r_TB2[n]])
                else:
                    P.dma('sp', TB2[n][64:128, :, :], src, reads=[r_pb], swrites=[r_TB2[n]])
            self.cp('act', BT[n][:], masks[:], [r_masks], [r_BT[n]])
            for ty, qrel0 in enumerate((0, 2, 4, 6, 8)):
                for half in range(2):
                    qr = qrel0 + half
                    k0, k1 = max(0, qr - 7), min(9, qr + 7)
                    d0 = k0 - qr + 7
                    nk = k1 - k0 + 1
                    ps_ = slice(half * 64, (half + 1) * 64)
                    o = BT[n][ps_, ty, k0 * 64:(k1 + 1) * 64]
                    self.tt('dve', o, o, TB2[n][ps_, d0:d0 + nk, :].rearrange("p a b -> p (a b)"), ALU.add,
                            [r_TB2[n], r_BT[n]], (), [r_BT[n]])
            pend = None
            for qb in range(16):
                tw = min(max(qb - 2, 0), 11)
                ty = {0: 0, 1: 1, 14: 3, 15: 4}.get(qb, 2)
                vlist = [(V[n][:, tw + b, :], r_h[n]) for b in range(5)] + [(cV[n][:, b, :], r_h[n]) for b in range(4)]
                bk = unit(qT[n][:, qb * 128:(qb + 1) * 128], r_h[n], kT[n][:, tw * 128:tw * 128 + 640], 640,
                          BT[n][:, ty, :], r_BT[n], ckT[n][:], r_ckT[n], vlist, yst[n][:, qb, :], r_yst[n],
                          gate[n][:, qb, :], r_h[n], qb == 0)
                if pend is not None:
                    pend()
                pend = bk
            pend()
            P.dma('sp', d['Y'][0:LS, hs].rearrange("(t p) e -> p t e", p=128), yst[n][:], reads=[r_yst[n]],
                  swrites=[r_y])
        for sq in range(2):
            t0 = LS + sq * LP
            for h in range(8):
                n = h % NB
                hs = slice(h * 128, (h + 1) * 128)
                P.dma('sp', qT[n][:, 0:LP], d['QTA'][h, :, t0:t0 + LP], writes=[r_h[n]])
                P.dma('sp', kT[n][:, 0:LP], d['KTA'][h, :, t0:t0 + LP], swrites=[r_h[n]])
                P.dma('sp', V[n][:, 0:2, :], d['VA'][t0:t0 + LP, hs].rearrange("(t p) e -> p t e", p=128),
                      swrites=[r_h[n]])
                P.dma('sp', gate[n][:, 0:2, :], d['GT'][t0:t0 + LP, hs].rearrange("(t p) e -> p t e", p=128),
                      swrites=[r_h[n]])
                pend = None
                for qb in range(2):
                    vlist = [(V[n][:, b, :], r_h[n]) for b in range(2)]
                    bk = unit(qT[n][:, qb * 128:(qb + 1) * 128], r_h[n], kT[n][:, 0:LP], LP, None, None, None, None,
                              vlist, yst[n][:, qb, :], r_yst[n], gate[n][:, qb, :], r_h[n], qb == 0)
                    if pend is not None:
                        pend()
                    pend = bk
                pend()
                P.dma('sp', d['Y'][t0:t0 + LP, hs].rearrange("(t p) e -> p t e", p=128), yst[n][:, 0:2, :],
                      reads=[r_yst[n]], swrites=[r_y])

    def mixer_B(self, l):
        nc, P, A = self.nc, self.P, self.A
        d = self.d
        bank, bank_bf, r_bank = self.bank, self.bank_bf, self.r_bank
        identb, r_const = self.identb, self.r_const
        lam_init = 0.8 - 0.6 * math.exp(-0.3 * l)
        A.reset()
        r_y = P.res('Yw')
        lt = A([128, 4, 128], F32, 'lt'); r_lt = P.res('lt')
        P.dma('sp', lt[:].rearrange("p a b -> p (a b)"), d['diff_lam'][l:l + 1, :].broadcast_to([128, 512]),
              writes=[r_lt])
        lm = A([128, 2, 128], F32, 'lm')
        self.tt('dve', lm[:, 0, :], lt[:, 0, :], lt[:, 1, :], ALU.mult, [r_lt], [r_lt])
        self.tt('dve', lm[:, 1, :], lt[:, 2, :], lt[:, 3, :], ALU.mult, [r_lt], [r_lt])
        lam = A([128, 4], F32, 'lam')
        P.add('dve', lambda e: e.reduce_sum(out=lam[:, 0:2], in_=lm[:], axis=AX.X), [r_lt], [r_lt])
        self.act(lam[:, 0:2], lam[:, 0:2], AF.Exp, [r_lt], [r_lt])
        self.tt('dve', lam[:, 2:3], lam[:, 0:1], lam[:, 1:2], ALU.subtract, [r_lt], [r_lt])
        self.ts('dve', lam[:, 2:3], lam[:, 2:3], lam_init, None, ALU.add, None, [r_lt], [r_lt])
        wsub = A([128, 256], F32, 'wsub')
        P.dma('sp', wsub[:], d['diff_subln'][l:l + 1, :].broadcast_to([128, 256]), writes=[r_lt])
        self.ts('dve', wsub[:], wsub[:], 1.0 - lam_init, None, ALU.mult, None, [r_lt], [r_lt])

        NKB = (PAST + LS) // 128
        qT = [A([128, LS], BF16, 'bqT%d' % t) for t in range(2)]
        kT = [A([128, PAST + LS], BF16, 'bkT%d' % t) for t in range(2)]
        V = A([128, NKB, 256], BF16, 'bV')
        ckl = A([128, 4, 256], BF16, 'bckl')
        gate = A([128, 16, 256], BF16, 'bgate')
        yst = A([128, 16, 256], BF16, 'byst')
        r_h = P.res('hB'); r_ckl = P.res(); r_yst = P.res()
        exs = [[A([128, PAST + LS], F32, 'ex%d_%d' % (t, i)) for t in range(2)] for i in range(2)]
        r_exs = [[P.res() for t in range(2)] for i in range(2)]
        abfs = [A([128, PAST + LS], BF16, 'abf%d' % i) for i in range(2)]; r_abfs = [P.res() for i in range(2)]
        aTs = [A([128, NKB, 128], BF16, 'aT%d' % i) for i in range(2)]; r_aTs = [P.res() for i in range(2)]
        stt_ = [A([128, 32], F32, 'bst%d' % i) for i in range(2)]
        r_st = [P.res() for i in range(2)]
        otmps = [A([128, 256], F32, 'otmp%d' % i) for i in range(2)]; r_otmps = [P.res() for i in range(2)]
        junks = [A([128, 256], F32, 'junk%d' % i) for i in range(2)]
        r_o = P.res('bo')
        self._sb = 0
        self._ub = 0
        self._tbb = 0
        ones_b = A([128, 128], BF16, 'ones_b'); r_ones = P.res()
        self.memset('dve', ones_b[:], 1.0, writes=[r_ones])
        sqk = A([128, PAST + LS], BF16, 'sqk'); r_sqk = P.res()
        sqq = A([128, LS], BF16, 'sqq'); r_sqq = P.res()
        kmx = A([128, 2, 8], F32, 'kmx')
        nbias = A([128, 2, 16], F32, 'nbias'); r_nb = P.res()

        def bounds(L, Lk):
            nch = (Lk + 511) // 512
            nq = L // 128
            for t in range(2):
                self.act(sqk[:, 0:Lk], kT[t][:, 0:Lk], AF.Square, [r_h], [r_sqk])
                self.act(sqq[:, 0:L], qT[t][:, 0:L], AF.Square, [r_h], [r_sqq])
                for c in range(nch):
                    w = min(512, Lk - c * 512)
                    b = self._sb % 6
                    self._sb += 1
                    self.mm(bank(b)[:, 0:w], ones_b[:], sqk[:, c * 512:c * 512 + w], True, True, [r_ones, r_sqk], r_bank[b])
                    P.add('dve', lambda e, o=kmx[:, t, c:c + 1], i=bank(b)[:, 0:w]: e.reduce_max(out=o, in_=i, axis=AX.X),
                          [r_bank[b]], [r_nb] if (t == 0 and c == 0) else (), () if (t == 0 and c == 0) else [r_nb])
                P.add('dve', lambda e, o=kmx[:, t, 7:8], i=kmx[:, t, 0:nch]: e.reduce_max(out=o, in_=i, axis=AX.X),
                      [r_nb], (), [r_nb])
                b = self._sb % 6
                self._sb += 1
                for qb in range(nq):
                    self.mm(bank(b)[:, 2 * qb:2 * qb + 2], sqq[:, qb * 128:(qb + 1) * 128], ones_b[:, 0:2], True, True,
                            [r_ones, r_sqq], r_bank[b], first=(qb == 0))
                src = bank(b)[:, 0:2 * nq].rearrange("p (q two) -> p q two", two=2)[:, :, 0]
                self.ts('dve', nbias[:, t, 0:nq], src, kmx[:, t, 7:8], None, ALU.mult, None, [r_bank[b], r_nb], (), [r_nb])
            nb2 = nbias[:, :, 0:nq]
            self.act(nb2, nb2, AF.Ln, [r_nb], [r_nb])
            self.act(nb2, nb2, AF.Exp, [r_nb], [r_nb], scale=0.5)
            self.ts('dve', nb2, nb2, -SCALE, None, ALU.mult, None, [r_nb], [r_nb])

        def unit(Lk, qcols, yout, first_out, gate_ap, qb):
            u = self._ub % 2
            self._ub += 1
            st = stt_[u]
            ex, r_ex, abf, r_abf, aT, r_aT = exs[u], r_exs[u], abfs[u], r_abfs[u], aTs[u], r_aTs[u]
            otmp, r_otmp, junk = otmps[u], r_otmps[u], junks[u]
            nkb = Lk // 128
            nch = (Lk + 511) // 512
            self.memset('dve', st[:], 0.0, writes=[r_st[u]])
            for t in range(2):
                for c in range(nch):
                    w = min(512, Lk - c * 512)
                    b = self._sb % 6
                    self._sb += 1
                    self.mm(bank(b)[:, 0:w], qT[t][:, qcols], kT[t][:, c * 512:c * 512 + w], True, True,
                            [r_h], r_bank[b])
                    self.act(ex[t][:, c * 512:c * 512 + w], bank(b)[:, 0:w], AF.Exp, [r_bank[b], r_st[u], r_nb],
                             [r_ex[t]] if c == 0 else (), () if c == 0 else [r_ex[t]],
                             scale=SCALE, bias=nbias[:, t, qb:qb + 1], accum_out=st[:, 14 + t * 5 + c:15 + t * 5 + c])
                P.add('dve', lambda e, o=st[:, 24 + t:25 + t], i=st[:, 14 + t * 5:14 + t * 5 + nch]:
                      e.reduce_sum(out=o, in_=i, axis=AX.X), [r_st[u], r_ex[t]], (), [r_st[u]])
            P.add('dve', lambda e, o=st[:, 26:28], i=st[:, 24:26]: e.reciprocal(out=o, in_=i), [r_st[u]], (), [r_st[u]])
            self.tt('dve', st[:, 28:29], st[:, 27:28], lam[:, 2:3], ALU.mult, [r_st[u], r_lt], (), [r_st[u]])
            self.ts('dve', ex[1][:, 0:Lk], ex[1][:, 0:Lk], st[:, 28:29], None, ALU.mult, None,
                    [r_ex[1], r_st[u]], [r_ex[1]])
            self.stt(abf[:, 0:Lk], ex[0][:, 0:Lk], st[:, 26:27], ex[1][:, 0:Lk], ALU.mult, ALU.subtract,
                     [r_ex[0], r_ex[1], r_st[u]], [r_abf])
            def back():
                self._b_back(Lk, nkb, u, st, abf, r_abf, aT, r_aT, otmp, r_otmp, junk, V, r_h, wsub, r_lt, r_st,
                             yout, first_out, gate_ap, r_yst)
            return back

        def _unused():
            kb = 0
            first = True
            while kb < nkb:
                nb = min(8, nkb - kb)
                bT = 6 + (self._tbb % 2)
                self._tbb += 1
                for j in range(nb):
                    self.tr(bank_bf(bT)[:, j * 128:(j + 1) * 128], abf[:, (kb + j) * 128:(kb + j + 1) * 128], identb[:],
                            [r_abf, r_const], r_bank[bT], j == 0)
                self.cp('act' if (self._tbb % 2) else 'dve', aT[:, kb:kb + nb, :].rearrange("p b q -> p (b q)"),
                        bank_bf(bT)[:, 0:nb * 128], [r_bank[bT]], [r_aT] if first else (), () if first else [r_aT])
                first = False
                kb += nb
            bo = self._sb % 6
            self._sb += 1
            for b_ in range(nkb):
                self.mm(bank(bo)[:, 0:256], aT[:, b_, :], V[:, b_, :], b_ == 0, b_ == nkb - 1, [r_aT, r_h], r_bank[bo])
            self.act(junk[:], bank(bo)[:, 0:256], AF.Square, [r_bank[bo]], [r_otmp], accum_out=st[:, 29:30])
            self.ts('dve', st[:, 30:31], st[:, 29:30], 1.0 / 256.0, EPS, ALU.mult, ALU.add, [r_st[u], r_otmp], (), [r_st[u]])
            self.act(st[:, 30:31], st[:, 30:31], AF.Ln, [r_st[u]], (), [r_st[u]])
            self.act(st[:, 30:31], st[:, 30:31], AF.Exp, [r_st[u]], (), [r_st[u]], scale=-0.5)
            self.stt(otmp[:], bank(bo)[:, 0:256], st[:, 30:31], wsub[:], ALU.mult, ALU.mult,
                     [r_bank[bo], r_st[u], r_lt], [r_otmp])
            self.tt('dve', yout, otmp[:], gate_ap, ALU.mult, [r_otmp, r_h], [r_yst] if first_out else (),
                    () if first_out else [r_yst])

        for h in range(4):
            vs = slice(h * 256, (h + 1) * 256)
            P.dma('pool', ckl[:], d['c_df_k'][l, :, vs].rearrange("(t p) e -> p t e", p=128), writes=[r_ckl])
            first = True
            for t in range(2):
                P.dma('sp', qT[t][:], d['QTB'][h * 2 + t, :, 0:LS], writes=[r_h] if first else (),
                      swrites=() if first else [r_h])
                first = False
                P.dma('sp', kT[t][:, PAST:PAST + LS], d['KTB'][h * 2 + t, :, 0:LS], swrites=[r_h])
                bT = 6 + (t % 2)
                for tb in range(4):
                    self.tr(bank_bf(bT)[:, tb * 128:(tb + 1) * 128], ckl[:, tb, t * 128:(t + 1) * 128], identb[:],
                            [r_ckl, r_const], r_bank[bT], tb == 0)
                self.cp('dve', kT[t][:, 0:PAST], bank_bf(bT)[:, 0:512], [r_bank[bT]], (), [r_h])
            P.dma('pool', V[:, 0:4, :], d['c_df_v'][l, :, vs].rearrange("(t p) e -> p t e", p=128), swrites=[r_h])
            P.dma('sp', V[:, 4:20, :], d['VB'][0:LS, vs].rearrange("(t p) e -> p t e", p=128), swrites=[r_h])
            P.dma('sp', gate[:], d['GT'][0:LS, GW + h * 256:GW + (h + 1) * 256].rearrange("(t p) e -> p t e", p=128),
                  swrites=[r_h])
            bounds(LS, PAST + LS)
            pend = None
            for qb in range(16):
                bk = unit(PAST + LS, slice(qb * 128, (qb + 1) * 128), yst[:, qb, :], qb == 0, gate[:, qb, :], qb)
                if pend is not None:
                    pend()
                pend = bk
            pend()
            P.dma('sp', d['Y'][0:LS, GW + h * 256:GW + (h + 1) * 256].rearrange("(t p) e -> p t e", p=128), yst[:],
                  reads=[r_yst], swrites=[r_y])
            for sq in range(2):
                t0 = LS + sq * LP
                first = True
                for t in range(2):
                    P.dma('sp', qT[t][:, 0:LP], d['QTB'][h * 2 + t, :, t0:t0 + LP], writes=[r_h] if first else (),
                          swrites=() if first else [r_h])
                    first = False
                    P.dma('sp', kT[t][:, 0:LP], d['KTB'][h * 2 + t, :, t0:t0 + LP], swrites=[r_h])
                P.dma('sp', V[:, 0:2, :], d['VB'][t0:t0 + LP, vs].rearrange("(t p) e -> p t e", p=128), swrites=[r_h])
                P.dma('sp', gate[:, 0:2, :],
                      d['GT'][t0:t0 + LP, GW + h * 256:GW + (h + 1) * 256].rearrange("(t p) e -> p t e", p=128),
                      swrites=[r_h])
                bounds(LP, LP)
                pend = None
                for qb in range(2):
                    bk = unit(LP, slice(qb * 128, (qb + 1) * 128), yst[:, qb, :], qb == 0, gate[:, qb, :], qb)
                    if pend is not None:
                        pend()
                    pend = bk
                pend()
                P.dma('sp', d['Y'][t0:t0 + LP, GW + h * 256:GW + (h + 1) * 256].rearrange("(t p) e -> p t e", p=128),
                      yst[:, 0:2, :], reads=[r_yst], swrites=[r_y])
    def _b_back(self, Lk, nkb, u, st, abf, r_abf, aT, r_aT, otmp, r_otmp, junk, V, r_h, wsub, r_lt, r_st,
                yout, first_out, gate_ap, r_yst):
        P = self.P
        bank, bank_bf, r_bank = self.bank, self.bank_bf, self.r_bank
        identb, r_const = self.identb, self.r_const
        kb = 0
        first = True
        while kb < nkb:
            nb = min(8, nkb - kb)
            bT = 6 + (self._tbb % 2)
            self._tbb += 1
            for j in range(nb):
                self.tr(bank_bf(bT)[:, j * 128:(j + 1) * 128], abf[:, (kb + j) * 128:(kb + j + 1) * 128], identb[:],
                        [r_abf, r_const], r_bank[bT], j == 0)
            self.cp('act' if (self._tbb % 2) else 'dve', aT[:, kb:kb + nb, :].rearrange("p b q -> p (b q)"),
                    bank_bf(bT)[:, 0:nb * 128], [r_bank[bT]], [r_aT] if first else (), () if first else [r_aT])
            first = False
            kb += nb
        bo = self._sb % 6
        self._sb += 1
        for b_ in range(nkb):
            self.mm(bank(bo)[:, 0:256], aT[:, b_, :], V[:, b_, :], b_ == 0, b_ == nkb - 1, [r_aT, r_h], r_bank[bo])
        self.act(junk[:], bank(bo)[:, 0:256], AF.Square, [r_bank[bo]], [r_otmp], accum_out=st[:, 29:30])
        self.ts('dve', st[:, 30:31], st[:, 29:30], 1.0 / 256.0, EPS, ALU.mult, ALU.add, [r_st[u], r_otmp], (), [r_st[u]])
        self.act(st[:, 30:31], st[:, 30:31], AF.Ln, [r_st[u]], (), [r_st[u]])
        self.act(st[:, 30:31], st[:, 30:31], AF.Exp, [r_st[u]], (), [r_st[u]], scale=-0.5)
        self.stt(otmp[:], bank(bo)[:, 0:256], st[:, 30:31], wsub[:], ALU.mult, ALU.mult,
                 [r_bank[bo], r_st[u], r_lt], [r_otmp])
        self.tt('dve', yout, otmp[:], gate_ap, ALU.mult, [r_otmp, r_h], [r_yst] if first_out else (),
                () if first_out else [r_yst])

    def mixer_C(self, l):
        nc, P, A = self.nc, self.P, self.A
        d = self.d
        bank, bank_bf, r_bank = self.bank, self.bank_bf, self.r_bank
        cdt, r_tab = self.tabs['cdt'], self.tabs['r_tab']
        A.reset()
        r_y = P.res('Yw')
        rmask = A([128, 2, 128], F32, 'rmask'); r_rm = P.res('rmask')
        P.dma('sp', rmask[:], d['k_rmask'].rearrange("a j i -> j a i"), writes=[r_rm])
        NS = 2
        names = ('QFT', 'QBT', 'KFT', 'KBT')
        fT = [[A([128, LS], BF16, 'c%s%d' % (nm, i)) for nm in names] for i in range(NS)]
        tk = [[A([128, 16, 128], BF16, 'c%s%d' % (nm, i)) for nm in ('KF', 'KB', 'VC')] for i in range(NS)]
        gate = [A([128, 16, 128], BF16, 'cg%d' % i) for i in range(NS)]
        oacc = [A([128, 16, 128], F32, 'oacc%d' % i) for i in range(NS)]
        tmp1 = [A([128, 16, 128], F32, 'ctmp%d' % i) for i in range(NS)]
        yst = [A([128, 16, 128], BF16, 'cy%d' % i) for i in range(NS)]
        S = [[A([128, 128], F32, 'S%d_%d' % (i, dr)) for dr in range(2)] for i in range(NS)]
        Sb = [[A([128, 128], BF16, 'Sb%d_%d' % (i, dr)) for dr in range(2)] for i in range(NS)]
        atm = [[A([128, 128], BF16, 'atm%d_%d' % (i, dr)) for dr in range(2)] for i in range(NS)]
        stat = [A([128, 4, 16], F32, 'cst%d' % i) for i in range(NS)]
        r_h = [P.res() for i in range(NS)]
        r_o = [P.res() for i in range(NS)]
        r_S = [[P.res() for dr in range(2)] for i in range(NS)]
        r_Sb = [[P.res() for dr in range(2)] for i in range(NS)]
        r_atm = [[P.res() for dr in range(2)] for i in range(NS)]
        r_stat = [P.res() for i in range(NS)]
        r_yst = [P.res() for i in range(NS)]
        r_pa = [[P.res('pa', bank=i * 2 + dr) for dr in range(2)] for i in range(NS)]
        r_po = [[P.res('po', bank=i * 2 + dr) for dr in range(2)] for i in range(NS)]
        r_pd = [[P.res('pd', bank=i * 2 + dr) for dr in range(2)] for i in range(NS)]

        seqs = [(0, LS, True, None)] + [(LS + sq * LP, LP, False, sq) for sq in range(2)]
        for (t0, L, is_s, sq) in seqs:
            ncn = L // 128
            for hg in range(0, 8, NS):
                for i in range(NS):
                    h = hg + i
                    hs = slice(h * 128, (h + 1) * 128)
                    first = True
                    for j, nm in enumerate(names):
                        P.dma('sp', fT[i][j][:, 0:L], d[nm][h, :, t0:t0 + L], writes=[r_h[i]] if first else (),
                              swrites=() if first else [r_h[i]])
                        first = False
                    for j, nm in enumerate(('KF', 'KB', 'VC')):
                        P.dma('sp', tk[i][j][:, 0:ncn, :], d[nm][t0:t0 + L, hs].rearrange("(t p) e -> p t e", p=128),
                              swrites=[r_h[i]])
                    P.dma('sp', gate[i][:, 0:ncn, :],
                          d['GT'][t0:t0 + L, 2 * GW + h * 128:2 * GW + (h + 1) * 128].rearrange("(t p) e -> p t e", p=128),
                          swrites=[r_h[i]])
                    for dr in range(2):
                        if is_s:
                            P.dma('sp', S[i][dr][:], d['st_ret'][l, dr, h], writes=[r_S[i][dr]])
                        else:
                            self.memset('dve', S[i][dr][:], 0.0, writes=[r_S[i][dr]])
                        self.cp('act', Sb[i][dr][:], S[i][dr][:], [r_S[i][dr]], [r_Sb[i][dr]])
                for step in range(ncn):
                    for i in range(NS):
                        h = hg + i
                        for dr in range(2):
                            c = step if dr == 0 else ncn - 1 - step
                            cs = slice(c * 128, (c + 1) * 128)
                            pb = i * 2 + dr
                            qTt, kTt = fT[i][dr], fT[i][2 + dr]
                            ktok, vtok = tk[i][dr], tk[i][2]
                            self.mm(bank(pb)[:, 0:128], kTt[:, cs], qTt[:, cs], True, True, [r_h[i]], r_pa[i][dr])
                            self.tt('dve', atm[i][dr][:], bank(pb)[:, 0:128], rmask[:, dr, :], ALU.mult,
                                    [r_pa[i][dr], r_rm], [r_atm[i][dr]])
                            self.mm(bank(pb)[:, 128:256], atm[i][dr][:], vtok[:, c, :], True, False,
                                    [r_atm[i][dr], r_h[i]], r_po[i][dr], first=True)
                            self.mm(bank(pb)[:, 128:256], qTt[:, cs], Sb[i][dr][:], False, True,
                                    [r_h[i], r_Sb[i][dr]], r_po[i][dr], first=False)
                            first_touch = (step < (ncn + 1) // 2) if ncn > 1 else (dr == 0)
                            if ncn % 2 == 1 and step == ncn // 2:
                                first_touch = (dr == 0)
                            if first_touch:
                                self.cp('act', oacc[i][:, c, :], bank(pb)[:, 128:256], [r_po[i][dr]], (), [r_o[i]])
                            else:
                                self.tt('dve', oacc[i][:, c, :], oacc[i][:, c, :], bank(pb)[:, 128:256], ALU.add,
                                        [r_po[i][dr], r_o[i]], [r_o[i]])
                            self.mm(bank(pb)[:, 256:384], ktok[:, c, :], vtok[:, c, :], True, True, [r_h[i]], r_pd[i][dr])
                            cd = cdt[:, dr * 8 + h:dr * 8 + h + 1]
                            self.ts('dve', S[i][dr][:], S[i][dr][:], cd, None, ALU.mult, None, [r_S[i][dr], r_tab],
                                    [r_S[i][dr]])
                            self.stt(S[i][dr][:], bank(pb)[:, 256:384], cd, S[i][dr][:], ALU.mult, ALU.add,
                                     [r_pd[i][dr], r_S[i][dr], r_tab], [r_S[i][dr]])
                            self.cp('act', Sb[i][dr][:], S[i][dr][:], [r_S[i][dr]], [r_Sb[i][dr]])
                for i in range(NS):
                    h = hg + i
                    o3 = oacc[i][:, 0:ncn, :]
                    t3 = tmp1[i][:, 0:ncn, :]
                    sm, sq2, mean, rstd = (stat[i][:, k, 0:ncn] for k in range(4))
                    P.add('dve', lambda e, o=sm, i_=o3: e.reduce_sum(out=o, in_=i_, axis=AX.X), [r_o[i]], [r_stat[i]])
                    self.act(t3, o3, AF.Square, [r_o[i]], [r_yst[i]])
                    P.add('dve', lambda e, o=sq2, i_=t3: e.reduce_sum(out=o, in_=i_, axis=AX.X), [r_yst[i]], (),
                          [r_stat[i]])
                    self.ts('dve', mean, sm, 1.0 / 128.0, None, ALU.mult, None, [r_stat[i]], [r_stat[i]])
                    self.tt('dve', sm, mean, mean, ALU.mult, [r_stat[i]], [r_stat[i]])
                    self.stt(sq2, sq2, 1.0 / 128.0, sm, ALU.mult, ALU.subtract, [r_stat[i]], [r_stat[i]])
                    self.rstd_from(rstd, sq2, r_stat[i])
                    self.tt('dve', t3, o3, mean.unsqueeze(2).broadcast_to([128, ncn, 128]), ALU.subtract,
                            [r_o[i], r_stat[i]], [r_yst[i]])
                    self.tt('dve', t3, t3, rstd.unsqueeze(2).broadcast_to([128, ncn, 128]), ALU.mult,
                            [r_yst[i], r_stat[i]], [r_yst[i]])
                    self.tt('dve', yst[i][:, 0:ncn, :], t3, gate[i][:, 0:ncn, :], ALU.mult, [r_yst[i], r_h[i]],
                            [r_yst[i]])
                    P.dma('sp', d['Y'][t0:t0 + L, 2 * GW + h * 128:2 * GW + (h + 1) * 128].rearrange("(t p) e -> p t e", p=128),
                          yst[i][:, 0:ncn, :], reads=[r_yst[i]], swrites=[r_y])
                    if not is_s:
                        for dr in range(2):
                            P.dma('sp', d['o_st'][sq, l, dr, h], S[i][dr][:], reads=[r_S[i][dr]],
                                  swrites=[self.r_scr['OUT']])

    def mixer_D(self, l):
        nc, P, A = self.nc, self.P, self.A
        d = self.d
        A.reset()
        r_y = P.res('Yw')
        NB = 2
        Lmax = LS
        u = [A([128, Lmax + 2], F32, 'du%d' % i) for i in range(NB)]
        gd = [A([128, Lmax], F32, 'dgd%d' % i) for i in range(NB)]
        tacc = [A([128, Lmax], F32, 'dt%d' % i) for i in range(NB)]
        yb = [A([128, Lmax], BF16, 'dy%d' % i) for i in range(NB)]
        cw = [A([128, 3], F32, 'cw%d' % i) for i in range(NB)]
        r_in = [P.res() for i in range(NB)]
        r_t = [P.res() for i in range(NB)]
        r_yb = [P.res() for i in range(NB)]
        n = 0
        for j in range(8):
            rows = slice(j * 128, (j + 1) * 128)
            for (t0, L) in ((0, LS), (LS, LP), (LS + LP, LP)):
                b = n % NB
                n += 1
                self.memset('dve', u[b][:, 0:1], 0.0, writes=[r_in[b]])
                self.memset('dve', u[b][:, L + 1:L + 2], 0.0, swrites=[r_in[b]])
                P.dma('sp', u[b][:, 1:L + 1], d['UT'][rows, t0:t0 + L], swrites=[r_in[b]])
                P.dma('sp', gd[b][:, 0:L], d['GDT'][rows, t0:t0 + L], swrites=[r_in[b]])
                P.dma('sp', cw[b][:], d['conv_w'][l, rows, :], swrites=[r_in[b]])
                self.act(tacc[b][:, 0:L], u[b][:, 0:L], AF.Identity, [r_in[b]], [r_t[b]], scale=cw[b][:, 0:1])
                self.stt(tacc[b][:, 0:L], u[b][:, 1:L + 1], cw[b][:, 1:2], tacc[b][:, 0:L], ALU.mult, ALU.add,
                         [r_in[b], r_t[b]], [r_t[b]])
                self.stt(tacc[b][:, 0:L], u[b][:, 2:L + 2], cw[b][:, 2:3], tacc[b][:, 0:L], ALU.mult, ALU.add,
                         [r_in[b], r_t[b]], [r_t[b]])
                self.tt('dve', yb[b][:, 0:L], tacc[b][:, 0:L], gd[b][:, 0:L], ALU.mult, [r_t[b], r_in[b]], [r_yb[b]])
                P.dma('sp', d['YTD'][rows, t0:t0 + L], yb[b][:, 0:L], reads=[r_yb[b]], swrites=[r_y])

    def phase_EF(self, l):
        nc, P, A = self.nc, self.P, self.A
        d = self.d
        bank, bank_bf, r_bank = self.bank, self.bank_bf, self.r_bank
        identb, r_const = self.identb, self.r_const
        A.off = self.base_mark
        r_z = P.res('Zw')
        gbc = A([128, 2, D], F32, 'gbc'); r_g = P.res('gbc')
        for cv in range(2):
            P.dma('sp', gbc[:, cv, :], d['MADA'][l, cv:cv + 1, 2 * D:3 * D].broadcast_to([128, D]),
                  writes=[r_g] if cv == 0 else (), swrites=() if cv == 0 else [r_g])
        yT = A([128, 32, 1280], BF16, 'yT')
        r_yT = [P.res() for s in range(10)]
        wbuf = [A([128, 32, 512], BF16, 'wo%d' % i) for i in range(2)]
        r_wbuf = [P.res() for i in range(2)]
        ytile = [A([128, 3 * GW], BF16, 'ytile0')] * 2
        r_ytile = [P.res()] * 2
        xch = [A([128, 512], F32, 'xch%d' % i) for i in range(2)] + [None]
        xch[2] = xch[0]
        r_xch = [P.res() for i in range(2)]
        r_xch.append(r_xch[0])
        zst = [A([128, 512], F32, 'zst%d' % i) for i in range(2)] + [None]
        zst[2] = zst[0]
        r_zst = [P.res() for i in range(2)]
        r_zst.append(r_zst[0])
        self.alloc_stg()
        w_o = d['w_out'][l].rearrange("(kc p) n -> p kc n", p=128)
        for gi, grp in enumerate(self.groups):
            tbi = 0
            for s, tt in enumerate(grp):
                b = s % 2
                tok0 = tt * 128
                P.dma('sp', ytile[b][:], d['Y'][tok0:tok0 + 128, :], writes=[r_ytile[b]])
                P.dma('sp', yT[:, 24:32, s * 128:(s + 1) * 128],
                      d['YTD'][:, tok0:tok0 + 128].rearrange("(j p) t -> p j t", p=128), swrites=[r_yT[s]])
                for k8 in range(3):
                    bT = 4 + (tbi % 4)
                    tbi += 1
                    for j in range(8):
                        kc = k8 * 8 + j
                        self.tr(bank_bf(bT)[:, j * 128:(j + 1) * 128], ytile[b][:, kc * 128:(kc + 1) * 128], identb[:],
                                [r_ytile[b], r_const], r_bank[bT], j == 0)
                    self.cp('act' if k8 % 2 == 0 else 'dve', yT[:, k8 * 8:(k8 + 1) * 8, s * 128:(s + 1) * 128],
                            bank_bf(bT)[:, 0:1024].rearrange("p (j t) -> p j t", j=8), [r_bank[bT]], (), [r_yT[s]])
            self.drain(self.w_pieces(wbuf[0][:], r_wbuf[0], w_o[:, :, 0:512]))
            acc_i = 0
            xi = 0
            for oc in range(8):
                ws = oc % 2
                wgen = None
                if oc + 1 < 8:
                    wgen = self.w_pieces(wbuf[(oc + 1) % 2][:], r_wbuf[(oc + 1) % 2],
                                         w_o[:, :, (oc + 1) * 512:(oc + 2) * 512])
                cols = slice(oc * 512, (oc + 1) * 512)
                for s, tt in enumerate(grp):
                    pb = acc_i % 4
                    acc_i += 1
                    k = xi % 2
                    xi += 1
                    cv = 0 if tt < 16 else 1
                    P.dma('sp', xch[k][:], self.x_src(l, tt, oc * 512, (oc + 1) * 512), writes=[r_xch[k]])
                    for kc in range(32):
                        self.mm(bank(pb), yT[:, kc, s * 128:(s + 1) * 128], wbuf[ws][:, kc, :], kc == 0, kc == 31,
                                [r_yT[s], r_wbuf[ws]], r_bank[pb])
                    self.tt('dve', zst[k][:], bank(pb), gbc[:, cv, cols], ALU.mult, [r_bank[pb], r_g], [r_zst[k]])
                    self.stt(zst[k][:], xch[k][:], ALPHA, zst[k][:], ALU.mult, ALU.add, [r_xch[k], r_zst[k]],
                             [r_zst[k]], eng='dve')
                    P.dma('sp', d['Z'][tt * 128:(tt + 1) * 128, cols], zst[k][:], reads=[r_zst[k]], swrites=[r_z])
                    self.drain(wgen, 2)
                self.drain(wgen)
            P.barrier()
        A.off = self.base_mark
        gb = A([128, 2, D], F32, 'lngb'); r_gb = P.res('lngb')
        P.dma('sp', gb[:, 0, :], d['ln_g'][l:l + 1, :].broadcast_to([128, D]), writes=[r_gb])
        P.dma('sp', gb[:, 1, :], d['ln_b'][l:l + 1, :].broadcast_to([128, D]), swrites=[r_gb])
        zt = [A([128, D], F32, 'zt%d' % i) for i in range(4)]
        r_zt = [P.res() for i in range(4)]
        st = [A([128, 8, 6], F32, 'fst%d' % i) for i in range(4)]
        mv = [A([128, 4], F32, 'fmv%d' % i) for i in range(4)]
        r_mv = [P.res() for i in range(4)]
        r_x = P.res('Xw')
        for tt in range(NT):
            b = tt % 4
            P.dma('sp', zt[b][:], d['Z'][tt * 128:(tt + 1) * 128, :], writes=[r_zt[b]])
            for c8 in range(8):
                P.add('dve', lambda e, o=st[b][:, c8, :], i=zt[b][:, c8 * 512:(c8 + 1) * 512]: e.bn_stats(out=o, in_=i),
                      reads=[r_zt[b]], writes=[r_mv[b]] if c8 == 0 else (), swrites=() if c8 == 0 else [r_mv[b]])
            P.add('dve', lambda e, o=mv[b][:, 0:2], i=st[b][:].rearrange("p a b -> p (a b)"): e.bn_aggr(out=o, in_=i),
                  reads=[r_mv[b]], writes=[r_mv[b]])
            self.rstd_from(mv[b][:, 2:3], mv[b][:, 1:2], r_mv[b])
            self.stt(mv[b][:, 3:4], mv[b][:, 0:1], -1.0, mv[b][:, 2:3], ALU.mult, ALU.mult, [r_mv[b]], [r_mv[b]])
            self.act(zt[b][:], zt[b][:], AF.Identity, [r_zt[b], r_mv[b]], [r_zt[b]], scale=mv[b][:, 2:3],
                     bias=mv[b][:, 3:4])
            self.tt('dve', zt[b][:], zt[b][:], gb[:, 0, :], ALU.mult, [r_zt[b], r_gb], [r_zt[b]])
            self.tt('dve', zt[b][:], zt[b][:], gb[:, 1, :], ALU.add, [r_zt[b], r_gb], [r_zt[b]])
            if l == DEPTH - 1:
                if tt < 16:
                    dst = d['o_ys'][tt * 128:(tt + 1) * 128, :]
                else:
                    dst = d['o_yp'][(tt - 16) * 128:(tt - 15) * 128, :]
            else:
                dst = d['XRES'][tt * 128:(tt + 1) * 128, :]
            P.dma('sp', dst, zt[b][:], reads=[r_zt[b]], swrites=[r_x])


def _consts():
    k = {}
    k['k_ident'] = np.eye(128, dtype=np.float32)
    t = np.arange(LS)
    nf = 32
    inv = (10000.0 ** (-np.arange(nf, dtype=np.float32) / nf)).astype(np.float32)
    ang_r = ((t // 64).astype(np.float32)[:, None] * inv).astype(np.float32)
    ang_c = ((t % 64).astype(np.float32)[:, None] * inv).astype(np.float32)
    cos = np.zeros((LS, 2, 2, 32), np.float32)
    sin = np.zeros((LS, 2, 2, 32), np.float32)
    for a, ang in enumerate((ang_r, ang_c)):
        cos[:, a, 0] = np.cos(ang); cos[:, a, 1] = np.cos(ang)
        sin[:, a, 0] = -np.sin(ang); sin[:, a, 1] = np.sin(ang)
    k['k_cos'] = cos.reshape(LS, 128)
    k['k_sin'] = sin.reshape(LS, 128)
    rows = LS // 64
    m = np.full((5, 128, 640), NEG, np.float32)
    col = np.arange(64)
    c0 = np.clip(col - 8, 0, 48)
    col_ok = (col[None, :] >= c0[:, None]) & (col[None, :] < c0[:, None] + 16)
    for ty, qb in enumerate((0, 1, 5, 14, 15)):
        tw = min(max(qb - 2, 0), 11)
        for half in range(2):
            r = 2 * qb + half
            w0 = min(max(r - 4, 0), rows - 8)
            for kr in range(10):
                ra = 2 * tw + kr
                if w0 <= ra < w0 + 8:
                    blk = np.where(col_ok, 0.0, NEG).astype(np.float32)
                    m[ty, half * 64:(half + 1) * 64, kr * 64:(kr + 1) * 64] = blk
    k['k_namask'] = m
    j = np.arange(128)[:, None]
    i = np.arange(128)[None, :]
    k['k_rmask'] = np.stack([(j <= i), (j >= i)]).astype(np.float32)
    ii = np.arange(128, dtype=np.float32)
    k['k_coef'] = np.stack([ii + 1.0, -(ii + 1.0), 128.0 - ii, -(128.0 - ii)], 1).astype(np.float32)
    return k


_CACHE = {}


def _get_builder(debug=False, stop_after=None):
    key = (debug, stop_after)
    if key not in _CACHE:
        b = Builder(debug=debug, stop_after=stop_after)
        b.stats = b.build()
        _CACHE[key] = b
    return _CACHE[key]


def make_in_maps(inputs, cores):
    f = lambda a: np.ascontiguousarray(np.asarray(a, dtype=np.float32))
    k = _consts()
    shared = dict(
        w_ada=f(inputs['w_ada']), b_ada=f(inputs['b_ada']), w_in=f(inputs['w_in']), w_out=f(inputs['w_out']),
        ln_g=f(inputs['ln_g']), ln_b=f(inputs['ln_b']),
        na_bias=f(inputs['na_bias']).reshape(DEPTH, 120, 31),
        diff_lam=f(inputs['diff_lam']).reshape(DEPTH, 512), diff_subln=f(inputs['diff_subln']),
        ret_decay=f(inputs['ret_decay']).reshape(DEPTH, 16), conv_w=f(inputs['conv_w']), **k)
    xs, xp = f(inputs['x_sample']), f(inputs['x_prompt'])
    maps = []
    for i in cores:
        m = dict(shared)
        m['x_s'] = xs[i]
        m['x_p'] = xp[2 * i:2 * i + 2].reshape(2 * LP, D)
        m['c_na_k'] = f(inputs['cache_na_k'][i]).reshape(DEPTH, PAST, 1024)
        m['c_na_v'] = f(inputs['cache_na_v'][i]).reshape(DEPTH, PAST, 1024)
        m['c_df_k'] = f(inputs['cache_diff_k'][i]).reshape(DEPTH, PAST, 1024)
        m['c_df_v'] = f(inputs['cache_diff_v'][i]).reshape(DEPTH, PAST, 1024)
        m['st_ret'] = f(inputs['state_ret'][i])
        m['cvec'] = np.stack([f(inputs['c'])[i], f(inputs['c_ctx'])])
        maps.append(m)
    return maps


def kernel(**inputs):
    n = 8
    b = _get_builder()
    maps = make_in_maps(inputs, range(n))
    res = run_bass_kernel_spmd(b.nc, maps, core_ids=list(range(n)))
    R = res.results
    y_s = np.stack([R[i]['o_ys'] for i in range(n)])
    y_p = np.concatenate([R[i]['o_yp'].reshape(2, LP, D) for i in range(n)])

    def cat(nm, shape):
        return np.concatenate([R[i][nm].reshape((2, DEPTH, LP) + shape) for i in range(n)])
    nak = cat('o_nak', (8, HD))
    nav = cat('o_nav', (8, HD))
    dfk = cat('o_dfk', (4, 2, HD))
    dfv = cat('o_dfv', (4, 2 * HD))
    st = np.concatenate([R[i]['o_st'] for i in range(n)])
    return (y_p.astype(np.float32), y_s.astype(np.float32), nak, nav, dfk, dfv, st)
```
